# Optimizing a Trainium2 kernel written in Bass

```python
import jax
import jax.numpy as jnp
from jax import lax
import numpy as np

D_MODEL = 1024
BATCH = 16
SEQ = 2048
DEPTH = 1

GRID_W = 64
CTX_LEN = 256
HY_WIDTH = 512
HY_BANDS = 16
HY_EMB = 1 + 2 * HY_BANDS
HY_FILT_HIDDEN = 64
HY_DECAY_MIN = 2.0
HY_DECAY_MAX = 20.0
GLA_HEADS = 4
GLA_DK = 64
GLA_DV = 128
GLA_LOWRANK = 16
GLA_TAU = 16.0
GLA_CHUNK = 64
FFN_HIDDEN = 2816
EPS = 1e-6

HY_COLS = 3 * HY_WIDTH
GLA_QK = GLA_HEADS * GLA_DK
GLA_VW = GLA_HEADS * GLA_DV
IN_SIZES = (HY_COLS, GLA_QK, GLA_QK, GLA_VW, GLA_VW, GLA_LOWRANK, GLA_LOWRANK, D_MODEL, D_MODEL)
IN_COLS = HY_COLS + 2 * GLA_QK + 2 * GLA_VW + 2 * GLA_LOWRANK + 2 * D_MODEL

kernel_name = "hyena_gla_convffn_prefix_block"


def rmsnorm(x, w):
    xf = x.astype(jnp.float32)
    y = xf * lax.rsqrt(jnp.mean(xf * xf, axis=-1, keepdims=True) + EPS)
    return (y * w.astype(jnp.float32)).astype(x.dtype)


def adaln_params(cond, w_ada, b_ada):
    mod = jax.nn.silu(cond) @ w_ada + b_ada
    return jnp.split(mod, 6, axis=-1)


def split_in(proj):
    idx = np.cumsum(np.array(IN_SIZES))[:-1].tolist()
    return jnp.split(proj, idx, axis=-1)


def flip_t(t):
    return jnp.flip(t, axis=1)


def short_conv(u, w):
    L = u.shape[1]
    up = jnp.pad(u, ((0, 0), (1, 1), (0, 0)))
    return up[:, :L] * w[0] + up[:, 1:L + 1] * w[1] + up[:, 2:] * w[2]


def hyena_filter(L, p):
    f32 = jnp.float32
    t = jnp.linspace(0.0, 1.0, L, dtype=f32)[:, None]
    w = (2.0 * np.pi / L) * jnp.arange(L, dtype=f32)[:, None]
    f = jnp.linspace(1e-4, HY_BANDS - 1, HY_BANDS, dtype=f32)[None, :]
    feats = jnp.concatenate([t, jnp.cos(f * w), -jnp.sin(f * w)], axis=-1)
    freq = p["hy_filt_freq"].astype(f32)
    hdn = jnp.sin(freq * (feats @ p["hy_filt_w1"].astype(f32) + p["hy_filt_b1"].astype(f32)))
    hdn = jnp.sin(freq * (hdn @ p["hy_filt_w2"].astype(f32) + p["hy_filt_b2"].astype(f32)))
    k = (hdn @ p["hy_filt_w3"].astype(f32)) * jnp.exp(-t * p["hy_decay"].astype(f32))
    k_fwd = k[:, :HY_WIDTH]
    k_bwd = jnp.flip(k[1:, HY_WIDTH:], axis=0)
    norm = jnp.sum(jnp.abs(k_fwd), axis=0) + jnp.sum(jnp.abs(k_bwd), axis=0)
    kbuf = jnp.concatenate([k_fwd, jnp.zeros((1, HY_WIDTH), f32), k_bwd], axis=0)
    return kbuf / norm


def hyena_branch(u, p):
    L = u.shape[1]
    u = short_conv(u, p["hy_conv_w"])
    x0, x1, v = jnp.split(u, 3, axis=-1)
    z = (x1 * v).astype(jnp.float32)
    kbuf = hyena_filter(L, p)
    zf = jnp.fft.rfft(z, n=2 * L, axis=1)
    kf = jnp.fft.rfft(kbuf, n=2 * L, axis=0)
    y = jnp.fft.irfft(zf * kf[None], n=2 * L, axis=1)[:, :L]
    y = y + p["hy_bias"].astype(jnp.float32) * z
    return (x0.astype(jnp.float32) * y).astype(u.dtype)


def heads(t, d):
    return t.astype(jnp.float32).reshape(t.shape[0], t.shape[1], GLA_HEADS, d)


def log_decay(a, w_up, bias):
    la = jax.nn.log_sigmoid(a.astype(jnp.float32) @ w_up.astype(jnp.float32) + bias.astype(jnp.float32))
    return heads(la, GLA_DK) / GLA_TAU


def gla_chunked(q, k, v, log_a, s0):
    B, L, H, _ = q.shape
    N = L // GLA_CHUNK
    rs = lambda t: t.reshape(B, N, GLA_CHUNK, H, t.shape[-1])
    q, k, v, log_a = rs(q), rs(k), rs(v), rs(log_a)
    b = jnp.cumsum(log_a, axis=2)
    b_last = b[:, :, -1:]
    qe = q * jnp.exp(b)
    ke = k * jnp.exp(-b)
    kd = k * jnp.exp(b_last - b)
    mask = jnp.tril(jnp.ones((GLA_CHUNK, GLA_CHUNK), dtype=bool))
    scores = jnp.where(mask, jnp.einsum("bnchd,bnshd->bnhcs", qe, ke), 0.0)
    intra = jnp.einsum("bnhcs,bnshv->bnchv", scores, v)
    upd = jnp.einsum("bnshd,bnshv->bnhdv", kd, v)
    g = jnp.exp(b_last[:, :, 0])

    def step(s, xs):
        gn, un = xs
        return gn[..., None] * s + un, s

    _, s_prev = lax.scan(step, s0, (jnp.moveaxis(g, 1, 0), jnp.moveaxis(upd, 1, 0)))
    s_prev = jnp.moveaxis(s_prev, 0, 1)
    inter = jnp.einsum("bnchd,bnhdv->bnchv", qe, s_prev)
    return (intra + inter).reshape(B, L, H, v.shape[-1])


def gla_final_state(k, v, log_a):
    b = jnp.cumsum(log_a, axis=1)
    return jnp.einsum("blhd,blhv->bhdv", k * jnp.exp(b[:, -1:] - b), v)


def mixer(proj, p, s_f0, s_b0):
    u_hy, q, k, v, r, a_f, a_b, g_hy, g_gla = split_in(proj)
    y_hy = hyena_branch(u_hy, p)
    q = heads(q, GLA_DK) * (GLA_DK ** -0.5)
    k = heads(k, GLA_DK)
    v = heads(v, GLA_DV)
    la_f = log_decay(a_f, p["gla_a_up_f"], p["gla_a_bias_f"])
    la_b = log_decay(a_b, p["gla_a_up_b"], p["gla_a_bias_b"])
    o_f = gla_chunked(q, k, v, la_f, s_f0)
    o_b = flip_t(gla_chunked(flip_t(q), flip_t(k), flip_t(v), flip_t(la_b), s_b0))
    o = o_f + o_b
    o = o * lax.rsqrt(jnp.mean(o * o, axis=-1, keepdims=True) + EPS)
    B, L = o.shape[0], o.shape[1]
    y_gla = (o.reshape(B, L, GLA_VW) * p["gla_norm_w"].astype(jnp.float32)
             * jax.nn.silu(r.astype(jnp.float32))).astype(proj.dtype)
    merged = (jax.nn.sigmoid(g_hy) * (y_hy @ p["w_hy_proj"])
              + jax.nn.sigmoid(g_gla) * (y_gla @ p["w_gla_proj"]))
    return merged @ p["w_out"]


def conv_ffn(h, p, grid_w):
    B, L, _ = h.shape
    rows = L // grid_w
    up, gate = jnp.split(h @ p["ffn_w_in"], 2, axis=-1)
    g = gate.reshape(B, rows, grid_w, FFN_HIDDEN)
    g = lax.conv_general_dilated(g, p["ffn_dw"][:, :, None, :], window_strides=(1, 1), padding="SAME",
                                 dimension_numbers=("NHWC", "HWIO", "NHWC"),
                                 feature_group_count=FFN_HIDDEN)
    g = g.reshape(B, L, FFN_HIDDEN) + p["ffn_dw_bias"]
    return (jax.nn.silu(g) * up) @ p["ffn_w_out"]


def layer(x, ctx, c, c_ctx, p, update_ctx):
    sh1, sc1, g1, sh2, sc2, g2 = adaln_params(c[:, None, :], p["w_ada"], p["b_ada"])
    csh1, csc1, cg1, csh2, csc2, cg2 = adaln_params(c_ctx, p["w_ada"], p["b_ada"])
    hc = rmsnorm(ctx, p["norm1_w"]) * (1.0 + csc1) + csh1
    pc = hc @ p["w_in"]
    _, _, kc, vc, _, ac_f, ac_b, _, _ = split_in(pc)
    kc = heads(kc, GLA_DK)
    vc = heads(vc, GLA_DV)
    s_f = gla_final_state(kc, vc, log_decay(ac_f, p["gla_a_up_f"], p["gla_a_bias_f"]))
    s_b = gla_final_state(flip_t(kc), flip_t(vc),
                          flip_t(log_decay(ac_b, p["gla_a_up_b"], p["gla_a_bias_b"])))
    hx = rmsnorm(x, p["norm1_w"]) * (1.0 + sc1) + sh1
    x_new = x + g1 * mixer(hx @ p["w_in"], p, s_f, s_b)
    x_new = x_new + g2 * conv_ffn(rmsnorm(x_new, p["norm2_w"]) * (1.0 + sc2) + sh2, p, GRID_W)
    if update_ctx:
        z0 = jnp.zeros_like(s_f)
        ctx = ctx + cg1 * mixer(pc, p, z0, z0)
        ctx = ctx + cg2 * conv_ffn(rmsnorm(ctx, p["norm2_w"]) * (1.0 + csc2) + csh2, p, ctx.shape[1])
    return x_new, ctx


def setup_inputs(seed: int = 0) -> dict:
    key = jax.random.key(seed)
    ks = jax.random.split(key, 32)
    f32 = jnp.float32
    nrm = lambda k, shape, s: jax.random.normal(k, shape, f32) * s
    L_ = DEPTH
    return {
        "x": nrm(ks[0], (BATCH, SEQ, D_MODEL), 1.0),
        "c": nrm(ks[1], (BATCH, D_MODEL), 1.0),
        "ctx": nrm(ks[2], (BATCH, CTX_LEN, D_MODEL), 1.0),
        "c_ctx": nrm(ks[3], (D_MODEL,), 1.0),
        "w_ada": nrm(ks[4], (L_, D_MODEL, 6 * D_MODEL), 0.5 * D_MODEL ** -0.5),
        "b_ada": nrm(ks[5], (L_, 6 * D_MODEL), 0.02),
        "norm1_w": 1.0 + nrm(ks[6], (L_, D_MODEL), 0.02),
        "norm2_w": 1.0 + nrm(ks[7], (L_, D_MODEL), 0.02),
        "w_in": nrm(ks[8], (L_, D_MODEL, IN_COLS), D_MODEL ** -0.5),
        "hy_conv_w": nrm(ks[9], (L_, 3, HY_COLS), 3 ** -0.5),
        "hy_filt_w1": nrm(ks[10], (L_, HY_EMB, HY_FILT_HIDDEN), HY_EMB ** -0.5),
        "hy_filt_b1": nrm(ks[11], (L_, HY_FILT_HIDDEN), 0.1),
        "hy_filt_w2": nrm(ks[12], (L_, HY_FILT_HIDDEN, HY_FILT_HIDDEN), HY_FILT_HIDDEN ** -0.5),
        "hy_filt_b2": nrm(ks[13], (L_, HY_FILT_HIDDEN), 0.1),
        "hy_filt_w3": nrm(ks[14], (L_, HY_FILT_HIDDEN, 2 * HY_WIDTH), HY_FILT_HIDDEN ** -0.5),
        "hy_filt_freq": 1.0 + nrm(ks[15], (L_, HY_FILT_HIDDEN), 0.1),
        "hy_decay": jax.random.uniform(ks[16], (L_, 2 * HY_WIDTH), f32, HY_DECAY_MIN, HY_DECAY_MAX),
        "hy_bias": nrm(ks[17], (L_, HY_WIDTH), 0.5),
        "gla_a_up_f": nrm(ks[18], (L_, GLA_LOWRANK, GLA_QK), GLA_LOWRANK ** -0.5),
        "gla_a_bias_f": nrm(ks[19], (L_, GLA_QK), 0.5),
        "gla_a_up_b": nrm(ks[20], (L_, GLA_LOWRANK, GLA_QK), GLA_LOWRANK ** -0.5),
        "gla_a_bias_b": nrm(ks[21], (L_, GLA_QK), 0.5),
        "gla_norm_w": 1.0 + nrm(ks[22], (L_, GLA_VW), 0.02),
        "w_hy_proj": nrm(ks[23], (L_, HY_WIDTH, D_MODEL), HY_WIDTH ** -0.5),
        "w_gla_proj": nrm(ks[24], (L_, GLA_VW, D_MODEL), GLA_VW ** -0.5),
        "w_out": nrm(ks[25], (L_, D_MODEL, D_MODEL), D_MODEL ** -0.5),
        "ffn_w_in": nrm(ks[26], (L_, D_MODEL, 2 * FFN_HIDDEN), D_MODEL ** -0.5),
        "ffn_dw": nrm(ks[27], (L_, 3, 3, FFN_HIDDEN), 1.0 / 3.0),
        "ffn_dw_bias": nrm(ks[28], (L_, FFN_HIDDEN), 0.02),
        "ffn_w_out": nrm(ks[29], (L_, FFN_HIDDEN, D_MODEL), FFN_HIDDEN ** -0.5),
        "final_norm_w": 1.0 + nrm(ks[30], (D_MODEL,), 0.02),
    }


def reference(x, c, ctx, c_ctx, w_ada, b_ada, norm1_w, norm2_w, w_in, hy_conv_w,
              hy_filt_w1, hy_filt_b1, hy_filt_w2, hy_filt_b2, hy_filt_w3, hy_filt_freq,
              hy_decay, hy_bias, gla_a_up_f, gla_a_bias_f, gla_a_up_b, gla_a_bias_b,
              gla_norm_w, w_hy_proj, w_gla_proj, w_out, ffn_w_in, ffn_dw, ffn_dw_bias,
              ffn_w_out, final_norm_w):
    for l in range(DEPTH):
        p = {
            "w_ada": w_ada[l], "b_ada": b_ada[l], "norm1_w": norm1_w[l], "norm2_w": norm2_w[l],
            "w_in": w_in[l], "hy_conv_w": hy_conv_w[l],
            "hy_filt_w1": hy_filt_w1[l], "hy_filt_b1": hy_filt_b1[l],
            "hy_filt_w2": hy_filt_w2[l], "hy_filt_b2": hy_filt_b2[l],
            "hy_filt_w3": hy_filt_w3[l], "hy_filt_freq": hy_filt_freq[l],
            "hy_decay": hy_decay[l], "hy_bias": hy_bias[l],
            "gla_a_up_f": gla_a_up_f[l], "gla_a_bias_f": gla_a_bias_f[l],
            "gla_a_up_b": gla_a_up_b[l], "gla_a_bias_b": gla_a_bias_b[l],
            "gla_norm_w": gla_norm_w[l], "w_hy_proj": w_hy_proj[l], "w_gla_proj": w_gla_proj[l],
            "w_out": w_out[l], "ffn_w_in": ffn_w_in[l], "ffn_dw": ffn_dw[l],
            "ffn_dw_bias": ffn_dw_bias[l], "ffn_w_out": ffn_w_out[l],
        }
        x, ctx = layer(x, ctx, c, c_ctx, p, l < DEPTH - 1)
    return rmsnorm(x, final_norm_w)
```

```python
import math
from contextlib import ExitStack
import numpy as np
import ml_dtypes
import concourse.bass as bass
import concourse.mybir as mybir
from concourse.bass_utils import run_bass_kernel_spmd

F32 = mybir.dt.float32
BF16 = mybir.dt.bfloat16
AF = mybir.ActivationFunctionType
ALU = mybir.AluOpType

ENGS = ["pe", "act", "dve", "pool", "sp"]
SAME_ENG_SYNC = True
EPS = 1e-6
PI = math.pi

D = 1024
T = 2048
TC = 256
NCORES = 8
HYW = 512
FFH = 2816
NJ = 22
INC = 5152


class StopBuild(Exception):
    pass


G_LIMIT = [0]


class Res:
    __slots__ = ("name", "w", "r", "dsem")

    def __init__(self, name):
        self.name = name
        self.w = {}
        self.r = {}
        self.dsem = None


def _rnd_tile(n):
    return 32 if n <= 32 else (64 if n <= 64 else 128)


class _PEProxy:
    def __init__(self, eng, sem):
        self.eng = eng
        self.sem = sem
        self.n = 0
        self.mode = None

    def _chk(self, w, tr):
        shp = list(w.shape)
        k = shp[0]
        m = 1
        for d in shp[1:]:
            m *= d
        mode = (_rnd_tile(k), _rnd_tile(m), str(w.dtype), tr)
        if self.mode is not None and mode != self.mode and self.n > 0:
            self.eng.wait_ge(self.sem, self.n)
        self.mode = mode

    def matmul(self, out, lhsT, rhs, **kw):
        self._chk(lhsT, False)
        return self.eng.matmul(out, lhsT=lhsT, rhs=rhs, **kw)

    def transpose(self, out, in_, identity):
        self._chk(in_, True)
        return self.eng.transpose(out=out, in_=in_, identity=identity)


class Prog:
    def __init__(self, nc):
        self.nc = nc
        self.q = {e: [] for e in ENGS}
        self.semh = {}
        self.semv = {}
        self.waited = {}
        for e in ENGS:
            self._newsem("E_" + e)
        self.nres = 0
        self.free_dsems = []
        self.stage_dsems = []

    def _newsem(self, name):
        self.semh[name] = self.nc.alloc_semaphore(name=name)
        self.semv[name] = 0

    def res(self, name=None, dma=False, keep=False):
        self.nres += 1
        r = Res(name or ("r%d" % self.nres))
        if dma:
            if self.free_dsems and not keep:
                r.dsem = self.free_dsems.pop()
            else:
                r.dsem = "D%d" % self.nres
                self._newsem(r.dsem)
            if not keep:
                self.stage_dsems.append(r.dsem)
        return r

    def _deps(self, eng, reads, writes, pwrites):
        need = {}

        def add(d):
            for s, v in d.items():
                if need.get(s, 0) < v:
                    need[s] = v

        for r in reads:
            add(r.w)
        for w in writes:
            add(w.w)
            add(w.r)
        for w in pwrites:
            add(w.r)
        waits = []
        own = "E_" + eng
        for s, v in need.items():
            if s == own and (eng == "pe" or not SAME_ENG_SYNC):
                continue
            if self.waited.get((eng, s), 0) >= v:
                continue
            self.waited[(eng, s)] = v
            waits.append((s, v))
        return waits

    def _commit(self, tok, reads, writes, pwrites):
        s, v = tok
        for r in reads:
            if r.r.get(s, 0) < v:
                r.r[s] = v
        for w in list(writes) + list(pwrites):
            if w.w.get(s, 0) < v:
                w.w[s] = v

    def op(self, eng, fn, reads=(), writes=(), pwrites=()):
        waits = self._deps(eng, reads, writes, pwrites)
        s = "E_" + eng
        self.semv[s] += 1
        self.q[eng].append((waits, fn, (s, 1)))
        self._commit((s, self.semv[s]), reads, writes, pwrites)

    def dma(self, eng, fn, sem_res, reads=(), writes=(), pwrites=()):
        waits = self._deps(eng, reads, writes, pwrites)
        s = sem_res.dsem
        assert s is not None, sem_res.name
        self.semv[s] += 16
        self.q[eng].append((waits, fn, (s, 16)))
        self._commit((s, self.semv[s]), reads, writes, pwrites)

    def barrier(self):
        for e in ENGS:
            waits = []
            for s, v in self.semv.items():
                if v > 0 and self.waited.get((e, s), 0) < v:
                    self.waited[(e, s)] = v
                    waits.append((s, v))
            self.q[e].append((waits, None, None))
        self.free_dsems.extend(self.stage_dsems)
        self.stage_dsems = []

    def emit(self):
        nc = self.nc
        with nc.Block() as block:
            def mk(e):
                def body(engine):
                    eng = engine
                    if e == "pe":
                        eng = _PEProxy(engine, self.semh["E_pe"])
                    for waits, fn, inc in self.q[e]:
                        for s, v in waits:
                            engine.wait_ge(self.semh[s], v)
                        if fn is not None:
                            ins = fn(eng)
                            ins.then_inc(self.semh[inc[0]], inc[1])
                            if e == "pe" and inc[0] == "E_pe":
                                eng.n += 1
                return body
            block.tensor(mk("pe"))
            block.scalar(mk("act"))
            block.vector(mk("dve"))
            block.gpsimd(mk("pool"))
            block.sync(mk("sp"))


_CONST = None


def _constants():
    global _CONST
    if _CONST is not None:
        return _CONST
    N = 2 * T
    t = np.arange(T, dtype=np.float64)
    f = np.arange(T, dtype=np.float64)
    ang = (2.0 * np.pi / N) * ((t[:, None] * f[None, :]) % N)
    FT = np.zeros((T, 2 * T), np.float64)
    FT[:, :T] = np.cos(ang)
    FT[:, T:] = -np.sin(ang)
    FT[:, T] = np.where(t % 2 == 0, 1.0, -1.0)
    FTh = FT.reshape(16, 128, 32, 128).transpose(2, 1, 0, 3)
    wf = np.full(T, 2.0); wf[0] = 1.0
    GI = np.zeros((2 * T, T), np.float64)
    GI[:T, :] = (wf[:, None] * np.cos(ang.T)) / N
    GI[T:, :] = (-2.0 * np.sin(ang.T)) / N
    GI[T, :] = np.where(t % 2 == 0, 1.0, -1.0) / N
    GIh = GI.reshape(32, 128, 8, 256).transpose(2, 1, 0, 3)
    L = T
    tl = np.linspace(0.0, 1.0, L, dtype=np.float32)[:, None]
    w = (np.float32(2.0 * np.pi / L) * np.arange(L, dtype=np.float32))[:, None]
    fb = np.linspace(1e-4, 15.0, 16, dtype=np.float32)[None, :]
    feats = np.concatenate([tl, np.cos(fb * w), -np.sin(fb * w)], axis=-1).astype(np.float32)
    s = np.arange(128)[:, None]; tt = np.arange(128)[None, :]
    same = (s // 64) == (tt // 64)
    U_f = (same & (s <= tt)).astype(np.float32)
    U_b = (same & (s >= tt)).astype(np.float32)
    M_f = (same & (s > tt)).astype(np.float32)
    M_b = (same & (s < tt)).astype(np.float32)
    c = np.arange(256)[None, :] % 64
    p = np.arange(128)[:, None] % 64
    maskF = (p <= c).astype(np.float32)
    maskB = (p >= c).astype(np.float32)
    _CONST = dict(
        FTh=np.ascontiguousarray(FTh).astype(ml_dtypes.bfloat16),
        GIh=np.ascontiguousarray(GIh).astype(ml_dtypes.bfloat16),
        featsT=np.ascontiguousarray(feats.T),
        tneg=np.ascontiguousarray(-tl[:, 0].reshape(16, 128).T),
        tri=np.ascontiguousarray(np.stack([U_f, U_b, M_f, M_b], 1)),
        masks=np.ascontiguousarray(np.stack([maskF, maskB], 1)),
        ident=np.eye(128, dtype=np.float32),
    )
    return _CONST


def build(debug=False, upto="ALL"):
    nc = bass.Bass("TRN2", target_bir_lowering=False)
    P = Prog(nc)
    dbg_out = {}

    def din(name, shape, dt=F32):
        return nc.dram_tensor(name, list(shape), dt, kind="ExternalInput").ap()

    def dscr(name, shape, dt=F32):
        kind = "ExternalOutput" if debug else "Internal"
        if debug:
            dbg_out[name] = True
        return nc.dram_tensor(name, list(shape), dt, kind=kind).ap()

    xT = din("xT", [2, D, T]); ctxT = din("ctxT", [2, D, TC]); c3T = din("c3T", [128, 8, 3])
    w_ada = din("w_ada", [D, 6 * D]); b_adaT = din("b_adaT", [128, 48])
    nw = din("nw", [128, 3, 8])
    w_in = din("w_in", [D, INC])
    hy_cw = din("hy_cw", [128, 12, 3]); hy_bias = din("hy_bias", [128, 4])
    f_w1 = din("f_w1", [33, 64]); f_w2 = din("f_w2", [64, 64]); f_w3 = din("f_w3", [64, 1024])
    f_vec = din("f_vec", [64, 4])
    f_decay = din("f_decay", [1, 1024])
    w_ext = din("w_ext", [33, 512]); gnw = din("gnw", [128, 4])
    w_hyp = din("w_hyp", [HYW, D]); w_glp = din("w_glp", [512, D]); w_out = din("w_out", [D, D])
    ffn_wi = din("ffn_wi", [D, 2 * FFH]); ffn_dw = din("ffn_dw", [128, NJ, 9]); ffn_db = din("ffn_db", [128, NJ])
    ffn_wo = din("ffn_wo", [FFH, D])
    FTh = din("FTh", [32, 128, 16, 128], BF16); GIh = din("GIh", [8, 128, 32, 256], BF16)
    featsT = din("featsT", [33, T]); tneg_d = din("tneg", [128, 16])
    tri_d = din("tri", [128, 4, 128]); masks_d = din("masks", [128, 2, 256]); ident_d = din("ident", [128, 128])
    outT = nc.dram_tensor("outT", [2, D, T], F32, kind="ExternalOutput").ap()

    ka_d = dscr("ka_d", [16, 128, 512], BF16); kb_d = dscr("kb_d", [16, 128, 512], BF16)
    x0_d = dscr("x0_d", [2, HYW, T]); z_d = dscr("z_d", [2, HYW, T])
    qT_d = dscr("qT_d", [2, 256, T]); kT_d = dscr("kT_d", [2, 256, T]); ktok_d = dscr("ktok_d", [2, T, 256])
    vtok_d = dscr("vtok_d", [2, T, 512], BF16); rs_d = dscr("rs_d", [2, 512, T]); aT_d = dscr("aT_d", [2, 32, T])
    gate_d = dscr("gate_d", [2, 2 * D, T], BF16)
    ckT_d = dscr("ckT_d", [2, 256, TC]); cktok_d = dscr("cktok_d", [2, TC, 256])
    cvtok_d = dscr("cvtok_d", [2, TC, 512], BF16); caT_d = dscr("caT_d", [2, 32, TC])
    ygla_d = dscr("ygla_d", [2, 512, T], BF16); yhy_d = dscr("yhy_d", [2, HYW, T], BF16)
    xnew_d = dscr("xnew_d", [2, D, T]); h2_d = dscr("h2_d", [2, D, T], BF16)
    act_d = dscr("act_d", [2, FFH, T], BF16)
    R = {n: P.res(n) for n in ["ka_d", "kb_d", "x0_d", "z_d", "qT_d", "kT_d", "ktok_d", "vtok_d", "rs_d", "aT_d",
                               "gate_d", "ckT_d", "cktok_d", "cvtok_d", "caT_d", "ygla_d", "yhy_d", "xnew_d",
                               "h2_d", "act_d", "outT", "dbg"]}

    top = ExitStack()

    uid = [0]

    def sb(es, name, shape, dt=F32, dma=False, keep=False):
        uid[0] += 1
        t = es.enter_context(nc.sbuf_tensor("%s_%d" % (name, uid[0]), list(shape), dt))
        return t, P.res(name, dma=dma, keep=keep)

    def ps(es, name, dt=F32, cols=512):
        uid[0] += 1
        t = es.enter_context(nc.psum_tensor("%s_%d" % (name, uid[0]), [128, cols], dt))
        return t, P.res(name)

    def dbg_dump(name, tile_ap, shape, dt, rres, sem_res):
        if not debug:
            return
        o = nc.dram_tensor(name, list(shape), dt, kind="ExternalOutput").ap()
        dbg_out[name] = True
        P.dma("sp", lambda e: e.dma_start(out=o, in_=tile_ap), sem_res, reads=[rres], pwrites=[R["dbg"]])

    ident_f, r_identf = sb(top, "ident_f", [128, 128], F32, dma=True, keep=True)
    ident_b, r_identb = sb(top, "ident_b", [128, 128], BF16)
    ones_b, r_onesb = sb(top, "ones_b", [128, 128], BF16)
    ones_f, r_onesf = sb(top, "ones_f", [128, 128], F32)
    mod, r_mod = sb(top, "mod", [128, 48, 3], F32, dma=True, keep=True)
    a1, r_a1 = sb(top, "a1", [128, 8, 3], F32)
    a2, r_a2 = sb(top, "a2", [128, 8, 3], F32)
    nwt, r_nwt = sb(top, "nwt", [128, 3, 8], F32, dma=True, keep=True)
    P.dma("sp", lambda e: e.dma_start(out=ident_f[:], in_=ident_d), r_identf, writes=[r_identf])
    P.dma("sp", lambda e: e.dma_start(out=nwt[:], in_=nw), r_nwt, writes=[r_nwt])
    P.op("dve", lambda e: e.tensor_copy(out=ident_b[:], in_=ident_f[:]), reads=[r_identf], writes=[r_identb])
    P.op("dve", lambda e: e.memset(ones_b[:], 1.0), writes=[r_onesb])
    P.op("dve", lambda e: e.memset(ones_f[:], 1.0), writes=[r_onesf])
    SH1, SC1, G1, SH2, SC2, G2 = 0, 8, 16, 24, 32, 40

    esA = ExitStack()
    c3, r_c3 = sb(esA, "c3", [128, 8, 3], F32, dma=True)
    sc, r_sc = sb(esA, "sc", [128, 8, 3], BF16)
    bad, r_bad = sb(esA, "bad", [128, 48], F32, dma=True)
    wa = [sb(esA, "wa%d" % i, [128, 8, 512], F32, dma=True) for i in range(2)]
    wab = [sb(esA, "wab%d" % i, [128, 8, 512], BF16) for i in range(2)]
    mrows = [sb(esA, "mrow%d" % i, [3, 512], F32) for i in range(2)]
    pm, r_pm = ps(esA, "pm")
    pa = [ps(esA, "pa%d" % i) for i in range(2)]
    P.dma("sp", lambda e: e.dma_start(out=c3[:], in_=c3T), r_c3, writes=[r_c3])
    P.dma("sp", lambda e: e.dma_start(out=bad[:], in_=b_adaT), r_bad, writes=[r_bad])
    P.op("act", lambda e: e.activation(out=sc[:], in_=c3[:], func=AF.Silu), reads=[r_c3], writes=[r_sc])
    wav = w_ada.rearrange("(k p) m -> p k m", p=128)
    a_state = {"ld": 0, "g": 0}

    def ld_wa():
        g = a_state["ld"]
        if g >= 12:
            return
        a_state["ld"] += 1
        wt, r_wt = wa[g % 2]
        P.dma("sp", lambda e, wt=wt, g=g: e.dma_start(out=wt[:], in_=wav[:, :, g * 512:(g + 1) * 512]),
              r_wt, writes=[r_wt])

    def emit_A_group():
        g = a_state["g"]
        if g >= 12:
            return
        a_state["g"] += 1
        wf_, r_wf_ = wa[g % 2]
        wt, r_wt = wab[g % 2]
        pt, r_pt = pa[g % 2]
        mr, r_mr = mrows[g % 2]
        for hh in range(2):
            if hh == 0:
                P.op("act", lambda e, wf_=wf_, wt=wt: e.activation(out=wt[:, 0:5, :], in_=wf_[:, 0:5, :], func=AF.Identity),
                     reads=[r_wf_], pwrites=[r_wt])
            else:
                P.op("pool", lambda e, wf_=wf_, wt=wt: e.tensor_copy(out=wt[:, 5:8, :], in_=wf_[:, 5:8, :]),
                     reads=[r_wf_], pwrites=[r_wt])
        ld_wa()
        for k in range(8):
            P.op("pe", lambda e, wt=wt, pt=pt, k=k: e.matmul(
                pt[0:3, :], lhsT=sc[:, k, :], rhs=wt[:, k, :], start=(k == 0), stop=(k == 7)),
                reads=[r_wt, r_sc], writes=[r_pt])
        P.op("act", lambda e, pt=pt, mr=mr: e.activation(out=mr[:], in_=pt[0:3, :], func=AF.Identity),
             reads=[r_pt], writes=[r_mr])
        for mc in range(4):
            m = g * 4 + mc
            P.op("pe", lambda e, m=m, mc=mc, mr=mr: e.transpose(out=pm[:, m * 3:(m + 1) * 3], in_=mr[0:3, mc * 128:(mc + 1) * 128],
                                                                identity=ident_f[0:3, 0:3]),
                 reads=[r_mr, r_identf], writes=[r_pm])

    a_slots = [0]

    def a_slot():
        a_slots[0] += 1
        if a_slots[0] % 7 == 2:
            emit_A_group()

    def finish_A():
        while a_state["g"] < 12:
            emit_A_group()
        pmv = pm[:, 0:144].rearrange("p (m j) -> p m j", j=3)
        for j in range(3):
            P.op("dve", lambda e, j=j: e.tensor_tensor(out=mod[:, :, j], in0=pmv[:, :, j], in1=bad[:], op=ALU.add),
                 reads=[r_pm, r_bad], writes=[r_mod])
        for j in range(3):
            P.op("dve", lambda e, j=j: e.scalar_tensor_tensor(out=a1[:, :, j], in0=mod[:, SC1:SC1 + 8, j], scalar=1.0,
                                                             in1=nwt[:, 0, :], op0=ALU.add, op1=ALU.mult),
                 reads=[r_mod, r_nwt], writes=[r_a1])
            P.op("dve", lambda e, j=j: e.scalar_tensor_tensor(out=a2[:, :, j], in0=mod[:, SC2:SC2 + 8, j], scalar=1.0,
                                                             in1=nwt[:, 1, :], op0=ALU.add, op1=ALU.mult),
                 reads=[r_mod, r_nwt], writes=[r_a2])
        dbg_dump("dbg_mod", mod[:], [128, 48, 3], F32, r_mod, r_mod)

    with ExitStack() as es:
        ft, r_ft = sb(es, "ft", [33, T], F32, dma=True)
        fw1, r_fw1 = sb(es, "fw1", [33, 64], F32, dma=True)
        fw2, r_fw2 = sb(es, "fw2", [64, 64], F32, dma=True)
        fw3, r_fw3 = sb(es, "fw3", [64, 1024], F32, dma=True)
        fv, r_fv = sb(es, "fv", [64, 4], F32, dma=True)
        fb, r_fb = sb(es, "fb", [64, 2], F32)
        fdec, r_fdec = sb(es, "fdec", [1, 1024], F32, dma=True)
        tv, r_tv = sb(es, "tv", [128, 16], F32, dma=True)
        decb, r_decb = sb(es, "decb", [128, 1024], F32)
        acc, r_acc = sb(es, "kacc", [128, 512], F32)
        h1, r_h1 = sb(es, "h1", [64, T], F32)
        h2f, r_h2f = sb(es, "h2f", [64, T], F32)
        arg, r_arg = sb(es, "arg", [64, 512], F32)
        mm_, r_mm = sb(es, "mm_", [64, 512], F32)
        kf, r_kf = sb(es, "kf", [128, 16, 512], F32)
        k2, r_k2 = sb(es, "k2", [128, 16, 512], F32)
        ee, r_ee = sb(es, "ee", [128, 512], F32)
        abs_ = [sb(es, "ab%d" % i, [128, 512], BF16) for i in range(2)]
        rn, r_rn = sb(es, "rn", [128, 512], F32)
        tmpa, r_tmpa = sb(es, "tmpa", [128, 512], F32)
        tmpb, r_tmpb = sb(es, "tmpb", [128, 512], F32)
        kst = [sb(es, "kst%d" % i, [128, 512], BF16, dma=True) for i in range(4)]
        pp = [ps(es, "fps%d" % i) for i in range(4)]
        pn, r_pn = ps(es, "fpn")
        for tl_, rr, src in ((ft, r_ft, featsT), (fw1, r_fw1, f_w1), (fw2, r_fw2, f_w2), (fw3, r_fw3, f_w3),
                             (fv, r_fv, f_vec), (fdec, r_fdec, f_decay), (tv, r_tv, tneg_d)):
            P.dma("sp", lambda e, tl_=tl_, src=src: e.dma_start(out=tl_[:], in_=src), rr, writes=[rr])
        ld_wa(); ld_wa()
        P.op("dve", lambda e: e.tensor_scalar(out=fb[:], in0=fv[:, 0:2], scalar1=fv[:, 2:3], scalar2=None,
                                              op0=ALU.mult), reads=[r_fv], writes=[r_fb])

        def sin_layer(wt, r_w, kdim, src, r_src, dst, r_dst, li):
            for n in range(4):
                pt, r_pt = pp[n % 4]
                P.op("pe", lambda e, pt=pt, n=n: e.matmul(pt[0:64, :], lhsT=wt[0:kdim, :], rhs=src[0:kdim, n * 512:(n + 1) * 512],
                                                          start=True, stop=True), reads=[r_w, r_src], writes=[r_pt])
                P.op("dve", lambda e, pt=pt: e.tensor_scalar(out=arg[:], in0=pt[0:64, :], scalar1=fv[:, 2:3],
                                                             scalar2=fb[:, li:li + 1], op0=ALU.mult, op1=ALU.add),
                     reads=[r_pt, r_fv, r_fb], writes=[r_arg])
                P.op("dve", lambda e: e.tensor_scalar(out=mm_[:], in0=arg[:], scalar1=PI, scalar2=-2.0 * PI,
                                                      op0=ALU.is_gt, op1=ALU.mult), reads=[r_arg], writes=[r_mm])
                P.op("dve", lambda e: e.tensor_tensor(out=arg[:], in0=arg[:], in1=mm_[:], op=ALU.add),
                     reads=[r_mm], writes=[r_arg])
                P.op("dve", lambda e: e.tensor_scalar(out=mm_[:], in0=arg[:], scalar1=-PI, scalar2=2.0 * PI,
                                                      op0=ALU.is_lt, op1=ALU.mult), reads=[r_arg], writes=[r_mm])
                P.op("dve", lambda e: e.tensor_tensor(out=arg[:], in0=arg[:], in1=mm_[:], op=ALU.add),
                     reads=[r_mm], writes=[r_arg])
                P.op("act", lambda e, n=n: e.activation(out=dst[:, n * 512:(n + 1) * 512], in_=arg[:], func=AF.Sin),
                     reads=[r_arg], writes=[r_dst])
                a_slot()

        sin_layer(fw1, r_fw1, 33, ft, r_ft, h1, r_h1, 0)
        sin_layer(fw2, r_fw2, 64, h1, r_h1, h2f, r_h2f, 1)
        for half in range(2):
            pe_, r_pe = pp[0]
            P.op("pe", lambda e, pe_=pe_, half=half: e.matmul(
                pe_[:, :], lhsT=ones_f[0:1, :], rhs=fdec[0:1, half * 512:(half + 1) * 512],
                start=True, stop=True), reads=[r_onesf, r_fdec], writes=[r_pe])
            P.op("act", lambda e, pe_=pe_, half=half: e.activation(out=decb[:, half * 512:(half + 1) * 512], in_=pe_[:, :],
                                                                   func=AF.Identity), reads=[r_pe], pwrites=[r_decb])
        for half, dst, r_dst in ((0, kf, r_kf), (1, k2, r_k2)):
            for i in range(16):
                pk, r_pk = pp[1 + (i % 2)]
                P.op("act", lambda e, i=i, half=half: e.activation(out=ee[:], in_=decb[:, half * 512:(half + 1) * 512],
                                                                   func=AF.Exp, scale=tv[:, i:i + 1]),
                     reads=[r_decb, r_tv], writes=[r_ee])
                P.op("pe", lambda e, pk=pk, i=i, half=half: e.matmul(
                    pk[:, :], lhsT=h2f[:, i * 128:(i + 1) * 128], rhs=fw3[:, half * 512:(half + 1) * 512],
                    start=True, stop=True), reads=[r_h2f, r_fw3], writes=[r_pk])
                P.op("dve", lambda e, pk=pk, dst=dst, i=i: e.tensor_tensor(out=dst[:, i, :], in0=pk[:, :], in1=ee[:],
                                                                          op=ALU.mult),
                     reads=[r_pk, r_ee], writes=[r_dst])
                a_slot()
        P.op("dve", lambda e: e.memset(k2[0:1, 0, :], 0.0), writes=[r_k2])
        cnt = 0
        for src, r_src in ((kf, r_kf), (k2, r_k2)):
            for i in range(16):
                abt, r_abt = abs_[cnt % 2]
                P.op("act", lambda e, src=src, i=i, abt=abt: e.activation(out=abt[:], in_=src[:, i, :], func=AF.Abs),
                     reads=[r_src], writes=[r_abt])
                P.op("pe", lambda e, cnt=cnt, abt=abt: e.matmul(pn[:, :], lhsT=ones_b[:], rhs=abt[:], start=(cnt == 0),
                                                                stop=(cnt == 31)), reads=[r_abt, r_onesb], writes=[r_pn])
                cnt += 1
                a_slot()
        P.op("dve", lambda e: e.reciprocal(out=rn[:], in_=pn[:, :]), reads=[r_pn], writes=[r_rn])
        for i in range(16):
            sa, r_sa = kst[(2 * i) % 4]
            sbb, r_sbb = kst[(2 * i + 1) % 4]
            P.op("dve", lambda e, i=i: e.tensor_tensor(out=tmpa[:], in0=kf[:, i, :], in1=k2[:, i, :], op=ALU.add),
                 reads=[r_kf, r_k2], writes=[r_tmpa])
            P.op("dve", lambda e, sa=sa: e.tensor_tensor(out=sa[:], in0=tmpa[:], in1=rn[:], op=ALU.mult),
                 reads=[r_tmpa, r_rn], writes=[r_sa])
            P.dma("sp", lambda e, sa=sa, i=i: e.dma_start(out=ka_d[i], in_=sa[:]), r_sa, reads=[r_sa],
                  pwrites=[R["ka_d"]])
            P.op("pool", lambda e, i=i: e.tensor_tensor(out=tmpb[:], in0=kf[:, i, :], in1=k2[:, i, :], op=ALU.subtract),
                 reads=[r_kf, r_k2], writes=[r_tmpb])
            P.op("pool", lambda e, sbb=sbb: e.tensor_tensor(out=sbb[:], in0=tmpb[:], in1=rn[:], op=ALU.mult),
                 reads=[r_tmpb, r_rn], writes=[r_sbb])
            P.dma("sp", lambda e, sbb=sbb, i=i: e.dma_start(out=kb_d[i], in_=sbb[:]), r_sbb, reads=[r_sbb],
                  pwrites=[R["kb_d"]])
            a_slot()
        finish_A()
        P.barrier()
    esA.close()
    if upto == "F":
        return _finish(nc, P, R, dbg_out)

    epst, r_epst = sb(top, "epst", [128, 1], F32)
    P.op("dve", lambda e: e.memset(epst[:], EPS), writes=[r_epst])
    mid = ExitStack()
    ztok, r_ztok = sb(mid, "ztok", [128, 2, 16, 512], BF16)
    cwt, r_cwt = sb(mid, "cwt", [128, 12, 3], F32, dma=True, keep=True)
    P.dma("sp", lambda e: e.dma_start(out=cwt[:], in_=hy_cw), r_cwt, writes=[r_cwt])
    tri, r_tri = sb(mid, "tri", [128, 4, 128], F32, dma=True, keep=True)
    msk, r_msk = sb(mid, "msk", [128, 2, 256], F32, dma=True, keep=True)
    wext, r_wext = sb(mid, "wext", [33, 512], F32, dma=True, keep=True)
    gnwt, r_gnwt = sb(mid, "gnwt", [128, 4], F32, dma=True, keep=True)
    for tl_, rr, src in ((tri, r_tri, tri_d), (msk, r_msk, masks_d), (wext, r_wext, w_ext), (gnwt, r_gnwt, gnw)):
        P.dma("sp", lambda e, tl_=tl_, src=src: e.dma_start(out=tl_[:], in_=src), rr, writes=[rr])

    def emit_norm(xt, r_xt, W, a_t, r_a, shoff, j, dst, r_dst, tl):
        sq, r_sq, pss, r_pss, std, r_std, tmp, r_tmp = tl
        P.op("act", lambda e: e.activation(out=sq[:, :, 0:W], in_=xt[:, :, 0:W], func=AF.Square),
             reads=[r_xt], writes=[r_sq])
        for k in range(8):
            P.op("pe", lambda e, k=k: e.matmul(pss[:, 0:W], lhsT=ones_b[:], rhs=sq[:, k, 0:W], start=(k == 0),
                                               stop=(k == 7)), reads=[r_sq, r_onesb], writes=[r_pss])
        P.op("act", lambda e: e.activation(out=std[:, 0:W], in_=pss[:, 0:W], func=AF.Ln, scale=1.0 / D,
                                           bias=epst[:, 0:1]), reads=[r_pss, r_epst], writes=[r_std])
        P.op("act", lambda e: e.activation(out=std[:, 0:W], in_=std[:, 0:W], func=AF.Exp, scale=-0.5),
             reads=[r_std], writes=[r_std])
        for k in range(8):
            tk, r_tk = tmp[k % 2], r_tmp[k % 2]
            P.op("dve", lambda e, k=k, tk=tk: e.scalar_tensor_tensor(out=tk[:, 0:W], in0=xt[:, k, 0:W],
                                                                    scalar=a_t[:, k, j:j + 1], in1=std[:, 0:W],
                                                                    op0=ALU.mult, op1=ALU.mult),
                 reads=[r_xt, r_a, r_std], writes=[r_tk])
            P.op("act", lambda e, k=k, tk=tk: e.activation(out=dst(k), in_=tk[:, 0:W], func=AF.Identity,
                                                           bias=mod[:, shoff + k, j:j + 1], scale=1.0),
                 reads=[r_tk, r_mod], pwrites=[r_dst])


    def stage_P(s, is_ctx):
        Tn = 2 * TC if is_ctx else T
        W = min(512, Tn)
        ntt = Tn // W
        jmod = 2 if is_ctx else s
        if is_ctx:
            src = None
        else:
            src = xT[s].rearrange("(k p) t -> p k t", p=128)
        with ExitStack() as es:
            hx, r_hx = sb(es, "hx", [128, 8, Tn], BF16)
            with ExitStack() as es1:
                xts = [sb(es1, "xt%d" % i, [128, 8, W], F32, dma=True) for i in range(2)]
                sq, r_sq = sb(es1, "sq", [128, 8, W], BF16)
                tmps = [sb(es1, "ntmp%d" % i, [128, W], F32) for i in range(2)]
                tmp = [t_[0] for t_ in tmps]; r_tmp = [t_[1] for t_ in tmps]
                std, r_std = sb(es1, "std", [128, W], F32)
                pss, r_pss = ps(es1, "pss")
                def ld_x(tt):
                    xt, r_xt = xts[tt % 2]
                    if is_ctx:
                        for sq_ in range(2):
                            P.dma("sp", lambda e, xt=xt, sq_=sq_: e.dma_start(
                                out=xt[:, :, sq_ * TC:(sq_ + 1) * TC],
                                in_=ctxT[sq_].rearrange("(k p) t -> p k t", p=128)), r_xt,
                                writes=[r_xt] if sq_ == 0 else [], pwrites=[] if sq_ == 0 else [r_xt])
                    else:
                        P.dma("sp", lambda e, xt=xt, tt=tt: e.dma_start(out=xt[:], in_=src[:, :, tt * W:(tt + 1) * W]),
                              r_xt, writes=[r_xt])
                ld_x(0)
                for tt in range(ntt):
                    xt, r_xt = xts[tt % 2]
                    if tt + 1 < ntt:
                        ld_x(tt + 1)
                    emit_norm(xt, r_xt, W, a1, r_a1, SH1, jmod,
                              lambda k, tt=tt: hx[:, k, tt * W:(tt + 1) * W], r_hx,
                              (sq, r_sq, pss, r_pss, std, r_std, tmp, r_tmp))
                P.barrier()
            wbufs = [sb(es, "wb%d" % i, [128, 8, 512], BF16, dma=True) for i in range(2)]
            stg = [sb(es, "stg%d" % i, [128, Tn], F32, dma=True) for i in range(2)]
            stgb = [sb(es, "stgb%d" % i, [128, Tn], BF16, dma=True) for i in range(2)]
            ktk = [sb(es, "ktk%d" % i, [128, 256], F32, dma=True) for i in range(2)]
            vtk = [sb(es, "vtk%d" % i, [128, 512], BF16, dma=True) for i in range(2)]
            pps = [ps(es, "pp%d" % i) for i in range(4)]
            ptb, r_ptb = ps(es, "ptb", BF16, 1024)
            winv = w_in.rearrange("(k p) m -> p k m", p=128)
            cnt = {"w": 0, "ps": 0, "stg": 0, "stgb": 0}

            def fm_chunk(wt, r_wt, coff, M, evac):
                for tt in range(ntt):
                    pt, r_pt = pps[cnt["ps"] % 4]
                    cnt["ps"] += 1
                    for k in range(8):
                        P.op("pe", lambda e, pt=pt, k=k, tt=tt: e.matmul(
                            pt[0:M, 0:W], lhsT=wt[:, k, coff:coff + M], rhs=hx[:, k, tt * W:(tt + 1) * W],
                            start=(k == 0), stop=(k == 7)), reads=[r_wt, r_hx], writes=[r_pt])
                    evac(tt, pt, r_pt)

            def next_stg():
                t_ = stg[cnt["stg"] % 2]
                cnt["stg"] += 1
                return t_

            def next_stgb():
                t_ = stgb[cnt["stgb"] % 2]
                cnt["stgb"] += 1
                return t_

            groups = []
            simple = []

            def add_group(c0, nch, M, dst_ap_fn, r_dst, func, bf):
                done = 0
                while done < nch:
                    n_here = min(4, nch - done)
                    gid = len(groups)
                    groups.append((c0 + done * 128, n_here * M if M < 128 else n_here * 128))
                    for q in range(n_here):
                        simple.append(("fm", gid, q, M, dst_ap_fn(done + q), r_dst, func, bf))
                    done += n_here

            if not is_ctx:
                add_group(1536, 2, 128, lambda i: qT_d[s, i * 128:(i + 1) * 128, :], R["qT_d"], AF.Identity, False)
                add_group(1792, 2, 128, lambda i: kT_d[s, i * 128:(i + 1) * 128, :], R["kT_d"], AF.Identity, False)
                add_group(3072, 1, 32, lambda i: aT_d[s, :, :], R["aT_d"], AF.Identity, False)
            else:
                add_group(1792, 2, 128, lambda i: ckT_d[:, i * 128:(i + 1) * 128, :].rearrange("s m t -> m s t"),
                          R["ckT_d"], AF.Identity, False)
                add_group(3072, 1, 32, lambda i: caT_d.rearrange("s m t -> m s t"), R["caT_d"], AF.Identity, False)
            if not is_ctx:
                add_group(2560, 4, 128, lambda i: rs_d[s, i * 128:(i + 1) * 128, :], R["rs_d"], AF.Silu, False)
                add_group(3104, 16, 128, lambda i: gate_d[s, i * 128:(i + 1) * 128, :], R["gate_d"], AF.Sigmoid, True)
            gkv = len(groups)
            groups.append((1792, 512))
            groups.append((2304, 256))
            kvt = [("kv", tc_) for tc_ in range(Tn // 128)]
            if is_ctx:
                order = simple + kvt
            else:
                order = []
                si = 0
                for u in [("x0", j) for j in range(4)] + [("pair", j) for j in range(4)]:
                    order.append(u)
                    take = 1 if u[0] == "x0" else 2
                    order += simple[si:si + take]
                    si += take
                order += simple[si:] + kvt
            issued = [0]

            def issue_upto(n):
                while issued[0] < min(n, len(groups)):
                    i = issued[0]
                    c0, ncols = groups[i]
                    wt, r_wt = wbufs[i % 2]
                    P.dma("pool", lambda e, wt=wt, c0=c0, ncols=ncols: e.dma_start(
                        out=wt[:, :, 0:ncols], in_=winv[:, :, c0:c0 + ncols]), r_wt, writes=[r_wt])
                    issued[0] += 1

            def get_w(gid, extra=1):
                if gid >= cnt["w"]:
                    assert gid == cnt["w"], (gid, cnt["w"])
                    cnt["w"] = gid + 1
                    issue_upto(gid + 1 + extra)
                return wbufs[gid % 2]

            if not is_ctx:
                whx = [sb(es, "whx%d" % i, [128, 8, 512], BF16, dma=True) for i in range(3)]
                for i in range(3):
                    P.dma("pool", lambda e, i=i: e.dma_start(out=whx[i][0][:], in_=winv[:, :, i * 512:(i + 1) * 512]),
                          whx[i][1], writes=[whx[i][1]])
                ub = [sb(es, "ub%d" % i, [128, T + 2], F32) for i in range(2)]
                x1c, r_x1c = sb(es, "x1c", [128, T], F32)
                zb, r_zb = sb(es, "zb", [128, T], BF16)
                for u_, r_u in ub:
                    P.op("dve", lambda e, u_=u_: e.memset(u_[:, 0:1], 0.0), writes=[r_u])
                    P.op("dve", lambda e, u_=u_: e.memset(u_[:, T + 1:T + 2], 0.0), writes=[r_u])

                def conv(u_, r_u, m, dst, r_dst):
                    P.op("dve", lambda e: e.scalar_tensor_tensor(out=dst[:, 0:T], in0=u_[:, 0:T], scalar=cwt[:, m, 0:1],
                                                                 in1=dst[:, 0:T], op0=ALU.mult, op1=ALU.add),
                         reads=[r_u, r_cwt], writes=[r_dst])
                    P.op("dve", lambda e: e.scalar_tensor_tensor(out=dst[:, 0:T], in0=u_[:, 2:T + 2], scalar=cwt[:, m, 2:3],
                                                                 in1=dst[:, 0:T], op0=ALU.mult, op1=ALU.add),
                         reads=[r_u, r_cwt], writes=[r_dst])

                def evac_to(u_, r_u, m, dst, r_dst):
                    def evac(tt, pt, r_pt):
                        P.op("act", lambda e: e.activation(out=u_[:, 1 + tt * W:1 + (tt + 1) * W], in_=pt[:, 0:W],
                                                           func=AF.Identity), reads=[r_pt], pwrites=[r_u])
                        P.op("act", lambda e: e.activation(out=dst[:, tt * W:(tt + 1) * W], in_=pt[:, 0:W],
                                                           func=AF.Identity, scale=cwt[:, m, 1:2]),
                             reads=[r_pt, r_cwt], pwrites=[r_dst])
                    return evac

                def do_x0(j):
                    wt, r_wt = whx[0]
                    u_, r_u = ub[j % 2]
                    st, r_st = next_stg()
                    fm_chunk(wt, r_wt, j * 128, 128, evac_to(u_, r_u, j, st, r_st))
                    conv(u_, r_u, j, st, r_st)
                    P.dma("sp", lambda e, st=st, j=j: e.dma_start(out=x0_d[s, j * 128:(j + 1) * 128, :], in_=st[:]),
                          r_st, reads=[r_st], pwrites=[R["x0_d"]])

                def do_pair(j):
                    w1t, r_w1t = whx[1]
                    w2t, r_w2t = whx[2]
                    fm_chunk(w1t, r_w1t, j * 128, 128, evac_to(ub[0][0], ub[0][1], 4 + j, x1c, r_x1c))
                    conv(ub[0][0], ub[0][1], 4 + j, x1c, r_x1c)
                    st, r_st = next_stg()
                    fm_chunk(w2t, r_w2t, j * 128, 128, evac_to(ub[1][0], ub[1][1], 8 + j, st, r_st))
                    conv(ub[1][0], ub[1][1], 8 + j, st, r_st)
                    P.op("dve", lambda e, st=st: e.tensor_tensor(out=st[:], in0=st[:], in1=x1c[:], op=ALU.mult),
                         reads=[r_x1c], writes=[r_st])
                    P.dma("sp", lambda e, st=st, j=j: e.dma_start(out=z_d[s, j * 128:(j + 1) * 128, :], in_=st[:]),
                          r_st, reads=[r_st], pwrites=[R["z_d"]])
                    P.op("act", lambda e, st=st: e.activation(out=zb[:], in_=st[:], func=AF.Identity),
                         reads=[r_st], writes=[r_zb])
                    for g in range(4):
                        for q in range(4):
                            tc_ = 4 * g + q
                            P.op("pe", lambda e, q=q, tc_=tc_: e.transpose(out=ptb[:, q * 128:(q + 1) * 128],
                                                                            in_=zb[:, tc_ * 128:(tc_ + 1) * 128],
                                                                            identity=ident_b[:]),
                                 reads=[r_zb, r_identb], writes=[r_ptb])
                        P.op("act", lambda e, g=g, j=j: e.activation(
                            out=ztok[:, s, 4 * g:4 * g + 4, j * 128:(j + 1) * 128],
                            in_=ptb[:, 0:512].rearrange("p (q c) -> p q c", c=128), func=AF.Identity),
                            reads=[r_ptb], pwrites=[r_ztok])

            def do_fm(gid, q, M, dap, r_dst, func, bf):
                wt, r_wt = get_w(gid)
                st, r_st = next_stgb() if bf else next_stg()

                def evac(tt, pt, r_pt):
                    P.op("act", lambda e: e.activation(out=st[0:M, tt * W:(tt + 1) * W], in_=pt[0:M, 0:W],
                                                       func=func), reads=[r_pt], pwrites=[r_st])
                fm_chunk(wt, r_wt, q * 128, M, evac)
                src_ap = st[0:M, :].rearrange("m (s t) -> m s t", s=2) if is_ctx else st[0:M, :]
                P.dma("sp", lambda e: e.dma_start(out=dap, in_=src_ap), r_st, reads=[r_st], pwrites=[r_dst])

            ktd = cktok_d if is_ctx else ktok_d
            vtd = cvtok_d if is_ctx else vtok_d
            r_ktd = R["cktok_d" if is_ctx else "ktok_d"]
            r_vtd = R["cvtok_d" if is_ctx else "vtok_d"]

            def do_kv(tc_):
                wa_, r_wa_ = get_w(gkv, extra=2)
                wb_, r_wb_ = get_w(gkv + 1, extra=0)
                p1, r_p1 = pps[cnt["ps"] % 4]; cnt["ps"] += 1
                p2, r_p2 = pps[cnt["ps"] % 4]; cnt["ps"] += 1
                for k in range(8):
                    P.op("pe", lambda e, k=k: e.matmul(
                        p1[:, 0:512], lhsT=hx[:, k, tc_ * 128:(tc_ + 1) * 128], rhs=wa_[:, k, 0:512],
                        start=(k == 0), stop=(k == 7)), reads=[r_hx, r_wa_], writes=[r_p1])
                for k in range(8):
                    P.op("pe", lambda e, k=k: e.matmul(
                        p2[:, 0:256], lhsT=hx[:, k, tc_ * 128:(tc_ + 1) * 128], rhs=wb_[:, k, 0:256],
                        start=(k == 0), stop=(k == 7)), reads=[r_hx, r_wb_], writes=[r_p2])
                kt, r_kt = ktk[tc_ % 2]
                vt, r_vt = vtk[tc_ % 2]
                P.op("act", lambda e: e.activation(out=kt[:], in_=p1[:, 0:256], func=AF.Identity),
                     reads=[r_p1], writes=[r_kt])
                P.op("act", lambda e: e.activation(out=vt[:, 0:256], in_=p1[:, 256:512], func=AF.Identity),
                     reads=[r_p1], writes=[r_vt])
                P.op("act", lambda e: e.activation(out=vt[:, 256:512], in_=p2[:, 0:256], func=AF.Identity),
                     reads=[r_p2], pwrites=[r_vt])
                so, tl = (tc_ // 2, tc_ % 2) if is_ctx else (s, tc_)
                P.dma("sp", lambda e: e.dma_start(out=ktd[so, tl * 128:(tl + 1) * 128, :], in_=kt[:]),
                      r_kt, reads=[r_kt], pwrites=[r_ktd])
                P.dma("sp", lambda e: e.dma_start(out=vtd[so, tl * 128:(tl + 1) * 128, :], in_=vt[:]),
                      r_vt, reads=[r_vt], pwrites=[r_vtd])

            issue_upto(1)
            for t_ in order:
                if t_[0] == "x0":
                    do_x0(t_[1])
                elif t_[0] == "pair":
                    do_pair(t_[1])
                elif t_[0] == "fm":
                    do_fm(*t_[1:])
                else:
                    do_kv(t_[1])
            P.barrier()

    stage_P(0, False)
    stage_P(None, True)
    stage_P(1, False)
    if debug:
        for s in range(2):
            dbg_dump("dbg_ztok%d" % s, ztok[:, s], [128, 16, 512], BF16, r_ztok, r_mod)
        P.barrier()
    if upto == "P":
        mid.close()
        return _finish(nc, P, R, dbg_out)

    def stage_G(s):
        with ExitStack() as esS:
            S = [[sb(esS, "S%d%d" % (d_, hc), [128, 128], F32) for hc in range(2)] for d_ in range(2)]
            Sb = [[sb(esS, "Sb%d%d" % (d_, hc), [128, 128], BF16) for hc in range(2)] for d_ in range(2)]
            for d_ in range(2):
                for hc in range(2):
                    P.op("dve", lambda e, t_=S[d_][hc][0]: e.memset(t_[:], 0.0), writes=[S[d_][hc][1]])
                    P.op("dve", lambda e, t_=Sb[d_][hc][0]: e.memset(t_[:], 0.0), writes=[Sb[d_][hc][1]])
            def phase(is_ctx):
                Tn = TC if is_ctx else T
                ntc = Tn // 128
                nch = Tn // 64
                a_src = (caT_d if is_ctx else aT_d)[s]
                kt_src = (cktok_d if is_ctx else ktok_d)[s]
                vt_src = (cvtok_d if is_ctx else vtok_d)[s]
                Rn = (lambda n: R[("c" + n) if is_ctx else n])
                with ExitStack() as esP:
                    kd, r_kd = sb(esP, "kd", [128, ntc, 2, 256], BF16)
                    vt, r_vt = sb(esP, "vt", [128, ntc, 512], BF16, dma=True)
                    gT, r_gT = sb(esP, "gT", [128, 4, nch], F32)
                    if not is_ctx:
                        qe = [sb(esP, "qe%d" % d_, [128, 2, Tn], BF16) for d_ in range(2)]
                        ke = [sb(esP, "ke%d" % d_, [128, 2, Tn], BF16) for d_ in range(2)]
                        osb, r_osb = sb(esP, "osb", [128, 4, Tn], F32)
                        r_osbn = [P.res("osb_c%d" % n_) for n_ in range(nch)]
                        P.op("pool", lambda e: e.memset(osb[:], 0.0), writes=[r_osb] + r_osbn)
                    P.dma("sp", lambda e: e.dma_start(out=vt[:], in_=vt_src.rearrange("(c p) v -> p c v", p=128)),
                          r_vt, reads=[Rn("vtok_d")], writes=[r_vt])
                    with ExitStack() as es:
                        aext, r_aext = sb(es, "aext", [33, Tn], F32, dma=True)
                        lat, r_lat = sb(es, "lat", [128, 512], F32)
                        t1, r_t1 = sb(es, "gt1", [128, 512], F32)
                        t2, r_t2 = sb(es, "gt2", [128, 512], F32)
                        t3, r_t3 = sb(es, "gt3", [128, 512], F32)
                        t4, r_t4 = sb(es, "gt4", [128, 512], F32)
                        kts = [sb(es, "kts%d" % i, [128, 256], F32, dma=True) for i in range(2)]
                        qs = [sb(es, "qs%d" % i, [128, 2, 128], F32, dma=True) for i in range(2)]
                        ks = [sb(es, "ks%d" % i, [128, 2, 128], F32, dma=True) for i in range(2)]
                        pla, r_pla = ps(es, "pla")
                        pm_, r_pm_ = ps(es, "pmm")
                        pb_, r_pb_ = ps(es, "pbb")
                        P.op("dve", lambda e: e.memset(aext[:], 1.0), writes=[r_aext])
                        P.dma("sp", lambda e: e.dma_start(out=aext[0:32, :], in_=a_src), r_aext,
                              reads=[Rn("aT_d")], writes=[r_aext])
                        def ld_tc(tc_):
                            tsl = slice(tc_ * 128, (tc_ + 1) * 128)
                            kt_, r_kt_ = kts[tc_ % 2]
                            P.dma("sp", lambda e, kt_=kt_, tsl=tsl: e.dma_start(out=kt_[:], in_=kt_src[tsl, :]), r_kt_,
                                  reads=[Rn("ktok_d")], writes=[r_kt_])
                            if not is_ctx:
                                q_, r_q_ = qs[tc_ % 2]
                                k_, r_k_ = ks[tc_ % 2]
                                P.dma("sp", lambda e, q_=q_, tsl=tsl: e.dma_start(
                                    out=q_[:], in_=qT_d[s].rearrange("(h p) t -> p h t", p=128)[:, :, tsl]), r_q_,
                                    reads=[R["qT_d"]], writes=[r_q_])
                                P.dma("sp", lambda e, k_=k_, tsl=tsl: e.dma_start(
                                    out=k_[:], in_=kT_d[s].rearrange("(h p) t -> p h t", p=128)[:, :, tsl]), r_k_,
                                    reads=[R["kT_d"]], writes=[r_k_])
                        ld_tc(0)
                        for tc_ in range(ntc):
                            tsl = slice(tc_ * 128, (tc_ + 1) * 128)
                            if tc_ + 1 < ntc:
                                ld_tc(tc_ + 1)
                            P.op("pe", lambda e, tsl=tsl: e.matmul(pla[:, :], lhsT=aext[0:33, tsl], rhs=wext[0:33, :],
                                                                   start=True, stop=True),
                                 reads=[r_aext, r_wext], writes=[r_pla])
                            P.op("act", lambda e: e.activation(out=t1[:], in_=pla[:, :], func=AF.Exp, scale=-1.0),
                                 reads=[r_pla], writes=[r_t1])
                            P.op("act", lambda e: e.activation(out=t1[:], in_=t1[:], func=AF.Ln, bias=ones_f[:, 0:1],
                                                               scale=1.0), reads=[r_onesf], writes=[r_t1])
                            P.op("dve", lambda e: e.tensor_scalar(out=lat[:], in0=t1[:], scalar1=-1.0 / 16.0, scalar2=None,
                                                                  op0=ALU.mult), reads=[r_t1], writes=[r_lat])
                            P.op("pe", lambda e: e.matmul(pm_[:, 0:256], lhsT=tri[:, 2, :], rhs=lat[:, 0:256],
                                                          start=True, stop=True), reads=[r_tri, r_lat], writes=[r_pm_])
                            P.op("pe", lambda e: e.matmul(pm_[:, 256:512], lhsT=tri[:, 3, :], rhs=lat[:, 256:512],
                                                          start=True, stop=True), reads=[r_tri, r_lat], writes=[r_pm_])
                            P.op("act", lambda e: e.activation(out=t2[:], in_=pm_[:, :], func=AF.Exp),
                                 reads=[r_pm_], writes=[r_t2])
                            kt_, r_kt_ = kts[tc_ % 2]
                            for d_ in range(2):
                                P.op("dve", lambda e, d_=d_, kt_=kt_, tc_=tc_: e.tensor_tensor(
                                    out=kd[:, tc_, d_, :], in0=t2[:, d_ * 256:(d_ + 1) * 256], in1=kt_[:], op=ALU.mult),
                                    reads=[r_t2, r_kt_], pwrites=[r_kd])
                            for fc in range(4):
                                P.op("pe", lambda e, fc=fc: e.matmul(
                                    pb_[:, fc * 128:(fc + 1) * 128], lhsT=lat[:, fc * 128:(fc + 1) * 128],
                                    rhs=tri[:, 0 if fc < 2 else 1, :], start=True, stop=True),
                                    reads=[r_tri, r_lat], writes=[r_pb_])
                            P.op("act", lambda e: e.activation(out=t3[:], in_=pb_[:, :], func=AF.Exp),
                                 reads=[r_pb_], writes=[r_t3])
                            t3v = t3[:].rearrange("p (f c t) -> p f c t", f=4, c=2)
                            P.op("dve", lambda e, tc_=tc_, t3v=t3v: e.tensor_copy(out=gT[:, 0:2, 2 * tc_:2 * tc_ + 2],
                                                                            in_=t3v[:, 0:2, :, 63]),
                                 reads=[r_t3], pwrites=[r_gT])
                            P.op("dve", lambda e, tc_=tc_, t3v=t3v: e.tensor_copy(out=gT[:, 2:4, 2 * tc_:2 * tc_ + 2],
                                                                            in_=t3v[:, 2:4, :, 0]),
                                 reads=[r_t3], pwrites=[r_gT])
                            if not is_ctx:
                                P.op("act", lambda e: e.activation(out=t4[:], in_=pb_[:, :], func=AF.Exp, scale=-1.0),
                                     reads=[r_pb_], writes=[r_t4])
                                q_, r_q_ = qs[tc_ % 2]
                                k_, r_k_ = ks[tc_ % 2]
                                for d_ in range(2):
                                    P.op("dve", lambda e, d_=d_, q_=q_, tsl=tsl: e.scalar_tensor_tensor(
                                        out=qe[d_][0][:, :, tsl],
                                        in0=t3[:, d_ * 256:(d_ + 1) * 256].rearrange("p (h t) -> p h t", h=2),
                                        scalar=0.125, in1=q_[:], op0=ALU.mult, op1=ALU.mult),
                                        reads=[r_t3, r_q_], pwrites=[qe[d_][1]])
                                    P.op("dve", lambda e, d_=d_, k_=k_, tsl=tsl: e.tensor_tensor(
                                        out=ke[d_][0][:, :, tsl],
                                        in0=t4[:, d_ * 256:(d_ + 1) * 256].rearrange("p (h t) -> p h t", h=2),
                                        in1=k_[:], op=ALU.mult), reads=[r_t4, r_k_], pwrites=[ke[d_][1]])
                        P.barrier()
                        if G_LIMIT[0] == (1 if is_ctx else 3):
                            return True
                    with ExitStack() as es:
                        pA = [ps(es, "pA%d" % par)[0] for par in range(2)]
                        pC = [ps(es, "pC%d" % par)[0] for par in range(2)]
                        pB = [ps(es, "pB%d" % d_)[0] for d_ in range(2)]
                        pU = [ps(es, "pU%d" % d_)[0] for d_ in range(2)]
                        rA = [[P.res("rA%d%d" % (par, d_)) for d_ in range(2)] for par in range(2)]
                        rC = [[P.res("rC%d%d" % (par, d_)) for d_ in range(2)] for par in range(2)]
                        rBo = [P.res("rBo%d" % d_) for d_ in range(2)]
                        rBu = [P.res("rBu%d" % d_) for d_ in range(2)]
                        scs = [sb(es, "scs%d" % d_, [128, 2, 2, 64], BF16) for d_ in range(2)]
                        csts = [sb(es, "cst%d" % d_, [128, 2, 2, 64], F32) for d_ in range(2)]
                        def g2_params(idx):
                            i, d_ = idx // 2, idx % 2
                            n = i if d_ == 0 else nch - 1 - i
                            return d_, n, n // 2, (n % 2) * 64, n * 64

                        def emit_scores(idx):
                            d_, n, tc_, pb, t0 = g2_params(idx)
                            sc_t, r_sc_t = scs[d_]
                            for h in range(4):
                                par = h % 2; hp = par * 64; hc = h // 2
                                P.op("pe", lambda e, d_=d_, par=par, hp=hp, hc=hc, pb=pb, t0=t0: e.matmul(
                                    pA[par][pb:pb + 64, d_ * 128 + hc * 64:d_ * 128 + (hc + 1) * 64],
                                    lhsT=ke[d_][0][hp:hp + 64, hc, t0:t0 + 64],
                                    rhs=qe[d_][0][hp:hp + 64, hc, t0:t0 + 64], start=True, stop=True),
                                    reads=[ke[d_][1], qe[d_][1]], writes=[rA[par][d_]])
                            for par in range(2):
                                P.op("dve", lambda e, d_=d_, pb=pb, sc_t=sc_t, par=par: e.tensor_tensor(
                                    out=sc_t[pb:pb + 64, par, :, :],
                                    in0=pA[par][pb:pb + 64, d_ * 128:(d_ + 1) * 128].rearrange("p (h c) -> p h c", h=2),
                                    in1=msk[pb:pb + 64, d_, 0:128].rearrange("p (h c) -> p h c", h=2), op=ALU.mult),
                                    reads=[rA[par][d_], r_msk], pwrites=[r_sc_t])

                        def emit_upd(idx):
                            d_, n, tc_, pb, t0 = g2_params(idx)
                            for h in range(4):
                                hp = (h % 2) * 64; hc = h // 2
                                P.op("pe", lambda e, d_=d_, h=h, hp=hp, hc=hc, pb=pb, tc_=tc_: e.matmul(
                                    pU[d_][hp:hp + 64, hc * 128:(hc + 1) * 128],
                                    lhsT=kd[pb:pb + 64, tc_, d_, h * 64:(h + 1) * 64],
                                    rhs=vt[pb:pb + 64, tc_, h * 128:(h + 1) * 128], start=True, stop=True),
                                    reads=[r_kd, r_vt], writes=[rBu[d_]])

                        def emit_o(idx):
                            d_, n, tc_, pb, t0 = g2_params(idx)
                            sc_t, r_sc_t = scs[d_]
                            for h in range(4):
                                par = h % 2; hp = par * 64; hc = h // 2
                                P.op("pe", lambda e, d_=d_, par=par, hp=hp, hc=hc, t0=t0: e.matmul(
                                    pC[par][:, d_ * 128 + hc * 64:d_ * 128 + (hc + 1) * 64],
                                    lhsT=Sb[d_][hc][0][hp:hp + 64, :],
                                    rhs=qe[d_][0][hp:hp + 64, hc, t0:t0 + 64], start=True, stop=True),
                                    reads=[Sb[d_][hc][1], qe[d_][1]], writes=[rC[par][d_]])
                            for h in range(4):
                                par = h % 2; hp = par * 64; hc = h // 2
                                P.op("pe", lambda e, d_=d_, h=h, par=par, hc=hc, pb=pb, tc_=tc_, sc_t=sc_t: e.matmul(
                                    pB[d_][:, h * 64:(h + 1) * 64],
                                    lhsT=vt[pb:pb + 64, tc_, h * 128:(h + 1) * 128],
                                    rhs=sc_t[pb:pb + 64, par, hc, :], start=True, stop=True),
                                    reads=[r_vt, r_sc_t], writes=[rBo[d_]])
                            cs_t, r_cs_t = csts[d_]
                            for par in range(2):
                                P.op("act", lambda e, d_=d_, par=par, cs_t=cs_t: e.activation(
                                    out=cs_t[:, :, par, :],
                                    in_=pC[par][:, d_ * 128:(d_ + 1) * 128].rearrange("p (h c) -> p h c", h=2),
                                    func=AF.Identity), reads=[rC[par][d_]], pwrites=[r_cs_t])
                            P.op("dve", lambda e, d_=d_, t0=t0: e.tensor_tensor(
                                out=osb[:, :, t0:t0 + 64], in0=pB[d_][:, 0:256].rearrange("p (h c) -> p h c", h=4),
                                in1=osb[:, :, t0:t0 + 64], op=ALU.add), reads=[rBo[d_]], writes=[r_osbn[n]])
                            P.op("pool", lambda e, t0=t0, cs_t=cs_t: e.tensor_tensor(
                                out=osb[:, :, t0:t0 + 64], in0=cs_t[:].rearrange("p a b c -> p (a b) c"),
                                in1=osb[:, :, t0:t0 + 64], op=ALU.add), reads=[r_cs_t], writes=[r_osbn[n]])

                        def emit_state(idx):
                            d_, n, tc_, pb, t0 = g2_params(idx)
                            for hc in range(2):
                                St, r_St = S[d_][hc]
                                Sbt, r_Sbt = Sb[d_][hc]
                                P.op("dve", lambda e, d_=d_, hc=hc, n=n, St=St: e.scalar_tensor_tensor(
                                    out=St[:], in0=St[:], scalar=gT[:, d_ * 2 + hc, n:n + 1],
                                    in1=pU[d_][:, hc * 128:(hc + 1) * 128], op0=ALU.mult, op1=ALU.add),
                                    reads=[rBu[d_], r_gT], writes=[r_St])
                                P.op("act", lambda e, St=St, Sbt=Sbt: e.activation(out=Sbt[:], in_=St[:], func=AF.Identity),
                                     reads=[r_St], writes=[r_Sbt])

                        nit = 2 * nch
                        if not is_ctx:
                            emit_scores(0)
                        for idx in range(nit):
                            emit_upd(idx)
                            if not is_ctx:
                                if idx + 1 < nit:
                                    emit_scores(idx + 1)
                                emit_o(idx)
                            emit_state(idx)
                        if is_ctx and debug:
                            for d_ in range(2):
                                for hc in range(2):
                                    dbg_dump("dbg_S%d_%d%d" % (s, d_, hc), S[d_][hc][0][:], [128, 128], F32, S[d_][hc][1], r_mod)
                        if G_LIMIT[0] == (2 if is_ctx else 4):
                            if not is_ctx and debug:
                                dbg_dump("dbg_o%d" % s, osb[:], [128, 4, T], F32, r_osb, r_mod)
                            P.barrier()
                            return True
                        if not is_ctx:
                            if debug:
                                dbg_dump("dbg_o%d" % s, osb[:], [128, 4, T], F32, r_osb, r_mod)
                            P.barrier()
                    if not is_ctx:
                        with ExitStack() as es:
                            sq, r_sq = sb(es, "gsq", [128, 512], BF16)
                            std, r_std = sb(es, "gstd", [128, 512], F32)
                            tmp, r_tmp = sb(es, "gtmp", [128, 512], F32)
                            rsl = [sb(es, "rsl%d" % i, [128, 4, 512], F32, dma=True) for i in range(2)]
                            yg, r_yg = sb(es, "yg", [128, 4, T], BF16, dma=True)
                            pss, r_pss = ps(es, "gpss")
                            for tt in range(4):
                                tsl = slice(tt * 512, (tt + 1) * 512)
                                rt, r_rt = rsl[tt % 2]
                                P.dma("sp", lambda e, rt=rt, tsl=tsl: e.dma_start(
                                    out=rt[:], in_=rs_d[s].rearrange("(h p) t -> p h t", p=128)[:, :, tsl]), r_rt,
                                    reads=[R["rs_d"]], writes=[r_rt])
                                for h in range(4):
                                    P.op("act", lambda e, h=h, tsl=tsl: e.activation(out=sq[:], in_=osb[:, h, tsl], func=AF.Square),
                                         reads=[r_osb] + r_osbn[8 * tt:8 * tt + 8], writes=[r_sq])
                                    P.op("pe", lambda e: e.matmul(pss[:, :], lhsT=ones_b[:], rhs=sq[:], start=True, stop=True),
                                         reads=[r_sq, r_onesb], writes=[r_pss])
                                    P.op("act", lambda e: e.activation(out=std[:], in_=pss[:, :], func=AF.Ln, scale=1.0 / 128.0,
                                                                       bias=epst[:, 0:1]), reads=[r_pss, r_epst], writes=[r_std])
                                    P.op("act", lambda e: e.activation(out=std[:], in_=std[:], func=AF.Exp, scale=-0.5),
                                         reads=[r_std], writes=[r_std])
                                    P.op("dve", lambda e, h=h, tsl=tsl: e.tensor_tensor(out=tmp[:], in0=osb[:, h, tsl], in1=std[:],
                                                                                      op=ALU.mult), reads=[r_osb, r_std] + r_osbn[8 * tt:8 * tt + 8], writes=[r_tmp])
                                    P.op("dve", lambda e, h=h, tsl=tsl, rt=rt: e.scalar_tensor_tensor(
                                        out=yg[:, h, tsl], in0=tmp[:], scalar=gnwt[:, h:h + 1], in1=rt[:, h, :],
                                        op0=ALU.mult, op1=ALU.mult), reads=[r_tmp, r_gnwt, r_rt], pwrites=[r_yg])
                            P.dma("sp", lambda e: e.dma_start(out=ygla_d[s].rearrange("(h p) t -> p h t", p=128), in_=yg[:]),
                                  r_yg, reads=[r_yg], pwrites=[R["ygla_d"]])
                            P.barrier()
                    else:
                        P.barrier()

            for is_ctx in (True, False):
                if phase(is_ctx):
                    return True

    for s in range(2):
        if stage_G(s):
            mid.close()
            return _finish(nc, P, R, dbg_out)
    if upto == "G":
        mid.close()
        return _finish(nc, P, R, dbg_out)

    hctx = ExitStack()
    Yp, r_Yp = sb(hctx, "Yp", [128, 2, 32, 512], BF16)
    with ExitStack() as es:
        kab, r_kab = sb(es, "kab", [128, 2, 16, 512], BF16, dma=True)
        ftb = [[sb(es, "ft%d%d" % (b_, q), [128, 16, 128], BF16, dma=True) for q in range(2)] for b_ in range(2)]
        Ksb = [sb(es, "Ksb%d" % q, [128, 512], F32) for q in range(2)]
        tt_ = [[sb(es, "ht%d%d" % (s_, q), [128, 512], F32) for q in range(4)] for s_ in range(2)]
        pZ = [[ps(es, "pZ%d%d" % (s_, q)) for q in range(2)] for s_ in range(2)]
        pK = [ps(es, "pK%d" % q) for q in range(2)]
        pKf, r_pKf = ps(es, "pKf")
        P.dma("sp", lambda e: e.dma_start(out=kab[:, 0], in_=ka_d.rearrange("i p c -> p i c")), r_kab,
              reads=[R["ka_d"]], writes=[r_kab])
        P.dma("sp", lambda e: e.dma_start(out=kab[:, 1], in_=kb_d.rearrange("i p c -> p i c")), r_kab,
              reads=[R["kb_d"]], pwrites=[r_kab])
        def ld_ft(j):
            fts = ftb[j % 2]
            for q in range(2):
                P.dma("sp", lambda e, q=q, j=j, fts=fts: e.dma_start(out=fts[q][0][:], in_=FTh[j + 16 * q]), fts[q][1],
                      writes=[fts[q][1]])
        ld_ft(0)
        for j in range(16):
            fts = ftb[j % 2]
            if j + 1 < 16:
                ld_ft(j + 1)
            for q in range(2):
                ft_, r_ft_ = fts[q]
                for s_ in range(2):
                    pz, r_pz = pZ[s_][q]
                    for i in range(16):
                        P.op("pe", lambda e, pz=pz, ft_=ft_, i=i, s_=s_: e.matmul(
                            pz[:, :], lhsT=ft_[:, i, :], rhs=ztok[:, s_, i, :], start=(i == 0), stop=(i == 15)),
                            reads=[r_ft_, r_ztok], writes=[r_pz])
                pk, r_pk = pK[q]
                for i in range(16):
                    P.op("pe", lambda e, pk=pk, ft_=ft_, i=i, q=q: e.matmul(
                        pk[:, :], lhsT=ft_[:, i, :], rhs=kab[:, q, i, :], start=(i == 0), stop=(i == 15)),
                        reads=[r_ft_, r_kab], writes=[r_pk])
                if j == 0 and q == 1:
                    for i in range(16):
                        P.op("pe", lambda e, ft_=ft_, i=i: e.matmul(
                            pKf[:, :], lhsT=ft_[:, i, :], rhs=kab[:, 0, i, :], start=(i == 0), stop=(i == 15)),
                            reads=[r_ft_, r_kab], writes=[r_pKf])
            for q in range(2):
                P.op("act", lambda e, q=q: e.activation(out=Ksb[q][0][:], in_=pK[q][0][:, :], func=AF.Identity),
                     reads=[pK[q][1]], writes=[Ksb[q][1]])
            if j == 0:
                P.op("act", lambda e: e.activation(out=Ksb[1][0][0:1, :], in_=pKf[0:1, :], func=AF.Identity),
                     reads=[r_pKf], writes=[Ksb[1][1]])
            for s_ in range(2):
                t = tt_[s_]
                zr, r_zr = pZ[s_][0]
                zi, r_zi = pZ[s_][1]
                for (dst_t, zsrc, r_zsrc, kq) in ((0, zr, r_zr, 0), (1, zi, r_zi, 1), (2, zr, r_zr, 1), (3, zi, r_zi, 0)):
                    P.op("dve", lambda e, t=t, dst_t=dst_t, zsrc=zsrc, kq=kq: e.tensor_tensor(
                        out=t[dst_t][0][:], in0=zsrc[:, :], in1=Ksb[kq][0][:], op=ALU.mult),
                        reads=[r_zsrc, Ksb[kq][1]], writes=[t[dst_t][1]])
                P.op("pool", lambda e, t=t, s_=s_, j=j: e.tensor_tensor(out=Yp[:, s_, j, :], in0=t[0][0][:], in1=t[1][0][:],
                                                                      op=ALU.subtract),
                     reads=[t[0][1], t[1][1]], pwrites=[r_Yp])
                P.op("pool", lambda e, t=t, s_=s_, j=j: e.tensor_tensor(out=Yp[:, s_, 16 + j, :], in0=t[2][0][:], in1=t[3][0][:],
                                                                      op=ALU.add),
                     reads=[t[2][1], t[3][1]], pwrites=[r_Yp])
                if j == 0:
                    P.op("pool", lambda e, t=t, s_=s_: e.tensor_copy(out=Yp[0:1, s_, 0, :], in_=t[0][0][0:1, :]),
                         reads=[t[0][1]], writes=[r_Yp])
                    P.op("pool", lambda e, t=t, s_=s_: e.tensor_copy(out=Yp[0:1, s_, 16, :], in_=t[1][0][0:1, :]),
                         reads=[t[1][1]], writes=[r_Yp])
        P.barrier()
    with ExitStack() as es:
        gib = [sb(es, "gib%d" % i, [128, 32, 256], BF16, dma=True) for i in range(2)]
        x0t = [sb(es, "x0t%d" % i, [128, 256], F32, dma=True) for i in range(2)]
        zt = [sb(es, "zt%d" % i, [128, 256], F32, dma=True) for i in range(2)]
        htmp, r_htmp = sb(es, "htmp", [128, 256], F32)
        ysb = [sb(es, "ysb%d" % i, [128, 256], BF16, dma=True) for i in range(2)]
        hb, r_hb = sb(es, "hb", [128, 4], F32, dma=True)
        pI = [ps(es, "pI%d" % i) for i in range(4)]
        P.dma("sp", lambda e: e.dma_start(out=hb[:], in_=hy_bias), r_hb, writes=[r_hb])
        items = [(n, s_, cj) for n in range(8) for s_ in range(2) for cj in range(4)]

        def ld_gi(n):
            gi, r_gi = gib[n % 2]
            P.dma("sp", lambda e, gi=gi, n=n: e.dma_start(out=gi[:], in_=GIh[n]), r_gi, writes=[r_gi])

        def ld_xz(idx):
            n, s_, cj = items[idx]
            tsl = slice(n * 256, (n + 1) * 256)
            x0_, r_x0 = x0t[idx % 2]
            z_, r_z = zt[idx % 2]
            P.dma("sp", lambda e, x0_=x0_, s_=s_, cj=cj, tsl=tsl: e.dma_start(
                out=x0_[:], in_=x0_d[s_, cj * 128:(cj + 1) * 128, tsl]), r_x0, reads=[R["x0_d"]], writes=[r_x0])
            P.dma("sp", lambda e, z_=z_, s_=s_, cj=cj, tsl=tsl: e.dma_start(
                out=z_[:], in_=z_d[s_, cj * 128:(cj + 1) * 128, tsl]), r_z, reads=[R["z_d"]], writes=[r_z])

        ld_gi(0)
        ld_xz(0)
        for idx, (n, s_, cj) in enumerate(items):
            gi, r_gi = gib[n % 2]
            tsl = slice(n * 256, (n + 1) * 256)
            if s_ == 0 and cj == 0 and n + 1 < 8:
                ld_gi(n + 1)
            if idx + 1 < len(items):
                ld_xz(idx + 1)
            pi_, r_pi = pI[idx % 4]
            x0_, r_x0 = x0t[idx % 2]
            z_, r_z = zt[idx % 2]
            y_, r_y = ysb[idx % 2]
            for fj in range(32):
                P.op("pe", lambda e, pi_=pi_, gi=gi, s_=s_, cj=cj, fj=fj: e.matmul(
                    pi_[:, 0:256], lhsT=Yp[:, s_, fj, cj * 128:(cj + 1) * 128], rhs=gi[:, fj, :],
                    start=(fj == 0), stop=(fj == 31)), reads=[r_Yp, r_gi], writes=[r_pi])
            P.op("dve", lambda e, z_=z_, cj=cj, pi_=pi_: e.scalar_tensor_tensor(
                out=htmp[:], in0=z_[:], scalar=hb[:, cj:cj + 1], in1=pi_[:, 0:256], op0=ALU.mult, op1=ALU.add),
                reads=[r_z, r_hb, r_pi], writes=[r_htmp])
            P.op("pool", lambda e, y_=y_, x0_=x0_: e.tensor_tensor(out=y_[:], in0=htmp[:], in1=x0_[:], op=ALU.mult),
                 reads=[r_htmp, r_x0], writes=[r_y])
            P.dma("sp", lambda e, y_=y_, s_=s_, cj=cj, tsl=tsl: e.dma_start(
                out=yhy_d[s_, cj * 128:(cj + 1) * 128, tsl], in_=y_[:]), r_y, reads=[r_y], pwrites=[R["yhy_d"]])
        P.barrier()
    hctx.close()
    mid.close()
    if upto == "H":
        return _finish(nc, P, R, dbg_out)

    with ExitStack() as es:
        whp, r_whp = sb(es, "whp", [128, 4, D], BF16, dma=True)
        wgp, r_wgp = sb(es, "wgp", [128, 4, D], BF16, dma=True)
        wo, r_wo = sb(es, "wo", [128, 8, D], BF16, dma=True)
        P.dma("pool", lambda e: e.dma_start(out=whp[:], in_=w_hyp.rearrange("(k p) m -> p k m", p=128)), r_whp, writes=[r_whp])
        P.dma("pool", lambda e: e.dma_start(out=wgp[:], in_=w_glp.rearrange("(k p) m -> p k m", p=128)), r_wgp, writes=[r_wgp])
        P.dma("pool", lambda e: e.dma_start(out=wo[:], in_=w_out.rearrange("(k p) m -> p k m", p=128)), r_wo, writes=[r_wo])
        yh = [sb(es, "yh%d" % i, [128, 4, 512], BF16, dma=True) for i in range(2)]
        yg_ = [sb(es, "ygm%d" % i, [128, 4, 512], BF16, dma=True) for i in range(2)]
        gts = [sb(es, "gts%d" % i, [128, 16, 512], BF16, dma=True) for i in range(2)]
        xts = [sb(es, "mxt%d" % i, [128, 8, 512], F32, dma=True) for i in range(2)]
        mgs = [sb(es, "mg%d" % i, [128, 8, 512], BF16) for i in range(2)]
        mt = [sb(es, "mt%d" % i, [128, 512], F32) for i in range(2)]
        sq, r_sq = sb(es, "msq", [128, 8, 512], BF16)
        std, r_std = sb(es, "mstd", [128, 512], F32)
        tmps = [sb(es, "mtmp%d" % i, [128, 512], F32) for i in range(2)]
        h2t = [sb(es, "h2t%d" % i, [128, 8, 512], BF16, dma=True) for i in range(2)]
        pps = [ps(es, "mp%d" % i) for i in range(6)]
        pss, r_pss = ps(es, "mpss")
        cnt = 0
        mitems = [(s_, tt) for s_ in range(2) for tt in range(4)]
        nm_ = len(mitems)

        def ld_merge(it):
            s_, tt = mitems[it]
            tsl = slice(tt * 512, (tt + 1) * 512)
            yh_, r_yh = yh[it % 2]; ygt, r_ygt = yg_[it % 2]; gt, r_gt = gts[it % 2]
            P.dma("sp", lambda e, yh_=yh_, s_=s_, tsl=tsl: e.dma_start(
                out=yh_[:], in_=yhy_d[s_].rearrange("(k p) t -> p k t", p=128)[:, :, tsl]), r_yh,
                reads=[R["yhy_d"]], writes=[r_yh])
            P.dma("sp", lambda e, ygt=ygt, s_=s_, tsl=tsl: e.dma_start(
                out=ygt[:], in_=ygla_d[s_].rearrange("(k p) t -> p k t", p=128)[:, :, tsl]), r_ygt,
                reads=[R["ygla_d"]], writes=[r_ygt])
            P.dma("sp", lambda e, gt=gt, s_=s_, tsl=tsl: e.dma_start(
                out=gt[:], in_=gate_d[s_].rearrange("(k p) t -> p k t", p=128)[:, :, tsl]), r_gt,
                reads=[R["gate_d"]], writes=[r_gt])

        def ld_x(it):
            s_, tt = mitems[it]
            tsl = slice(tt * 512, (tt + 1) * 512)
            xt, r_xt = xts[it % 2]
            P.dma("sp", lambda e, xt=xt, s_=s_, tsl=tsl: e.dma_start(
                out=xt[:], in_=xT[s_].rearrange("(k p) t -> p k t", p=128)[:, :, tsl]), r_xt, writes=[r_xt])

        mcnt = [0]

        def nps():
            t_ = pps[mcnt[0] % 6]
            mcnt[0] += 1
            return t_

        def merge(it):
            yh_, r_yh = yh[it % 2]; ygt, r_ygt = yg_[it % 2]; gt, r_gt = gts[it % 2]
            mg_, r_mg_ = mgs[it % 2]
            for m in range(8):
                ph, r_ph = nps()
                pg, r_pg = nps()
                for k in range(4):
                    P.op("pe", lambda e, ph=ph, k=k, m=m, yh_=yh_: e.matmul(
                        ph[:, :], lhsT=whp[:, k, m * 128:(m + 1) * 128], rhs=yh_[:, k, :], start=(k == 0), stop=(k == 3)),
                        reads=[r_whp, r_yh], writes=[r_ph])
                for k in range(4):
                    P.op("pe", lambda e, pg=pg, k=k, m=m, ygt=ygt: e.matmul(
                        pg[:, :], lhsT=wgp[:, k, m * 128:(m + 1) * 128], rhs=ygt[:, k, :], start=(k == 0), stop=(k == 3)),
                        reads=[r_wgp, r_ygt], writes=[r_pg])
                P.op("dve", lambda e, ph=ph, gt=gt, m=m: e.tensor_tensor(out=mt[0][0][:], in0=ph[:, :], in1=gt[:, m, :], op=ALU.mult),
                     reads=[r_ph, r_gt], writes=[mt[0][1]])
                P.op("dve", lambda e, pg=pg, gt=gt, m=m: e.tensor_tensor(out=mt[1][0][:], in0=pg[:, :], in1=gt[:, 8 + m, :], op=ALU.mult),
                     reads=[r_pg, r_gt], writes=[mt[1][1]])
                P.op("pool", lambda e, m=m, mg_=mg_: e.tensor_tensor(out=mg_[:, m, :], in0=mt[0][0][:], in1=mt[1][0][:], op=ALU.add),
                     reads=[mt[0][1], mt[1][1]], pwrites=[r_mg_])

        def outproj(it):
            s_, tt = mitems[it]
            tsl = slice(tt * 512, (tt + 1) * 512)
            xt, r_xt = xts[it % 2]
            h2_, r_h2 = h2t[it % 2]
            mg_, r_mg_ = mgs[it % 2]
            for m2 in range(8):
                po, r_po = nps()
                for m in range(8):
                    P.op("pe", lambda e, po=po, m=m, m2=m2, mg_=mg_: e.matmul(
                        po[:, :], lhsT=wo[:, m, m2 * 128:(m2 + 1) * 128], rhs=mg_[:, m, :], start=(m == 0), stop=(m == 7)),
                        reads=[r_wo, r_mg_], writes=[r_po])
                P.op("dve", lambda e, po=po, m2=m2, xt=xt, s_=s_: e.scalar_tensor_tensor(
                    out=xt[:, m2, :], in0=po[:, :], scalar=mod[:, G1 + m2, s_:s_ + 1], in1=xt[:, m2, :],
                    op0=ALU.mult, op1=ALU.add), reads=[r_po, r_mod], writes=[r_xt])
            P.dma("sp", lambda e, xt=xt, s_=s_, tsl=tsl: e.dma_start(
                out=xnew_d[s_].rearrange("(k p) t -> p k t", p=128)[:, :, tsl], in_=xt[:]), r_xt,
                reads=[r_xt], pwrites=[R["xnew_d"]])
            emit_norm(xt, r_xt, 512, a2, r_a2, SH2, s_, lambda k, h2_=h2_: h2_[:, k, :], r_h2,
                      (sq, r_sq, pss, r_pss, std, r_std, [t_[0] for t_ in tmps], [t_[1] for t_ in tmps]))
            P.dma("sp", lambda e, h2_=h2_, s_=s_, tsl=tsl: e.dma_start(
                out=h2_d[s_].rearrange("(k p) t -> p k t", p=128)[:, :, tsl], in_=h2_[:]), r_h2,
                reads=[r_h2], pwrites=[R["h2_d"]])

        ld_merge(0); ld_x(0); ld_merge(1)
        merge(0)
        for it in range(nm_):
            if it + 1 < nm_:
                merge(it + 1)
            if it + 2 < nm_:
                ld_merge(it + 2)
            if it + 1 < nm_:
                ld_x(it + 1)
            outproj(it)
        P.barrier()
    if upto == "M":
        return _finish(nc, P, R, dbg_out)

    GW = 66
    with ExitStack() as es:
        dwt, r_dwt = sb(es, "dwt", [128, NJ, 9], F32, dma=True)
        dbt, r_dbt = sb(es, "dbt", [128, NJ], F32, dma=True)
        P.dma("sp", lambda e: e.dma_start(out=dwt[:], in_=ffn_dw), r_dwt, writes=[r_dwt])
        P.dma("sp", lambda e: e.dma_start(out=dbt[:], in_=ffn_db), r_dbt, writes=[r_dbt])
        h2Ts = [sb(es, "h2T%d" % i, [128, 8, T], BF16, dma=True) for i in range(2)]
        wgs = [sb(es, "wg%d" % i, [128, 8, 512], BF16, dma=True) for i in range(2)]
        wus = [sb(es, "wu%d" % i, [128, 8, 512], BF16, dma=True) for i in range(2)]
        dgs = [sb(es, "dg%d" % i, [128, 9, 128], BF16) for i in range(2)]
        gps = [sb(es, "gp%d" % i, [128, 2 + 34 * GW], BF16) for i in range(2)]
        sgs = [sb(es, "sg%d" % i, [128, T], F32) for i in range(2)]
        acs = [sb(es, "ac%d" % i, [128, T], BF16, dma=True) for i in range(2)]
        pg_ = [ps(es, "fpg%d" % i) for i in range(4)]
        pc_ = [ps(es, "fpc%d" % i) for i in range(3)]
        for gp, r_gp in gps:
            P.op("pool", lambda e, gp=gp: e.memset(gp[:], 0.0), writes=[r_gp])
        wiv = ffn_wi.rearrange("(k p) m -> p k m", p=128)
        cg = 0; cc = 0; it = 0
        for s_ in range(2):
            P.dma("sp", lambda e, s_=s_: e.dma_start(out=h2Ts[s_][0][:], in_=h2_d[s_].rearrange("(k p) t -> p k t", p=128)),
                  h2Ts[s_][1], reads=[R["h2_d"]], writes=[h2Ts[s_][1]])
        groups = [(s_, g) for s_ in range(2) for g in range((NJ + 3) // 4)]

        def ld_wg(gi_):
            s_, g = groups[gi_]
            nj = min(4, NJ - 4 * g)
            wg4, r_wg4 = wgs[gi_ % 2]; wu4, r_wu4 = wus[gi_ % 2]
            P.dma("pool", lambda e, wg4=wg4, g=g, nj=nj: e.dma_start(
                out=wg4[:, :, 0:nj * 128], in_=wiv[:, :, FFH + g * 512:FFH + g * 512 + nj * 128]), r_wg4, writes=[r_wg4])
            P.dma("pool", lambda e, wu4=wu4, g=g, nj=nj: e.dma_start(
                out=wu4[:, :, 0:nj * 128], in_=wiv[:, :, g * 512:g * 512 + nj * 128]), r_wu4, writes=[r_wu4])

        ld_wg(0)
        for gi_, (s_, g) in enumerate(groups):
            if gi_ + 1 < len(groups):
                ld_wg(gi_ + 1)
            h2T, r_h2T = h2Ts[s_]
            wg4, r_wg = wgs[gi_ % 2]; wu4, r_wu = wus[gi_ % 2]
            for jq in range(min(4, NJ - 4 * g)):
                j = 4 * g + jq
                wg = wg4[:, :, jq * 128:(jq + 1) * 128]
                wu = wu4[:, :, jq * 128:(jq + 1) * 128]
                dg, r_dg = dgs[it % 2]
                gp, r_gp = gps[it % 2]; sg, r_sg = sgs[it % 2]; ac, r_ac = acs[it % 2]
                it += 1
                for tap in range(9):
                    P.op("dve", lambda e, dg=dg, j=j, tap=tap: e.tensor_scalar(
                        out=dg[:, tap, :], in0=ident_f[:], scalar1=dwt[:, j, tap:tap + 1], scalar2=None, op0=ALU.mult),
                        reads=[r_identf, r_dwt], writes=[r_dg])
                gpv = gp[:, 1:1 + 34 * GW].rearrange("p (r w) -> p r w", w=GW)
                for tt in range(4):
                    pt, r_pt = pg_[cg % 4]; cg += 1
                    for k in range(8):
                        P.op("pe", lambda e, pt=pt, wg=wg, k=k, tt=tt, h2T=h2T: e.matmul(
                            pt[:, :], lhsT=wg[:, k, :], rhs=h2T[:, k, tt * 512:(tt + 1) * 512], start=(k == 0), stop=(k == 7)),
                            reads=[r_wg, r_h2T], writes=[r_pt])
                    P.op("act", lambda e, pt=pt, gpv=gpv, tt=tt: e.activation(
                        out=gpv[:, 1 + 8 * tt:9 + 8 * tt, 1:65], in_=pt[:, :].rearrange("p (r w) -> p r w", w=64),
                        func=AF.Identity), reads=[r_pt], writes=[r_gp])
                for (R0, nr) in ((1, 7), (8, 7), (15, 7), (22, 7), (29, 4)):
                    pc, r_pc = pc_[cc % 3]; cc += 1
                    p0 = 1 + R0 * GW
                    for tap in range(9):
                        off = (tap // 3 - 1) * GW + (tap % 3 - 1)
                        P.op("pe", lambda e, pc=pc, dg=dg, gp=gp, tap=tap, off=off, p0=p0, nr=nr: e.matmul(
                            pc[:, 0:nr * GW], lhsT=dg[:, tap, :], rhs=gp[:, p0 + off:p0 + off + nr * GW],
                            start=(tap == 0), stop=(tap == 8)), reads=[r_dg, r_gp], writes=[r_pc])
                    P.op("act", lambda e, pc=pc, sg=sg, R0=R0, nr=nr, j=j: e.activation(
                        out=sg[:, (R0 - 1) * 64:(R0 - 1 + nr) * 64].rearrange("p (r w) -> p r w", w=64),
                        in_=pc[:, 0:nr * GW].rearrange("p (r w) -> p r w", w=GW)[:, :, 1:65],
                        func=AF.Silu, bias=dbt[:, j:j + 1], scale=1.0), reads=[r_pc, r_dbt], writes=[r_sg])
                for tt in range(4):
                    pt, r_pt = pg_[cg % 4]; cg += 1
                    for k in range(8):
                        P.op("pe", lambda e, pt=pt, wu=wu, k=k, tt=tt, h2T=h2T: e.matmul(
                            pt[:, :], lhsT=wu[:, k, :], rhs=h2T[:, k, tt * 512:(tt + 1) * 512], start=(k == 0), stop=(k == 7)),
                            reads=[r_wu, r_h2T], writes=[r_pt])
                    P.op("dve", lambda e, pt=pt, sg=sg, ac=ac, tt=tt: e.tensor_tensor(
                        out=ac[:, tt * 512:(tt + 1) * 512], in0=pt[:, :], in1=sg[:, tt * 512:(tt + 1) * 512], op=ALU.mult),
                        reads=[r_pt, r_sg], writes=[r_ac])
                P.dma("sp", lambda e, ac=ac, s_=s_, j=j: e.dma_start(out=act_d[s_, j * 128:(j + 1) * 128, :], in_=ac[:]),
                      r_ac, reads=[r_ac], pwrites=[R["act_d"]])
        P.barrier()
    if upto == "FF":
        return _finish(nc, P, R, dbg_out)

    with ExitStack() as es:
        wd, r_wd = sb(es, "wd", [128, NJ, D], BF16, dma=True)
        wov = ffn_wo.rearrange("(j p) m -> p j m", p=128)
        for jj in range(0, NJ, 2):
            P.dma("pool", lambda e, jj=jj: e.dma_start(out=wd[:, jj:jj + 2, :], in_=wov[:, jj:jj + 2, :]), r_wd, pwrites=[r_wd])
        ats = [sb(es, "at%d" % i, [128, NJ, 512], BF16, dma=True) for i in range(2)]
        xns = [sb(es, "xn%d" % i, [128, 8, 512], F32, dma=True) for i in range(2)]
        ots = [sb(es, "ot%d" % i, [128, 8, 512], F32, dma=True) for i in range(2)]
        sq, r_sq = sb(es, "dsq", [128, 8, 512], BF16)
        std, r_std = sb(es, "dstd", [128, 512], F32)
        pd = [ps(es, "pd%d" % i) for i in range(4)]
        pss, r_pss = ps(es, "dpss")
        cnt = 0
        ditems = [(s_, tt) for s_ in range(2) for tt in range(4)]

        def ld_d(it):
            s_, tt = ditems[it]
            tsl = slice(tt * 512, (tt + 1) * 512)
            at, r_at = ats[it % 2]; xn, r_xn = xns[it % 2]
            P.dma("sp", lambda e, at=at, s_=s_, tsl=tsl: e.dma_start(
                out=at[:], in_=act_d[s_].rearrange("(j p) t -> p j t", p=128)[:, :, tsl]), r_at,
                reads=[R["act_d"]], writes=[r_at])
            P.dma("sp", lambda e, xn=xn, s_=s_, tsl=tsl: e.dma_start(
                out=xn[:], in_=xnew_d[s_].rearrange("(k p) t -> p k t", p=128)[:, :, tsl]), r_xn,
                reads=[R["xnew_d"]], writes=[r_xn])

        ld_d(0)
        for it, (s_, tt) in enumerate(ditems):
            if True:
                tsl = slice(tt * 512, (tt + 1) * 512)
                at, r_at = ats[it % 2]; xn, r_xn = xns[it % 2]; ot, r_ot = ots[it % 2]
                if it + 1 < len(ditems):
                    ld_d(it + 1)
                for m2 in range(8):
                    pt, r_pt = pd[cnt % 4]; cnt += 1
                    for j in range(NJ):
                        P.op("pe", lambda e, pt=pt, j=j, m2=m2, at=at: e.matmul(
                            pt[:, :], lhsT=wd[:, j, m2 * 128:(m2 + 1) * 128], rhs=at[:, j, :], start=(j == 0), stop=(j == NJ - 1)),
                            reads=[r_wd, r_at], writes=[r_pt])
                    P.op("dve", lambda e, pt=pt, m2=m2, xn=xn, s_=s_: e.scalar_tensor_tensor(
                        out=xn[:, m2, :], in0=pt[:, :], scalar=mod[:, G2 + m2, s_:s_ + 1], in1=xn[:, m2, :],
                        op0=ALU.mult, op1=ALU.add), reads=[r_pt, r_mod], writes=[r_xn])
                P.op("act", lambda e, xn=xn: e.activation(out=sq[:], in_=xn[:], func=AF.Square), reads=[r_xn], writes=[r_sq])
                for k in range(8):
                    P.op("pe", lambda e, k=k: e.matmul(pss[:, :], lhsT=ones_b[:], rhs=sq[:, k, :], start=(k == 0), stop=(k == 7)),
                         reads=[r_sq, r_onesb], writes=[r_pss])
                P.op("act", lambda e: e.activation(out=std[:], in_=pss[:, :], func=AF.Ln, scale=1.0 / D, bias=epst[:, 0:1]),
                     reads=[r_pss, r_epst], writes=[r_std])
                P.op("act", lambda e: e.activation(out=std[:], in_=std[:], func=AF.Exp, scale=-0.5),
                     reads=[r_std], writes=[r_std])
                for k in range(8):
                    P.op("dve", lambda e, k=k, xn=xn, ot=ot: e.scalar_tensor_tensor(
                        out=ot[:, k, :], in0=xn[:, k, :], scalar=nwt[:, 2, k:k + 1], in1=std[:], op0=ALU.mult, op1=ALU.mult),
                        reads=[r_xn, r_nwt, r_std], pwrites=[r_ot])
                P.dma("sp", lambda e, ot=ot, s_=s_, tsl=tsl: e.dma_start(
                    out=outT[s_].rearrange("(k p) t -> p k t", p=128)[:, :, tsl], in_=ot[:]), r_ot,
                    reads=[r_ot], pwrites=[R["outT"]])
        P.barrier()
    return _finish(nc, P, R, dbg_out)


def _finish(nc, P, R, dbg_out):
    P.q["sp"].append((P._deps("sp", list(R.values()), (), ()), None, None))
    P.barrier()
    P.emit()
    return nc, dbg_out


def _shared_inputs(inp):
    cst = _constants()
    f32 = np.float32
    A = lambda a: np.ascontiguousarray(a, dtype=f32)
    d = {}
    d["w_ada"] = A(inp["w_ada"][0])
    d["b_adaT"] = A(inp["b_ada"][0].reshape(48, 128).T)
    d["nw"] = A(np.stack([inp["norm1_w"][0].reshape(8, 128).T, inp["norm2_w"][0].reshape(8, 128).T,
                          inp["final_norm_w"].reshape(8, 128).T], axis=1))
    d["w_in"] = A(inp["w_in"][0])
    d["hy_cw"] = A(inp["hy_conv_w"][0].reshape(3, 12, 128).transpose(2, 1, 0))
    d["hy_bias"] = A(inp["hy_bias"][0].reshape(4, 128).T)
    d["f_w1"] = A(inp["hy_filt_w1"][0]); d["f_w2"] = A(inp["hy_filt_w2"][0]); d["f_w3"] = A(inp["hy_filt_w3"][0])
    d["f_vec"] = A(np.stack([inp["hy_filt_b1"][0], inp["hy_filt_b2"][0], inp["hy_filt_freq"][0],
                             np.zeros(64, f32)], axis=1))
    d["f_decay"] = A(inp["hy_decay"][0][None, :])
    we = np.zeros((33, 512), f32)
    we[0:16, 0:256] = inp["gla_a_up_f"][0]; we[16:32, 256:512] = inp["gla_a_up_b"][0]
    we[32, 0:256] = inp["gla_a_bias_f"][0]; we[32, 256:512] = inp["gla_a_bias_b"][0]
    d["w_ext"] = we
    d["gnw"] = A(inp["gla_norm_w"][0].reshape(4, 128).T)
    d["w_hyp"] = A(inp["w_hy_proj"][0]); d["w_glp"] = A(inp["w_gla_proj"][0]); d["w_out"] = A(inp["w_out"][0])
    d["ffn_wi"] = A(inp["ffn_w_in"][0])
    d["ffn_dw"] = A(inp["ffn_dw"][0].reshape(9, NJ, 128).transpose(2, 1, 0))
    d["ffn_db"] = A(inp["ffn_dw_bias"][0].reshape(NJ, 128).T)
    d["ffn_wo"] = A(inp["ffn_w_out"][0])
    for k in ("FTh", "GIh", "featsT", "tneg", "tri", "masks", "ident"):
        d[k] = cst[k]
    return d


def _core_inputs(inp, shared, ci):
    b0 = 2 * ci
    d = dict(shared)
    d["xT"] = np.ascontiguousarray(np.transpose(inp["x"][b0:b0 + 2], (0, 2, 1)), dtype=np.float32)
    d["ctxT"] = np.ascontiguousarray(np.transpose(inp["ctx"][b0:b0 + 2], (0, 2, 1)), dtype=np.float32)
    c3 = np.stack([inp["c"][b0], inp["c"][b0 + 1], inp["c_ctx"]], axis=0)
    d["c3T"] = np.ascontiguousarray(c3.reshape(3, 8, 128).transpose(2, 1, 0), dtype=np.float32)
    return d


def kernel(**inputs):
    inp = {k: np.asarray(v) for k, v in inputs.items()}
    nc, _ = build(debug=False, upto="ALL")
    shared = _shared_inputs(inp)
    in_maps = [_core_inputs(inp, shared, ci) for ci in range(NCORES)]
    res = run_bass_kernel_spmd(nc, in_maps, core_ids=list(range(NCORES)))
    outs = [np.transpose(np.asarray(r["outT"]), (0, 2, 1)) for r in res.results]
    return np.ascontiguousarray(np.concatenate(outs, axis=0), dtype=np.float32)
```

```python
import math
from contextlib import ExitStack
import numpy as np
import ml_dtypes
import concourse.bass as bass
import concourse.mybir as mybir
from concourse.bass_utils import run_bass_kernel_spmd

F32 = mybir.dt.float32
BF16 = mybir.dt.bfloat16
AF = mybir.ActivationFunctionType
ALU = mybir.AluOpType

ENGS = ["pe", "act", "dve", "pool", "sp"]
SAME_ENG_SYNC = True
EPS = 1e-6
PI = math.pi

D = 1024
T = 2048
TC = 256
NCORES = 8
HYW = 512
FFH = 2816
NJ = 22
INC = 5152


class StopBuild(Exception):
    pass


G_LIMIT = [0]


class Res:
    __slots__ = ("name", "w", "r", "dsem")

    def __init__(self, name):
        self.name = name
        self.w = {}
        self.r = {}
        self.dsem = None


def _rnd_tile(n):
    return 32 if n <= 32 else (64 if n <= 64 else 128)


class _PEProxy:
    def __init__(self, eng, sem):
        self.eng = eng
        self.sem = sem
        self.n = 0
        self.mode = None

    def _chk(self, w, tr):
        shp = list(w.shape)
        k = shp[0]
        m = 1
        for d in shp[1:]:
            m *= d
        mode = (_rnd_tile(k), _rnd_tile(m), str(w.dtype), tr)
        if self.mode is not None and mode != self.mode and self.n > 0:
            self.eng.wait_ge(self.sem, self.n)
        self.mode = mode

    def matmul(self, out, lhsT, rhs, **kw):
        self._chk(lhsT, False)
        return self.eng.matmul(out, lhsT=lhsT, rhs=rhs, **kw)

    def transpose(self, out, in_, identity):
        self._chk(in_, True)
        return self.eng.transpose(out=out, in_=in_, identity=identity)


class Prog:
    def __init__(self, nc):
        self.nc = nc
        self.q = {e: [] for e in ENGS}
        self.semh = {}
        self.semv = {}
        self.waited = {}
        for e in ENGS:
            self._newsem("E_" + e)
        self.nres = 0
        self.free_dsems = []
        self.stage_dsems = []

    def _newsem(self, name):
        self.semh[name] = self.nc.alloc_semaphore(name=name)
        self.semv[name] = 0

    def res(self, name=None, dma=False, keep=False):
        self.nres += 1
        r = Res(name or ("r%d" % self.nres))
        if dma:
            if self.free_dsems and not keep:
                r.dsem = self.free_dsems.pop()
            else:
                r.dsem = "D%d" % self.nres
                self._newsem(r.dsem)
            if not keep:
                self.stage_dsems.append(r.dsem)
        return r

    def _deps(self, eng, reads, writes, pwrites):
        need = {}

        def add(d):
            for s, v in d.items():
                if need.get(s, 0) < v:
                    need[s] = v

        for r in reads:
            add(r.w)
        for w in writes:
            add(w.w)
            add(w.r)
        for w in pwrites:
            add(w.r)
        waits = []
        own = "E_" + eng
        for s, v in need.items():
            if s == own and (eng == "pe" or not SAME_ENG_SYNC):
                continue
            if self.waited.get((eng, s), 0) >= v:
                continue
            self.waited[(eng, s)] = v
            waits.append((s, v))
        return waits

    def _commit(self, tok, reads, writes, pwrites):
        s, v = tok
        for r in reads:
            if r.r.get(s, 0) < v:
                r.r[s] = v
        for w in list(writes) + list(pwrites):
            if w.w.get(s, 0) < v:
                w.w[s] = v

    def op(self, eng, fn, reads=(), writes=(), pwrites=()):
        waits = self._deps(eng, reads, writes, pwrites)
        s = "E_" + eng
        self.semv[s] += 1
        self.q[eng].append((waits, fn, (s, 1)))
        self._commit((s, self.semv[s]), reads, writes, pwrites)

    def dma(self, eng, fn, sem_res, reads=(), writes=(), pwrites=()):
        waits = self._deps(eng, reads, writes, pwrites)
        s = sem_res.dsem
        assert s is not None, sem_res.name
        self.semv[s] += 16
        self.q[eng].append((waits, fn, (s, 16)))
        self._commit((s, self.semv[s]), reads, writes, pwrites)

    def barrier(self):
        for e in ENGS:
            waits = []
            for s, v in self.semv.items():
                if v > 0 and self.waited.get((e, s), 0) < v:
                    self.waited[(e, s)] = v
                    waits.append((s, v))
            self.q[e].append((waits, None, None))
        self.free_dsems.extend(self.stage_dsems)
        self.stage_dsems = []

    def emit(self):
        nc = self.nc
        with nc.Block() as block:
            def mk(e):
                def body(engine):
                    eng = engine
                    if e == "pe":
                        eng = _PEProxy(engine, self.semh["E_pe"])
                    for waits, fn, inc in self.q[e]:
                        for s, v in waits:
                            engine.wait_ge(self.semh[s], v)
                        if fn is not None:
                            ins = fn(eng)
                            ins.then_inc(self.semh[inc[0]], inc[1])
                            if e == "pe" and inc[0] == "E_pe":
                                eng.n += 1
                return body
            block.tensor(mk("pe"))
            block.scalar(mk("act"))
            block.vector(mk("dve"))
            block.gpsimd(mk("pool"))
            block.sync(mk("sp"))


_CONST = None


def _constants():
    global _CONST
    if _CONST is not None:
        return _CONST
    N = 2 * T
    t = np.arange(T, dtype=np.float64)
    f = np.arange(T, dtype=np.float64)
    ang = (2.0 * np.pi / N) * ((t[:, None] * f[None, :]) % N)
    FT = np.zeros((T, 2 * T), np.float64)
    FT[:, :T] = np.cos(ang)
    FT[:, T:] = -np.sin(ang)
    FT[:, T] = np.where(t % 2 == 0, 1.0, -1.0)
    FTh = FT.reshape(16, 128, 32, 128).transpose(2, 1, 0, 3)
    wf = np.full(T, 2.0); wf[0] = 1.0
    GI = np.zeros((2 * T, T), np.float64)
    GI[:T, :] = (wf[:, None] * np.cos(ang.T)) / N
    GI[T:, :] = (-2.0 * np.sin(ang.T)) / N
    GI[T, :] = np.where(t % 2 == 0, 1.0, -1.0) / N
    GIh = GI.reshape(32, 128, 8, 256).transpose(2, 1, 0, 3)
    L = T
    tl = np.linspace(0.0, 1.0, L, dtype=np.float32)[:, None]
    w = (np.float32(2.0 * np.pi / L) * np.arange(L, dtype=np.float32))[:, None]
    fb = np.linspace(1e-4, 15.0, 16, dtype=np.float32)[None, :]
    feats = np.concatenate([tl, np.cos(fb * w), -np.sin(fb * w)], axis=-1).astype(np.float32)
    s = np.arange(128)[:, None]; tt = np.arange(128)[None, :]
    same = (s // 64) == (tt // 64)
    U_f = (same & (s <= tt)).astype(np.float32)
    U_b = (same & (s >= tt)).astype(np.float32)
    M_f = (same & (s > tt)).astype(np.float32)
    M_b = (same & (s < tt)).astype(np.float32)
    c = np.arange(256)[None, :] % 64
    p = np.arange(128)[:, None] % 64
    maskF = (p <= c).astype(np.float32)
    maskB = (p >= c).astype(np.float32)
    _CONST = dict(
        FTh=np.ascontiguousarray(FTh).astype(ml_dtypes.bfloat16),
        GIh=np.ascontiguousarray(GIh).astype(ml_dtypes.bfloat16),
        featsT=np.ascontiguousarray(feats.T),
        tneg=np.ascontiguousarray(-tl[:, 0].reshape(16, 128).T),
        tri=np.ascontiguousarray(np.stack([U_f, U_b, M_f, M_b], 1)),
        masks=np.ascontiguousarray(np.stack([maskF, maskB], 1)),
        ident=np.eye(128, dtype=np.float32),
    )
    return _CONST


def build(debug=False, upto="ALL"):
    nc = bass.Bass("TRN2", target_bir_lowering=False)
    P = Prog(nc)
    dbg_out = {}

    def din(name, shape, dt=F32):
        return nc.dram_tensor(name, list(shape), dt, kind="ExternalInput").ap()

    def dscr(name, shape, dt=F32):
        kind = "ExternalOutput" if debug else "Internal"
        if debug:
            dbg_out[name] = True
        return nc.dram_tensor(name, list(shape), dt, kind=kind).ap()

    xT = din("xT", [2, D, T]); ctxT = din("ctxT", [2, D, TC]); c3T = din("c3T", [128, 8, 3])
    w_ada = din("w_ada", [D, 6 * D]); b_adaT = din("b_adaT", [128, 48])
    nw = din("nw", [128, 3, 8])
    w_in = din("w_in", [D, INC])
    hy_cw = din("hy_cw", [128, 12, 3]); hy_bias = din("hy_bias", [128, 4])
    f_w1 = din("f_w1", [33, 64]); f_w2 = din("f_w2", [64, 64]); f_w3 = din("f_w3", [64, 1024])
    f_vec = din("f_vec", [64, 4])
    f_decay = din("f_decay", [1, 1024])
    w_ext = din("w_ext", [33, 512]); gnw = din("gnw", [128, 4])
    w_hyp = din("w_hyp", [HYW, D]); w_glp = din("w_glp", [512, D]); w_out = din("w_out", [D, D])
    ffn_wi = din("ffn_wi", [D, 2 * FFH]); ffn_dw = din("ffn_dw", [128, NJ, 9]); ffn_db = din("ffn_db", [128, NJ])
    ffn_wo = din("ffn_wo", [FFH, D])
    FTh = din("FTh", [32, 128, 16, 128], BF16); GIh = din("GIh", [8, 128, 32, 256], BF16)
    featsT = din("featsT", [33, T]); tneg_d = din("tneg", [128, 16])
    tri_d = din("tri", [128, 4, 128]); masks_d = din("masks", [128, 2, 256]); ident_d = din("ident", [128, 128])
    outT = nc.dram_tensor("outT", [2, D, T], F32, kind="ExternalOutput").ap()

    ka_d = dscr("ka_d", [16, 128, 512], BF16); kb_d = dscr("kb_d", [16, 128, 512], BF16)
    x0_d = dscr("x0_d", [2, HYW, T]); z_d = dscr("z_d", [2, HYW, T])
    qT_d = dscr("qT_d", [2, 256, T]); kT_d = dscr("kT_d", [2, 256, T]); ktok_d = dscr("ktok_d", [2, T, 256])
    vtok_d = dscr("vtok_d", [2, T, 512], BF16); rs_d = dscr("rs_d", [2, 512, T]); aT_d = dscr("aT_d", [2, 32, T])
    gate_d = dscr("gate_d", [2, 2 * D, T], BF16)
    ckT_d = dscr("ckT_d", [2, 256, TC]); cktok_d = dscr("cktok_d", [2, TC, 256])
    cvtok_d = dscr("cvtok_d", [2, TC, 512], BF16); caT_d = dscr("caT_d", [2, 32, TC])
    ygla_d = dscr("ygla_d", [2, 512, T], BF16); yhy_d = dscr("yhy_d", [2, HYW, T], BF16)
    xnew_d = dscr("xnew_d", [2, D, T]); h2_d = dscr("h2_d", [2, D, T], BF16)
    act_d = dscr("act_d", [2, FFH, T], BF16)
    R = {n: P.res(n) for n in ["ka_d", "kb_d", "x0_d", "z_d", "qT_d", "kT_d", "ktok_d", "vtok_d", "rs_d", "aT_d",
                               "gate_d", "ckT_d", "cktok_d", "cvtok_d", "caT_d", "ygla_d", "yhy_d", "xnew_d",
                               "h2_d", "act_d", "outT", "dbg"]}

    top = ExitStack()

    uid = [0]

    def sb(es, name, shape, dt=F32, dma=False, keep=False):
        uid[0] += 1
        t = es.enter_context(nc.sbuf_tensor("%s_%d" % (name, uid[0]), list(shape), dt))
        return t, P.res(name, dma=dma, keep=keep)

    def ps(es, name, dt=F32, cols=512):
        uid[0] += 1
        t = es.enter_context(nc.psum_tensor("%s_%d" % (name, uid[0]), [128, cols], dt))
        return t, P.res(name)

    def dbg_dump(name, tile_ap, shape, dt, rres, sem_res):
        if not debug:
            return
        o = nc.dram_tensor(name, list(shape), dt, kind="ExternalOutput").ap()
        dbg_out[name] = True
        P.dma("sp", lambda e: e.dma_start(out=o, in_=tile_ap), sem_res, reads=[rres], pwrites=[R["dbg"]])

    ident_f, r_identf = sb(top, "ident_f", [128, 128], F32, dma=True, keep=True)
    ident_b, r_identb = sb(top, "ident_b", [128, 128], BF16)
    ones_b, r_onesb = sb(top, "ones_b", [128, 128], BF16)
    ones_f, r_onesf = sb(top, "ones_f", [128, 128], F32)
    mod, r_mod = sb(top, "mod", [128, 48, 3], F32, dma=True, keep=True)
    a1, r_a1 = sb(top, "a1", [128, 8, 3], F32)
    a2, r_a2 = sb(top, "a2", [128, 8, 3], F32)
    nwt, r_nwt = sb(top, "nwt", [128, 3, 8], F32, dma=True, keep=True)
    P.dma("sp", lambda e: e.dma_start(out=ident_f[:], in_=ident_d), r_identf, writes=[r_identf])
    P.dma("sp", lambda e: e.dma_start(out=nwt[:], in_=nw), r_nwt, writes=[r_nwt])
    P.op("dve", lambda e: e.tensor_copy(out=ident_b[:], in_=ident_f[:]), reads=[r_identf], writes=[r_identb])
    P.op("dve", lambda e: e.memset(ones_b[:], 1.0), writes=[r_onesb])
    P.op("dve", lambda e: e.memset(ones_f[:], 1.0), writes=[r_onesf])
    SH1, SC1, G1, SH2, SC2, G2 = 0, 8, 16, 24, 32, 40

    esA = ExitStack()
    c3, r_c3 = sb(esA, "c3", [128, 8, 3], F32, dma=True)
    sc, r_sc = sb(esA, "sc", [128, 8, 3], BF16)
    bad, r_bad = sb(esA, "bad", [128, 48], F32, dma=True)
    wa = [sb(esA, "wa%d" % i, [128, 8, 512], F32, dma=True) for i in range(2)]
    wab = [sb(esA, "wab%d" % i, [128, 8, 512], BF16) for i in range(2)]
    mrows = [sb(esA, "mrow%d" % i, [3, 512], F32) for i in range(2)]
    pm, r_pm = ps(esA, "pm")
    pa = [ps(esA, "pa%d" % i) for i in range(2)]
    P.dma("sp", lambda e: e.dma_start(out=c3[:], in_=c3T), r_c3, writes=[r_c3])
    P.dma("sp", lambda e: e.dma_start(out=bad[:], in_=b_adaT), r_bad, writes=[r_bad])
    P.op("act", lambda e: e.activation(out=sc[:], in_=c3[:], func=AF.Silu), reads=[r_c3], writes=[r_sc])
    wav = w_ada.rearrange("(k p) m -> p k m", p=128)
    a_state = {"ld": 0, "g": 0}

    def ld_wa():
        g = a_state["ld"]
        if g >= 12:
            return
        a_state["ld"] += 1
        wt, r_wt = wa[g % 2]
        P.dma("sp", lambda e, wt=wt, g=g: e.dma_start(out=wt[:], in_=wav[:, :, g * 512:(g + 1) * 512]),
              r_wt, writes=[r_wt])

    def emit_A_group():
        g = a_state["g"]
        if g >= 12:
            return
        a_state["g"] += 1
        wf_, r_wf_ = wa[g % 2]
        wt, r_wt = wab[g % 2]
        pt, r_pt = pa[g % 2]
        mr, r_mr = mrows[g % 2]
        for hh in range(2):
            if hh == 0:
                P.op("act", lambda e, wf_=wf_, wt=wt: e.activation(out=wt[:, 0:5, :], in_=wf_[:, 0:5, :], func=AF.Identity),
                     reads=[r_wf_], pwrites=[r_wt])
            else:
                P.op("pool", lambda e, wf_=wf_, wt=wt: e.tensor_copy(out=wt[:, 5:8, :], in_=wf_[:, 5:8, :]),
                     reads=[r_wf_], pwrites=[r_wt])
        ld_wa()
        for k in range(8):
            P.op("pe", lambda e, wt=wt, pt=pt, k=k: e.matmul(
                pt[0:3, :], lhsT=sc[:, k, :], rhs=wt[:, k, :], start=(k == 0), stop=(k == 7)),
                reads=[r_wt, r_sc], writes=[r_pt])
        P.op("act", lambda e, pt=pt, mr=mr: e.activation(out=mr[:], in_=pt[0:3, :], func=AF.Identity),
             reads=[r_pt], writes=[r_mr])
        for mc in range(4):
            m = g * 4 + mc
            P.op("pe", lambda e, m=m, mc=mc, mr=mr: e.transpose(out=pm[:, m * 3:(m + 1) * 3], in_=mr[0:3, mc * 128:(mc + 1) * 128],
                                                                identity=ident_f[0:3, 0:3]),
                 reads=[r_mr, r_identf], writes=[r_pm])

    a_slots = [0]

    def a_slot():
        a_slots[0] += 1
        if a_slots[0] % 7 == 2:
            emit_A_group()

    def finish_A():
        while a_state["g"] < 12:
            emit_A_group()
        pmv = pm[:, 0:144].rearrange("p (m j) -> p m j", j=3)
        for j in range(3):
            P.op("dve", lambda e, j=j: e.tensor_tensor(out=mod[:, :, j], in0=pmv[:, :, j], in1=bad[:], op=ALU.add),
                 reads=[r_pm, r_bad], writes=[r_mod])
        for j in range(3):
            P.op("dve", lambda e, j=j: e.scalar_tensor_tensor(out=a1[:, :, j], in0=mod[:, SC1:SC1 + 8, j], scalar=1.0,
                                                             in1=nwt[:, 0, :], op0=ALU.add, op1=ALU.mult),
                 reads=[r_mod, r_nwt], writes=[r_a1])
            P.op("dve", lambda e, j=j: e.scalar_tensor_tensor(out=a2[:, :, j], in0=mod[:, SC2:SC2 + 8, j], scalar=1.0,
                                                             in1=nwt[:, 1, :], op0=ALU.add, op1=ALU.mult),
                 reads=[r_mod, r_nwt], writes=[r_a2])
        dbg_dump("dbg_mod", mod[:], [128, 48, 3], F32, r_mod, r_mod)

    with ExitStack() as es:
        ft, r_ft = sb(es, "ft", [33, T], F32, dma=True)
        fw1, r_fw1 = sb(es, "fw1", [33, 64], F32, dma=True)
        fw2, r_fw2 = sb(es, "fw2", [64, 64], F32, dma=True)
        fw3, r_fw3 = sb(es, "fw3", [64, 1024], F32, dma=True)
        fv, r_fv = sb(es, "fv", [64, 4], F32, dma=True)
        fb, r_fb = sb(es, "fb", [64, 2], F32)
        fdec, r_fdec = sb(es, "fdec", [1, 1024], F32, dma=True)
        tv, r_tv = sb(es, "tv", [128, 16], F32, dma=True)
        decb, r_decb = sb(es, "decb", [128, 1024], F32)
        acc, r_acc = sb(es, "kacc", [128, 512], F32)
        h1, r_h1 = sb(es, "h1", [64, T], F32)
        h2f, r_h2f = sb(es, "h2f", [64, T], F32)
        arg, r_arg = sb(es, "arg", [64, 512], F32)
        mm_, r_mm = sb(es, "mm_", [64, 512], F32)
        kf, r_kf = sb(es, "kf", [128, 16, 512], F32)
        k2, r_k2 = sb(es, "k2", [128, 16, 512], F32)
        ee, r_ee = sb(es, "ee", [128, 512], F32)
        abs_ = [sb(es, "ab%d" % i, [128, 512], BF16) for i in range(2)]
        rn, r_rn = sb(es, "rn", [128, 512], F32)
        tmpa, r_tmpa = sb(es, "tmpa", [128, 512], F32)
        tmpb, r_tmpb = sb(es, "tmpb", [128, 512], F32)
        kst = [sb(es, "kst%d" % i, [128, 512], BF16, dma=True) for i in range(4)]
        pp = [ps(es, "fps%d" % i) for i in range(4)]
        pn, r_pn = ps(es, "fpn")
        for tl_, rr, src in ((ft, r_ft, featsT), (fw1, r_fw1, f_w1), (fw2, r_fw2, f_w2), (fw3, r_fw3, f_w3),
                             (fv, r_fv, f_vec), (fdec, r_fdec, f_decay), (tv, r_tv, tneg_d)):
            P.dma("sp", lambda e, tl_=tl_, src=src: e.dma_start(out=tl_[:], in_=src), rr, writes=[rr])
        ld_wa(); ld_wa()
        P.op("dve", lambda e: e.tensor_scalar(out=fb[:], in0=fv[:, 0:2], scalar1=fv[:, 2:3], scalar2=None,
                                              op0=ALU.mult), reads=[r_fv], writes=[r_fb])

        def sin_layer(wt, r_w, kdim, src, r_src, dst, r_dst, li):
            for n in range(4):
                pt, r_pt = pp[n % 4]
                P.op("pe", lambda e, pt=pt, n=n: e.matmul(pt[0:64, :], lhsT=wt[0:kdim, :], rhs=src[0:kdim, n * 512:(n + 1) * 512],
                                                          start=True, stop=True), reads=[r_w, r_src], writes=[r_pt])
                P.op("dve", lambda e, pt=pt: e.tensor_scalar(out=arg[:], in0=pt[0:64, :], scalar1=fv[:, 2:3],
                                                             scalar2=fb[:, li:li + 1], op0=ALU.mult, op1=ALU.add),
                     reads=[r_pt, r_fv, r_fb], writes=[r_arg])
                P.op("dve", lambda e: e.tensor_scalar(out=mm_[:], in0=arg[:], scalar1=PI, scalar2=-2.0 * PI,
                                                      op0=ALU.is_gt, op1=ALU.mult), reads=[r_arg], writes=[r_mm])
                P.op("dve", lambda e: e.tensor_tensor(out=arg[:], in0=arg[:], in1=mm_[:], op=ALU.add),
                     reads=[r_mm], writes=[r_arg])
                P.op("dve", lambda e: e.tensor_scalar(out=mm_[:], in0=arg[:], scalar1=-PI, scalar2=2.0 * PI,
                                                      op0=ALU.is_lt, op1=ALU.mult), reads=[r_arg], writes=[r_mm])
                P.op("dve", lambda e: e.tensor_tensor(out=arg[:], in0=arg[:], in1=mm_[:], op=ALU.add),
                     reads=[r_mm], writes=[r_arg])
                P.op("act", lambda e, n=n: e.activation(out=dst[:, n * 512:(n + 1) * 512], in_=arg[:], func=AF.Sin),
                     reads=[r_arg], writes=[r_dst])
                a_slot()

        sin_layer(fw1, r_fw1, 33, ft, r_ft, h1, r_h1, 0)
        sin_layer(fw2, r_fw2, 64, h1, r_h1, h2f, r_h2f, 1)
        for half in range(2):
            pe_, r_pe = pp[0]
            P.op("pe", lambda e, pe_=pe_, half=half: e.matmul(
                pe_[:, :], lhsT=ones_f[0:1, :], rhs=fdec[0:1, half * 512:(half + 1) * 512],
                start=True, stop=True), reads=[r_onesf, r_fdec], writes=[r_pe])
            P.op("act", lambda e, pe_=pe_, half=half: e.activation(out=decb[:, half * 512:(half + 1) * 512], in_=pe_[:, :],
                                                                   func=AF.Identity), reads=[r_pe], pwrites=[r_decb])
        for half, dst, r_dst in ((0, kf, r_kf), (1, k2, r_k2)):
            for i in range(16):
                pk, r_pk = pp[1 + (i % 2)]
                P.op("act", lambda e, i=i, half=half: e.activation(out=ee[:], in_=decb[:, half * 512:(half + 1) * 512],
                                                                   func=AF.Exp, scale=tv[:, i:i + 1]),
                     reads=[r_decb, r_tv], writes=[r_ee])
                P.op("pe", lambda e, pk=pk, i=i, half=half: e.matmul(
                    pk[:, :], lhsT=h2f[:, i * 128:(i + 1) * 128], rhs=fw3[:, half * 512:(half + 1) * 512],
                    start=True, stop=True), reads=[r_h2f, r_fw3], writes=[r_pk])
                P.op("dve", lambda e, pk=pk, dst=dst, i=i: e.tensor_tensor(out=dst[:, i, :], in0=pk[:, :], in1=ee[:],
                                                                          op=ALU.mult),
                     reads=[r_pk, r_ee], writes=[r_dst])
                a_slot()
        P.op("dve", lambda e: e.memset(k2[0:1, 0, :], 0.0), writes=[r_k2])
        cnt = 0
        for src, r_src in ((kf, r_kf), (k2, r_k2)):
            for i in range(16):
                abt, r_abt = abs_[cnt % 2]
                P.op("act", lambda e, src=src, i=i, abt=abt: e.activation(out=abt[:], in_=src[:, i, :], func=AF.Abs),
                     reads=[r_src], writes=[r_abt])
                P.op("pe", lambda e, cnt=cnt, abt=abt: e.matmul(pn[:, :], lhsT=ones_b[:], rhs=abt[:], start=(cnt == 0),
                                                                stop=(cnt == 31)), reads=[r_abt, r_onesb], writes=[r_pn])
                cnt += 1
                a_slot()
        P.op("dve", lambda e: e.reciprocal(out=rn[:], in_=pn[:, :]), reads=[r_pn], writes=[r_rn])
        for i in range(16):
            sa, r_sa = kst[(2 * i) % 4]
            sbb, r_sbb = kst[(2 * i + 1) % 4]
            P.op("dve", lambda e, i=i: e.tensor_tensor(out=tmpa[:], in0=kf[:, i, :], in1=k2[:, i, :], op=ALU.add),
                 reads=[r_kf, r_k2], writes=[r_tmpa])
            P.op("dve", lambda e, sa=sa: e.tensor_tensor(out=sa[:], in0=tmpa[:], in1=rn[:], op=ALU.mult),
                 reads=[r_tmpa, r_rn], writes=[r_sa])
            P.dma("sp", lambda e, sa=sa, i=i: e.dma_start(out=ka_d[i], in_=sa[:]), r_sa, reads=[r_sa],
                  pwrites=[R["ka_d"]])
            P.op("pool", lambda e, i=i: e.tensor_tensor(out=tmpb[:], in0=kf[:, i, :], in1=k2[:, i, :], op=ALU.subtract),
                 reads=[r_kf, r_k2], writes=[r_tmpb])
            P.op("pool", lambda e, sbb=sbb: e.tensor_tensor(out=sbb[:], in0=tmpb[:], in1=rn[:], op=ALU.mult),
                 reads=[r_tmpb, r_rn], writes=[r_sbb])
            P.dma("sp", lambda e, sbb=sbb, i=i: e.dma_start(out=kb_d[i], in_=sbb[:]), r_sbb, reads=[r_sbb],
                  pwrites=[R["kb_d"]])
            a_slot()
        finish_A()
        P.barrier()
    esA.close()
    if upto == "F":
        return _finish(nc, P, R, dbg_out)

    epst, r_epst = sb(top, "epst", [128, 1], F32)
    P.op("dve", lambda e: e.memset(epst[:], EPS), writes=[r_epst])
    mid = ExitStack()
    ztok, r_ztok = sb(mid, "ztok", [128, 2, 16, 512], BF16)
    cwt, r_cwt = sb(mid, "cwt", [128, 12, 3], F32, dma=True, keep=True)
    P.dma("sp", lambda e: e.dma_start(out=cwt[:], in_=hy_cw), r_cwt, writes=[r_cwt])
    tri, r_tri = sb(mid, "tri", [128, 4, 128], F32, dma=True, keep=True)
    msk, r_msk = sb(mid, "msk", [128, 2, 256], F32, dma=True, keep=True)
    wext, r_wext = sb(mid, "wext", [33, 512], F32, dma=True, keep=True)
    gnwt, r_gnwt = sb(mid, "gnwt", [128, 4], F32, dma=True, keep=True)
    for tl_, rr, src in ((tri, r_tri, tri_d), (msk, r_msk, masks_d), (wext, r_wext, w_ext), (gnwt, r_gnwt, gnw)):
        P.dma("sp", lambda e, tl_=tl_, src=src: e.dma_start(out=tl_[:], in_=src), rr, writes=[rr])

    def emit_norm(xt, r_xt, W, a_t, r_a, shoff, j, dst, r_dst, tl, phase="both"):
        sq, r_sq, pss, r_pss, std, r_std, tmp, r_tmp = tl
        if phase in ("both", "stats"):
            _norm_stats(xt, r_xt, W, sq, r_sq, pss, r_pss, std, r_std)
        if phase in ("both", "apply"):
            _norm_apply(xt, r_xt, W, a_t, r_a, shoff, j, dst, r_dst, std, r_std, tmp, r_tmp)

    def _norm_stats(xt, r_xt, W, sq, r_sq, pss, r_pss, std, r_std):
        P.op("act", lambda e: e.activation(out=sq[:, :, 0:W], in_=xt[:, :, 0:W], func=AF.Square),
             reads=[r_xt], writes=[r_sq])
        for k in range(8):
            P.op("pe", lambda e, k=k: e.matmul(pss[:, 0:W], lhsT=ones_b[:], rhs=sq[:, k, 0:W], start=(k == 0),
                                               stop=(k == 7)), reads=[r_sq, r_onesb], writes=[r_pss])
        P.op("act", lambda e: e.activation(out=std[:, 0:W], in_=pss[:, 0:W], func=AF.Ln, scale=1.0 / D,
                                           bias=epst[:, 0:1]), reads=[r_pss, r_epst], writes=[r_std])
        P.op("act", lambda e: e.activation(out=std[:, 0:W], in_=std[:, 0:W], func=AF.Exp, scale=-0.5),
             reads=[r_std], writes=[r_std])

    def _norm_apply(xt, r_xt, W, a_t, r_a, shoff, j, dst, r_dst, std, r_std, tmp, r_tmp):
        for k in range(8):
            tk, r_tk = tmp[k % 2], r_tmp[k % 2]
            P.op("dve", lambda e, k=k, tk=tk: e.scalar_tensor_tensor(out=tk[:, 0:W], in0=xt[:, k, 0:W],
                                                                    scalar=a_t[:, k, j:j + 1], in1=std[:, 0:W],
                                                                    op0=ALU.mult, op1=ALU.mult),
                 reads=[r_xt, r_a, r_std], writes=[r_tk])
            P.op("act", lambda e, k=k, tk=tk: e.activation(out=dst(k), in_=tk[:, 0:W], func=AF.Identity,
                                                           bias=mod[:, shoff + k, j:j + 1], scale=1.0),
                 reads=[r_tk, r_mod], pwrites=[r_dst])


    def stage_P(s, is_ctx):
        Tn = 2 * TC if is_ctx else T
        W = min(512, Tn)
        ntt = Tn // W
        jmod = 2 if is_ctx else s
        if is_ctx:
            src = None
        else:
            src = xT[s].rearrange("(k p) t -> p k t", p=128)
        with ExitStack() as es:
            hx, r_hx = sb(es, "hx", [128, 8, Tn], BF16)
            with ExitStack() as es1:
                xts = [sb(es1, "xt%d" % i, [128, 8, W], F32, dma=True) for i in range(2)]
                sqs = [sb(es1, "sq%d" % i, [128, 8, W], BF16) for i in range(2)]
                tmps = [sb(es1, "ntmp%d" % i, [128, W], F32) for i in range(2)]
                tmp = [t_[0] for t_ in tmps]; r_tmp = [t_[1] for t_ in tmps]
                stds = [sb(es1, "std%d" % i, [128, W], F32) for i in range(2)]
                psss = [ps(es1, "pss%d" % i) for i in range(2)]
                def ld_x(tt):
                    xt, r_xt = xts[tt % 2]
                    if is_ctx:
                        for sq_ in range(2):
                            P.dma("sp", lambda e, xt=xt, sq_=sq_: e.dma_start(
                                out=xt[:, :, sq_ * TC:(sq_ + 1) * TC],
                                in_=ctxT[sq_].rearrange("(k p) t -> p k t", p=128)), r_xt,
                                writes=[r_xt] if sq_ == 0 else [], pwrites=[] if sq_ == 0 else [r_xt])
                    else:
                        P.dma("sp", lambda e, xt=xt, tt=tt: e.dma_start(out=xt[:], in_=src[:, :, tt * W:(tt + 1) * W]),
                              r_xt, writes=[r_xt])
                def p1(tt, phase):
                    xt, r_xt = xts[tt % 2]
                    sq, r_sq = sqs[tt % 2]; std, r_std = stds[tt % 2]; pss, r_pss = psss[tt % 2]
                    emit_norm(xt, r_xt, W, a1, r_a1, SH1, jmod,
                              lambda k, tt=tt: hx[:, k, tt * W:(tt + 1) * W], r_hx,
                              (sq, r_sq, pss, r_pss, std, r_std, tmp, r_tmp), phase=phase)
                ld_x(0)
                if ntt > 1:
                    ld_x(1)
                p1(0, "stats")
                for tt in range(ntt):
                    if tt + 1 < ntt:
                        p1(tt + 1, "stats")
                    p1(tt, "apply")
                    if tt + 2 < ntt:
                        ld_x(tt + 2)
                P.barrier()
            wbufs = [sb(es, "wb%d" % i, [128, 8, 512], BF16, dma=True) for i in range(2)]
            stg = [sb(es, "stg%d" % i, [128, Tn], F32, dma=True) for i in range(2)]
            stgb = [sb(es, "stgb%d" % i, [128, Tn], BF16, dma=True) for i in range(2)]
            ktk = [sb(es, "ktk%d" % i, [128, 256], F32, dma=True) for i in range(2)]
            vtk = [sb(es, "vtk%d" % i, [128, 512], BF16, dma=True) for i in range(2)]
            pps = [ps(es, "pp%d" % i) for i in range(4)]
            ptb, r_ptb = ps(es, "ptb", BF16, 1024)
            winv = w_in.rearrange("(k p) m -> p k m", p=128)
            cnt = {"w": 0, "ps": 0, "stg": 0, "stgb": 0}

            def fm_chunk(wt, r_wt, coff, M, evac):
                for tt in range(ntt):
                    pt, r_pt = pps[cnt["ps"] % 4]
                    cnt["ps"] += 1
                    for k in range(8):
                        P.op("pe", lambda e, pt=pt, k=k, tt=tt: e.matmul(
                            pt[0:M, 0:W], lhsT=wt[:, k, coff:coff + M], rhs=hx[:, k, tt * W:(tt + 1) * W],
                            start=(k == 0), stop=(k == 7)), reads=[r_wt, r_hx], writes=[r_pt])
                    evac(tt, pt, r_pt)

            def next_stg():
                t_ = stg[cnt["stg"] % 2]
                cnt["stg"] += 1
                return t_

            def next_stgb():
                t_ = stgb[cnt["stgb"] % 2]
                cnt["stgb"] += 1
                return t_

            groups = []
            simple = []

            def add_group(c0, nch, M, dst_ap_fn, r_dst, func, bf):
                done = 0
                while done < nch:
                    n_here = min(4, nch - done)
                    gid = len(groups)
                    groups.append((c0 + done * 128, n_here * M if M < 128 else n_here * 128))
                    for q in range(n_here):
                        simple.append(("fm", gid, q, M, dst_ap_fn(done + q), r_dst, func, bf))
                    done += n_here

            if not is_ctx:
                add_group(1536, 2, 128, lambda i: qT_d[s, i * 128:(i + 1) * 128, :], R["qT_d"], AF.Identity, False)
                add_group(1792, 2, 128, lambda i: kT_d[s, i * 128:(i + 1) * 128, :], R["kT_d"], AF.Identity, False)
                add_group(3072, 1, 32, lambda i: aT_d[s, :, :], R["aT_d"], AF.Identity, False)
            else:
                add_group(1792, 2, 128, lambda i: ckT_d[:, i * 128:(i + 1) * 128, :].rearrange("s m t -> m s t"),
                          R["ckT_d"], AF.Identity, False)
                add_group(3072, 1, 32, lambda i: caT_d.rearrange("s m t -> m s t"), R["caT_d"], AF.Identity, False)
            if not is_ctx:
                add_group(2560, 4, 128, lambda i: rs_d[s, i * 128:(i + 1) * 128, :], R["rs_d"], AF.Silu, False)
                add_group(3104, 16, 128, lambda i: gate_d[s, i * 128:(i + 1) * 128, :], R["gate_d"], AF.Sigmoid, True)
            gkv = len(groups)
            groups.append((1792, 512))
            groups.append((2304, 256))
            kvt = [("kv", tc_) for tc_ in range(Tn // 128)]
            if is_ctx:
                order = simple + kvt
            else:
                order = []
                si = 0
                for u in [("x0", j) for j in range(4)] + [("pair", j) for j in range(4)]:
                    order.append(u)
                    take = 1 if u[0] == "x0" else 2
                    order += simple[si:si + take]
                    si += take
                order += simple[si:] + kvt
            issued = [0]

            def issue_upto(n):
                while issued[0] < min(n, len(groups)):
                    i = issued[0]
                    c0, ncols = groups[i]
                    wt, r_wt = wbufs[i % 2]
                    P.dma("pool", lambda e, wt=wt, c0=c0, ncols=ncols: e.dma_start(
                        out=wt[:, :, 0:ncols], in_=winv[:, :, c0:c0 + ncols]), r_wt, writes=[r_wt])
                    issued[0] += 1

            def get_w(gid, extra=1):
                if gid >= cnt["w"]:
                    assert gid == cnt["w"], (gid, cnt["w"])
                    cnt["w"] = gid + 1
                    issue_upto(gid + 1 + extra)
                return wbufs[gid % 2]

            if not is_ctx:
                whx = [sb(es, "whx%d" % i, [128, 8, 512], BF16, dma=True) for i in range(3)]
                for i in range(3):
                    P.dma("pool", lambda e, i=i: e.dma_start(out=whx[i][0][:], in_=winv[:, :, i * 512:(i + 1) * 512]),
                          whx[i][1], writes=[whx[i][1]])
                ub = [sb(es, "ub%d" % i, [128, T + 2], F32) for i in range(2)]
                x1c, r_x1c = sb(es, "x1c", [128, T], F32)
                zb, r_zb = sb(es, "zb", [128, T], BF16)
                for u_, r_u in ub:
                    P.op("dve", lambda e, u_=u_: e.memset(u_[:, 0:1], 0.0), writes=[r_u])
                    P.op("dve", lambda e, u_=u_: e.memset(u_[:, T + 1:T + 2], 0.0), writes=[r_u])

                def conv(u_, r_u, m, dst, r_dst):
                    P.op("dve", lambda e: e.scalar_tensor_tensor(out=dst[:, 0:T], in0=u_[:, 0:T], scalar=cwt[:, m, 0:1],
                                                                 in1=dst[:, 0:T], op0=ALU.mult, op1=ALU.add),
                         reads=[r_u, r_cwt], writes=[r_dst])
                    P.op("dve", lambda e: e.scalar_tensor_tensor(out=dst[:, 0:T], in0=u_[:, 2:T + 2], scalar=cwt[:, m, 2:3],
                                                                 in1=dst[:, 0:T], op0=ALU.mult, op1=ALU.add),
                         reads=[r_u, r_cwt], writes=[r_dst])

                def evac_to(u_, r_u, m, dst, r_dst):
                    def evac(tt, pt, r_pt):
                        P.op("act", lambda e: e.activation(out=u_[:, 1 + tt * W:1 + (tt + 1) * W], in_=pt[:, 0:W],
                                                           func=AF.Identity), reads=[r_pt], pwrites=[r_u])
                        P.op("act", lambda e: e.activation(out=dst[:, tt * W:(tt + 1) * W], in_=pt[:, 0:W],
                                                           func=AF.Identity, scale=cwt[:, m, 1:2]),
                             reads=[r_pt, r_cwt], pwrites=[r_dst])
                    return evac

                def do_x0(j):
                    wt, r_wt = whx[0]
                    u_, r_u = ub[j % 2]
                    st, r_st = next_stg()
                    fm_chunk(wt, r_wt, j * 128, 128, evac_to(u_, r_u, j, st, r_st))
                    conv(u_, r_u, j, st, r_st)
                    P.dma("sp", lambda e, st=st, j=j: e.dma_start(out=x0_d[s, j * 128:(j + 1) * 128, :], in_=st[:]),
                          r_st, reads=[r_st], pwrites=[R["x0_d"]])

                def do_pair(j):
                    w1t, r_w1t = whx[1]
                    w2t, r_w2t = whx[2]
                    fm_chunk(w1t, r_w1t, j * 128, 128, evac_to(ub[0][0], ub[0][1], 4 + j, x1c, r_x1c))
                    conv(ub[0][0], ub[0][1], 4 + j, x1c, r_x1c)
                    st, r_st = next_stg()
                    fm_chunk(w2t, r_w2t, j * 128, 128, evac_to(ub[1][0], ub[1][1], 8 + j, st, r_st))
                    conv(ub[1][0], ub[1][1], 8 + j, st, r_st)
                    P.op("dve", lambda e, st=st: e.tensor_tensor(out=st[:], in0=st[:], in1=x1c[:], op=ALU.mult),
                         reads=[r_x1c], writes=[r_st])
                    P.dma("sp", lambda e, st=st, j=j: e.dma_start(out=z_d[s, j * 128:(j + 1) * 128, :], in_=st[:]),
                          r_st, reads=[r_st], pwrites=[R["z_d"]])
                    P.op("act", lambda e, st=st: e.activation(out=zb[:], in_=st[:], func=AF.Identity),
                         reads=[r_st], writes=[r_zb])
                    for g in range(4):
                        for q in range(4):
                            tc_ = 4 * g + q
                            P.op("pe", lambda e, q=q, tc_=tc_: e.transpose(out=ptb[:, q * 128:(q + 1) * 128],
                                                                            in_=zb[:, tc_ * 128:(tc_ + 1) * 128],
                                                                            identity=ident_b[:]),
                                 reads=[r_zb, r_identb], writes=[r_ptb])
                        P.op("act", lambda e, g=g, j=j: e.activation(
                            out=ztok[:, s, 4 * g:4 * g + 4, j * 128:(j + 1) * 128],
                            in_=ptb[:, 0:512].rearrange("p (q c) -> p q c", c=128), func=AF.Identity),
                            reads=[r_ptb], pwrites=[r_ztok])

            def do_fm(gid, q, M, dap, r_dst, func, bf):
                wt, r_wt = get_w(gid)
                st, r_st = next_stgb() if bf else next_stg()

                def evac(tt, pt, r_pt):
                    P.op("act", lambda e: e.activation(out=st[0:M, tt * W:(tt + 1) * W], in_=pt[0:M, 0:W],
                                                       func=func), reads=[r_pt], pwrites=[r_st])
                fm_chunk(wt, r_wt, q * 128, M, evac)
                src_ap = st[0:M, :].rearrange("m (s t) -> m s t", s=2) if is_ctx else st[0:M, :]
                P.dma("sp", lambda e: e.dma_start(out=dap, in_=src_ap), r_st, reads=[r_st], pwrites=[r_dst])

            ktd = cktok_d if is_ctx else ktok_d
            vtd = cvtok_d if is_ctx else vtok_d
            r_ktd = R["cktok_d" if is_ctx else "ktok_d"]
            r_vtd = R["cvtok_d" if is_ctx else "vtok_d"]

            def do_kv(tc_):
                wa_, r_wa_ = get_w(gkv, extra=2)
                wb_, r_wb_ = get_w(gkv + 1, extra=0)
                p1, r_p1 = pps[cnt["ps"] % 4]; cnt["ps"] += 1
                p2, r_p2 = pps[cnt["ps"] % 4]; cnt["ps"] += 1
                for k in range(8):
                    P.op("pe", lambda e, k=k: e.matmul(
                        p1[:, 0:512], lhsT=hx[:, k, tc_ * 128:(tc_ + 1) * 128], rhs=wa_[:, k, 0:512],
                        start=(k == 0), stop=(k == 7)), reads=[r_hx, r_wa_], writes=[r_p1])
                for k in range(8):
                    P.op("pe", lambda e, k=k: e.matmul(
                        p2[:, 0:256], lhsT=hx[:, k, tc_ * 128:(tc_ + 1) * 128], rhs=wb_[:, k, 0:256],
                        start=(k == 0), stop=(k == 7)), reads=[r_hx, r_wb_], writes=[r_p2])
                kt, r_kt = ktk[tc_ % 2]
                vt, r_vt = vtk[tc_ % 2]
                P.op("act", lambda e: e.activation(out=kt[:], in_=p1[:, 0:256], func=AF.Identity),
                     reads=[r_p1], writes=[r_kt])
                P.op("act", lambda e: e.activation(out=vt[:, 0:256], in_=p1[:, 256:512], func=AF.Identity),
                     reads=[r_p1], writes=[r_vt])
                P.op("act", lambda e: e.activation(out=vt[:, 256:512], in_=p2[:, 0:256], func=AF.Identity),
                     reads=[r_p2], pwrites=[r_vt])
                so, tl = (tc_ // 2, tc_ % 2) if is_ctx else (s, tc_)
                P.dma("sp", lambda e: e.dma_start(out=ktd[so, tl * 128:(tl + 1) * 128, :], in_=kt[:]),
                      r_kt, reads=[r_kt], pwrites=[r_ktd])
                P.dma("sp", lambda e: e.dma_start(out=vtd[so, tl * 128:(tl + 1) * 128, :], in_=vt[:]),
                      r_vt, reads=[r_vt], pwrites=[r_vtd])

            issue_upto(1)
            for t_ in order:
                if t_[0] == "x0":
                    do_x0(t_[1])
                elif t_[0] == "pair":
                    do_pair(t_[1])
                elif t_[0] == "fm":
                    do_fm(*t_[1:])
                else:
                    do_kv(t_[1])
            P.barrier()

    stage_P(0, False)
    stage_P(None, True)
    stage_P(1, False)
    if debug:
        for s in range(2):
            dbg_dump("dbg_ztok%d" % s, ztok[:, s], [128, 16, 512], BF16, r_ztok, r_mod)
        P.barrier()
    if upto == "P":
        mid.close()
        return _finish(nc, P, R, dbg_out)

    def stage_G(s):
        with ExitStack() as esS:
            S = [[sb(esS, "S%d%d" % (d_, hc), [128, 128], F32) for hc in range(2)] for d_ in range(2)]
            Sb = [[sb(esS, "Sb%d%d" % (d_, hc), [128, 128], BF16) for hc in range(2)] for d_ in range(2)]
            for d_ in range(2):
                for hc in range(2):
                    P.op("dve", lambda e, t_=S[d_][hc][0]: e.memset(t_[:], 0.0), writes=[S[d_][hc][1]])
                    P.op("dve", lambda e, t_=Sb[d_][hc][0]: e.memset(t_[:], 0.0), writes=[Sb[d_][hc][1]])
            def phase(is_ctx):
                Tn = TC if is_ctx else T
                ntc = Tn // 128
                nch = Tn // 64
                a_src = (caT_d if is_ctx else aT_d)[s]
                kt_src = (cktok_d if is_ctx else ktok_d)[s]
                vt_src = (cvtok_d if is_ctx else vtok_d)[s]
                Rn = (lambda n: R[("c" + n) if is_ctx else n])
                with ExitStack() as esP:
                    kd, r_kd = sb(esP, "kd", [128, ntc, 2, 256], BF16)
                    vt, r_vt = sb(esP, "vt", [128, ntc, 512], BF16, dma=True)
                    gT, r_gT = sb(esP, "gT", [128, 4, nch], F32)
                    if not is_ctx:
                        qe = [sb(esP, "qe%d" % d_, [128, 2, Tn], BF16) for d_ in range(2)]
                        ke = [sb(esP, "ke%d" % d_, [128, 2, Tn], BF16) for d_ in range(2)]
                        osb, r_osb = sb(esP, "osb", [128, 4, Tn], F32)
                        r_osbn = [P.res("osb_c%d" % n_) for n_ in range(nch)]
                        P.op("pool", lambda e: e.memset(osb[:], 0.0), writes=[r_osb] + r_osbn)
                    P.dma("sp", lambda e: e.dma_start(out=vt[:], in_=vt_src.rearrange("(c p) v -> p c v", p=128)),
                          r_vt, reads=[Rn("vtok_d")], writes=[r_vt])
                    with ExitStack() as es:
                        aext, r_aext = sb(es, "aext", [33, Tn], F32, dma=True)
                        lats = [sb(es, "lat%d" % i, [128, 512], F32) for i in range(2)]
                        t1s = [sb(es, "gt1%d" % i, [128, 512], F32) for i in range(2)]
                        t2s = [sb(es, "gt2%d" % i, [128, 512], F32) for i in range(2)]
                        t3s = [sb(es, "gt3%d" % i, [128, 512], F32) for i in range(2)]
                        t4s = [sb(es, "gt4%d" % i, [128, 512], F32) for i in range(2)]
                        kts = [sb(es, "kts%d" % i, [128, 256], F32, dma=True) for i in range(2)]
                        qs = [sb(es, "qs%d" % i, [128, 2, 128], F32, dma=True) for i in range(2)]
                        ks = [sb(es, "ks%d" % i, [128, 2, 128], F32, dma=True) for i in range(2)]
                        plas = [ps(es, "pla%d" % i) for i in range(2)]
                        pms = [ps(es, "pmm%d" % i) for i in range(2)]
                        pbs = [ps(es, "pbb%d" % i) for i in range(2)]
                        P.op("dve", lambda e: e.memset(aext[:], 1.0), writes=[r_aext])
                        P.dma("sp", lambda e: e.dma_start(out=aext[0:32, :], in_=a_src), r_aext,
                              reads=[Rn("aT_d")], writes=[r_aext])
                        def ld_tc(tc_):
                            tsl = slice(tc_ * 128, (tc_ + 1) * 128)
                            kt_, r_kt_ = kts[tc_ % 2]
                            P.dma("sp", lambda e, kt_=kt_, tsl=tsl: e.dma_start(out=kt_[:], in_=kt_src[tsl, :]), r_kt_,
                                  reads=[Rn("ktok_d")], writes=[r_kt_])
                            if not is_ctx:
                                q_, r_q_ = qs[tc_ % 2]
                                k_, r_k_ = ks[tc_ % 2]
                                P.dma("sp", lambda e, q_=q_, tsl=tsl: e.dma_start(
                                    out=q_[:], in_=qT_d[s].rearrange("(h p) t -> p h t", p=128)[:, :, tsl]), r_q_,
                                    reads=[R["qT_d"]], writes=[r_q_])
                                P.dma("sp", lambda e, k_=k_, tsl=tsl: e.dma_start(
                                    out=k_[:], in_=kT_d[s].rearrange("(h p) t -> p h t", p=128)[:, :, tsl]), r_k_,
                                    reads=[R["kT_d"]], writes=[r_k_])
                        def g1a(tc_):
                            tsl = slice(tc_ * 128, (tc_ + 1) * 128)
                            pla, r_pla = plas[tc_ % 2]; t1, r_t1 = t1s[tc_ % 2]; lat, r_lat = lats[tc_ % 2]
                            P.op("pe", lambda e: e.matmul(pla[:, :], lhsT=aext[0:33, tsl], rhs=wext[0:33, :],
                                                          start=True, stop=True),
                                 reads=[r_aext, r_wext], writes=[r_pla])
                            P.op("act", lambda e: e.activation(out=t1[:], in_=pla[:, :], func=AF.Exp, scale=-1.0),
                                 reads=[r_pla], writes=[r_t1])
                            P.op("act", lambda e: e.activation(out=t1[:], in_=t1[:], func=AF.Ln, bias=ones_f[:, 0:1],
                                                               scale=1.0), reads=[r_onesf], writes=[r_t1])
                            P.op("dve", lambda e: e.tensor_scalar(out=lat[:], in0=t1[:], scalar1=-1.0 / 16.0, scalar2=None,
                                                                  op0=ALU.mult), reads=[r_t1], writes=[r_lat])

                        def g1b(tc_):
                            tsl = slice(tc_ * 128, (tc_ + 1) * 128)
                            lat, r_lat = lats[tc_ % 2]
                            pm_, r_pm_ = pms[tc_ % 2]; pb_, r_pb_ = pbs[tc_ % 2]
                            t2, r_t2 = t2s[tc_ % 2]; t3, r_t3 = t3s[tc_ % 2]; t4, r_t4 = t4s[tc_ % 2]
                            P.op("pe", lambda e: e.matmul(pm_[:, 0:256], lhsT=tri[:, 2, :], rhs=lat[:, 0:256],
                                                          start=True, stop=True), reads=[r_tri, r_lat], writes=[r_pm_])
                            P.op("pe", lambda e: e.matmul(pm_[:, 256:512], lhsT=tri[:, 3, :], rhs=lat[:, 256:512],
                                                          start=True, stop=True), reads=[r_tri, r_lat], writes=[r_pm_])
                            for fc in range(4):
                                P.op("pe", lambda e, fc=fc: e.matmul(
                                    pb_[:, fc * 128:(fc + 1) * 128], lhsT=lat[:, fc * 128:(fc + 1) * 128],
                                    rhs=tri[:, 0 if fc < 2 else 1, :], start=True, stop=True),
                                    reads=[r_tri, r_lat], writes=[r_pb_])
                            P.op("act", lambda e: e.activation(out=t2[:], in_=pm_[:, :], func=AF.Exp),
                                 reads=[r_pm_], writes=[r_t2])
                            P.op("act", lambda e: e.activation(out=t3[:], in_=pb_[:, :], func=AF.Exp),
                                 reads=[r_pb_], writes=[r_t3])
                            if not is_ctx:
                                P.op("act", lambda e: e.activation(out=t4[:], in_=pb_[:, :], func=AF.Exp, scale=-1.0),
                                     reads=[r_pb_], writes=[r_t4])
                            kt_, r_kt_ = kts[tc_ % 2]
                            for d_ in range(2):
                                P.op("dve", lambda e, d_=d_: e.tensor_tensor(
                                    out=kd[:, tc_, d_, :], in0=t2[:, d_ * 256:(d_ + 1) * 256], in1=kt_[:], op=ALU.mult),
                                    reads=[r_t2, r_kt_], pwrites=[r_kd])
                            t3v = t3[:].rearrange("p (f c t) -> p f c t", f=4, c=2)
                            P.op("dve", lambda e: e.tensor_copy(out=gT[:, 0:2, 2 * tc_:2 * tc_ + 2], in_=t3v[:, 0:2, :, 63]),
                                 reads=[r_t3], pwrites=[r_gT])
                            P.op("dve", lambda e: e.tensor_copy(out=gT[:, 2:4, 2 * tc_:2 * tc_ + 2], in_=t3v[:, 2:4, :, 0]),
                                 reads=[r_t3], pwrites=[r_gT])
                            if not is_ctx:
                                q_, r_q_ = qs[tc_ % 2]
                                k_, r_k_ = ks[tc_ % 2]
                                for d_ in range(2):
                                    P.op("dve", lambda e, d_=d_: e.scalar_tensor_tensor(
                                        out=qe[d_][0][:, :, tsl],
                                        in0=t3[:, d_ * 256:(d_ + 1) * 256].rearrange("p (h t) -> p h t", h=2),
                                        scalar=0.125, in1=q_[:], op0=ALU.mult, op1=ALU.mult),
                                        reads=[r_t3, r_q_], pwrites=[qe[d_][1]])
                                    P.op("dve", lambda e, d_=d_: e.tensor_tensor(
                                        out=ke[d_][0][:, :, tsl],
                                        in0=t4[:, d_ * 256:(d_ + 1) * 256].rearrange("p (h t) -> p h t", h=2),
                                        in1=k_[:], op=ALU.mult), reads=[r_t4, r_k_], pwrites=[ke[d_][1]])

                        ld_tc(0)
                        g1a(0)
                        for tc_ in range(ntc):
                            if tc_ + 1 < ntc:
                                ld_tc(tc_ + 1)
                                g1a(tc_ + 1)
                            g1b(tc_)
                        P.barrier()
                        if G_LIMIT[0] == (1 if is_ctx else 3):
                            return True
                    with ExitStack() as es:
                        pA = [ps(es, "pA%d" % par)[0] for par in range(2)]
                        pC = [ps(es, "pC%d" % par)[0] for par in range(2)]
                        pB = [ps(es, "pB%d" % d_)[0] for d_ in range(2)]
                        pU = [ps(es, "pU%d" % d_)[0] for d_ in range(2)]
                        rA = [[P.res("rA%d%d" % (par, d_)) for d_ in range(2)] for par in range(2)]
                        rC = [[P.res("rC%d%d" % (par, d_)) for d_ in range(2)] for par in range(2)]
                        rBo = [P.res("rBo%d" % d_) for d_ in range(2)]
                        rBu = [P.res("rBu%d" % d_) for d_ in range(2)]
                        scs = [sb(es, "scs%d" % d_, [128, 2, 2, 64], BF16) for d_ in range(2)]
                        csts = [sb(es, "cst%d" % d_, [128, 2, 2, 64], F32) for d_ in range(2)]
                        def g2_params(idx):
                            i, d_ = idx // 2, idx % 2
                            n = i if d_ == 0 else nch - 1 - i
                            return d_, n, n // 2, (n % 2) * 64, n * 64

                        def emit_scores(idx):
                            d_, n, tc_, pb, t0 = g2_params(idx)
                            sc_t, r_sc_t = scs[d_]
                            for h in range(4):
                                par = h % 2; hp = par * 64; hc = h // 2
                                P.op("pe", lambda e, d_=d_, par=par, hp=hp, hc=hc, pb=pb, t0=t0: e.matmul(
                                    pA[par][pb:pb + 64, d_ * 128 + hc * 64:d_ * 128 + (hc + 1) * 64],
                                    lhsT=ke[d_][0][hp:hp + 64, hc, t0:t0 + 64],
                                    rhs=qe[d_][0][hp:hp + 64, hc, t0:t0 + 64], start=True, stop=True),
                                    reads=[ke[d_][1], qe[d_][1]], writes=[rA[par][d_]])
                            for par in range(2):
                                P.op("dve", lambda e, d_=d_, pb=pb, sc_t=sc_t, par=par: e.tensor_tensor(
                                    out=sc_t[pb:pb + 64, par, :, :],
                                    in0=pA[par][pb:pb + 64, d_ * 128:(d_ + 1) * 128].rearrange("p (h c) -> p h c", h=2),
                                    in1=msk[pb:pb + 64, d_, 0:128].rearrange("p (h c) -> p h c", h=2), op=ALU.mult),
                                    reads=[rA[par][d_], r_msk], pwrites=[r_sc_t])

                        def emit_upd(idx):
                            d_, n, tc_, pb, t0 = g2_params(idx)
                            for h in range(4):
                                hp = (h % 2) * 64; hc = h // 2
                                P.op("pe", lambda e, d_=d_, h=h, hp=hp, hc=hc, pb=pb, tc_=tc_: e.matmul(
                                    pU[d_][hp:hp + 64, hc * 128:(hc + 1) * 128],
                                    lhsT=kd[pb:pb + 64, tc_, d_, h * 64:(h + 1) * 64],
                                    rhs=vt[pb:pb + 64, tc_, h * 128:(h + 1) * 128], start=True, stop=True),
                                    reads=[r_kd, r_vt], writes=[rBu[d_]])

                        def emit_o(idx):
                            d_, n, tc_, pb, t0 = g2_params(idx)
                            sc_t, r_sc_t = scs[d_]
                            for h in range(4):
                                par = h % 2; hp = par * 64; hc = h // 2
                                P.op("pe", lambda e, d_=d_, par=par, hp=hp, hc=hc, t0=t0: e.matmul(
                                    pC[par][:, d_ * 128 + hc * 64:d_ * 128 + (hc + 1) * 64],
                                    lhsT=Sb[d_][hc][0][hp:hp + 64, :],
                                    rhs=qe[d_][0][hp:hp + 64, hc, t0:t0 + 64], start=True, stop=True),
                                    reads=[Sb[d_][hc][1], qe[d_][1]], writes=[rC[par][d_]])
                            for h in range(4):
                                par = h % 2; hp = par * 64; hc = h // 2
                                P.op("pe", lambda e, d_=d_, h=h, par=par, hc=hc, pb=pb, tc_=tc_, sc_t=sc_t: e.matmul(
                                    pB[d_][:, h * 64:(h + 1) * 64],
                                    lhsT=vt[pb:pb + 64, tc_, h * 128:(h + 1) * 128],
                                    rhs=sc_t[pb:pb + 64, par, hc, :], start=True, stop=True),
                                    reads=[r_vt, r_sc_t], writes=[rBo[d_]])
                            cs_t, r_cs_t = csts[d_]
                            for par in range(2):
                                P.op("act", lambda e, d_=d_, par=par, cs_t=cs_t: e.activation(
                                    out=cs_t[:, :, par, :],
                                    in_=pC[par][:, d_ * 128:(d_ + 1) * 128].rearrange("p (h c) -> p h c", h=2),
                                    func=AF.Identity), reads=[rC[par][d_]], pwrites=[r_cs_t])
                            P.op("dve", lambda e, d_=d_, t0=t0: e.tensor_tensor(
                                out=osb[:, :, t0:t0 + 64], in0=pB[d_][:, 0:256].rearrange("p (h c) -> p h c", h=4),
                                in1=osb[:, :, t0:t0 + 64], op=ALU.add), reads=[rBo[d_]], writes=[r_osbn[n]])
                            P.op("pool", lambda e, t0=t0, cs_t=cs_t: e.tensor_tensor(
                                out=osb[:, :, t0:t0 + 64], in0=cs_t[:].rearrange("p a b c -> p (a b) c"),
                                in1=osb[:, :, t0:t0 + 64], op=ALU.add), reads=[r_cs_t], writes=[r_osbn[n]])

                        def emit_state(idx):
                            d_, n, tc_, pb, t0 = g2_params(idx)
                            for hc in range(2):
                                St, r_St = S[d_][hc]
                                Sbt, r_Sbt = Sb[d_][hc]
                                P.op("dve", lambda e, d_=d_, hc=hc, n=n, St=St: e.scalar_tensor_tensor(
                                    out=St[:], in0=St[:], scalar=gT[:, d_ * 2 + hc, n:n + 1],
                                    in1=pU[d_][:, hc * 128:(hc + 1) * 128], op0=ALU.mult, op1=ALU.add),
                                    reads=[rBu[d_], r_gT], writes=[r_St])
                                P.op("act", lambda e, St=St, Sbt=Sbt: e.activation(out=Sbt[:], in_=St[:], func=AF.Identity),
                                     reads=[r_St], writes=[r_Sbt])

                        nit = 2 * nch
                        if not is_ctx:
                            emit_scores(0)
                        for idx in range(nit):
                            emit_upd(idx)
                            if not is_ctx:
                                if idx + 1 < nit:
                                    emit_scores(idx + 1)
                                emit_o(idx)
                            emit_state(idx)
                        if is_ctx and debug:
                            for d_ in range(2):
                                for hc in range(2):
                                    dbg_dump("dbg_S%d_%d%d" % (s, d_, hc), S[d_][hc][0][:], [128, 128], F32, S[d_][hc][1], r_mod)
                        if G_LIMIT[0] == (2 if is_ctx else 4):
                            if not is_ctx and debug:
                                dbg_dump("dbg_o%d" % s, osb[:], [128, 4, T], F32, r_osb, r_mod)
                            P.barrier()
                            return True
                        if not is_ctx:
                            if debug:
                                dbg_dump("dbg_o%d" % s, osb[:], [128, 4, T], F32, r_osb, r_mod)
                            P.barrier()
                    if not is_ctx:
                        with ExitStack() as es:
                            sq, r_sq = sb(es, "gsq", [128, 512], BF16)
                            std, r_std = sb(es, "gstd", [128, 512], F32)
                            tmp, r_tmp = sb(es, "gtmp", [128, 512], F32)
                            rsl = [sb(es, "rsl%d" % i, [128, 4, 512], F32, dma=True) for i in range(2)]
                            yg, r_yg = sb(es, "yg", [128, 4, T], BF16, dma=True)
                            pss, r_pss = ps(es, "gpss")
                            for tt in range(4):
                                tsl = slice(tt * 512, (tt + 1) * 512)
                                rt, r_rt = rsl[tt % 2]
                                P.dma("sp", lambda e, rt=rt, tsl=tsl: e.dma_start(
                                    out=rt[:], in_=rs_d[s].rearrange("(h p) t -> p h t", p=128)[:, :, tsl]), r_rt,
                                    reads=[R["rs_d"]], writes=[r_rt])
                                for h in range(4):
                                    P.op("act", lambda e, h=h, tsl=tsl: e.activation(out=sq[:], in_=osb[:, h, tsl], func=AF.Square),
                                         reads=[r_osb] + r_osbn[8 * tt:8 * tt + 8], writes=[r_sq])
                                    P.op("pe", lambda e: e.matmul(pss[:, :], lhsT=ones_b[:], rhs=sq[:], start=True, stop=True),
                                         reads=[r_sq, r_onesb], writes=[r_pss])
                                    P.op("act", lambda e: e.activation(out=std[:], in_=pss[:, :], func=AF.Ln, scale=1.0 / 128.0,
                                                                       bias=epst[:, 0:1]), reads=[r_pss, r_epst], writes=[r_std])
                                    P.op("act", lambda e: e.activation(out=std[:], in_=std[:], func=AF.Exp, scale=-0.5),
                                         reads=[r_std], writes=[r_std])
                                    P.op("dve", lambda e, h=h, tsl=tsl: e.tensor_tensor(out=tmp[:], in0=osb[:, h, tsl], in1=std[:],
                                                                                      op=ALU.mult), reads=[r_osb, r_std] + r_osbn[8 * tt:8 * tt + 8], writes=[r_tmp])
                                    P.op("dve", lambda e, h=h, tsl=tsl, rt=rt: e.scalar_tensor_tensor(
                                        out=yg[:, h, tsl], in0=tmp[:], scalar=gnwt[:, h:h + 1], in1=rt[:, h, :],
                                        op0=ALU.mult, op1=ALU.mult), reads=[r_tmp, r_gnwt, r_rt], pwrites=[r_yg])
                            P.dma("sp", lambda e: e.dma_start(out=ygla_d[s].rearrange("(h p) t -> p h t", p=128), in_=yg[:]),
                                  r_yg, reads=[r_yg], pwrites=[R["ygla_d"]])
                            P.barrier()
                    else:
                        P.barrier()

            for is_ctx in (True, False):
                if phase(is_ctx):
                    return True

    for s in range(2):
        if stage_G(s):
            mid.close()
            return _finish(nc, P, R, dbg_out)
    if upto == "G":
        mid.close()
        return _finish(nc, P, R, dbg_out)

    hctx = ExitStack()
    Yp, r_Yp = sb(hctx, "Yp", [128, 2, 32, 512], BF16)
    with ExitStack() as es:
        kab, r_kab = sb(es, "kab", [128, 2, 16, 512], BF16, dma=True)
        ftb = [[sb(es, "ft%d%d" % (b_, q), [128, 16, 128], BF16, dma=True) for q in range(2)] for b_ in range(2)]
        Ksb = [sb(es, "Ksb%d" % q, [128, 512], F32) for q in range(2)]
        tt_ = [[sb(es, "ht%d%d" % (s_, q), [128, 512], F32) for q in range(4)] for s_ in range(2)]
        pZ = [[ps(es, "pZ%d%d" % (s_, q)) for q in range(2)] for s_ in range(2)]
        pK = [ps(es, "pK%d" % q) for q in range(2)]
        pKf, r_pKf = ps(es, "pKf")
        P.dma("sp", lambda e: e.dma_start(out=kab[:, 0], in_=ka_d.rearrange("i p c -> p i c")), r_kab,
              reads=[R["ka_d"]], writes=[r_kab])
        P.dma("sp", lambda e: e.dma_start(out=kab[:, 1], in_=kb_d.rearrange("i p c -> p i c")), r_kab,
              reads=[R["kb_d"]], pwrites=[r_kab])
        def ld_ft(j):
            fts = ftb[j % 2]
            for q in range(2):
                P.dma("sp", lambda e, q=q, j=j, fts=fts: e.dma_start(out=fts[q][0][:], in_=FTh[j + 16 * q]), fts[q][1],
                      writes=[fts[q][1]])
        ld_ft(0)
        for j in range(16):
            fts = ftb[j % 2]
            if j + 1 < 16:
                ld_ft(j + 1)
            for q in range(2):
                ft_, r_ft_ = fts[q]
                for s_ in range(2):
                    pz, r_pz = pZ[s_][q]
                    for i in range(16):
                        P.op("pe", lambda e, pz=pz, ft_=ft_, i=i, s_=s_: e.matmul(
                            pz[:, :], lhsT=ft_[:, i, :], rhs=ztok[:, s_, i, :], start=(i == 0), stop=(i == 15)),
                            reads=[r_ft_, r_ztok], writes=[r_pz])
                pk, r_pk = pK[q]
                for i in range(16):
                    P.op("pe", lambda e, pk=pk, ft_=ft_, i=i, q=q: e.matmul(
                        pk[:, :], lhsT=ft_[:, i, :], rhs=kab[:, q, i, :], start=(i == 0), stop=(i == 15)),
                        reads=[r_ft_, r_kab], writes=[r_pk])
                if j == 0 and q == 1:
                    for i in range(16):
                        P.op("pe", lambda e, ft_=ft_, i=i: e.matmul(
                            pKf[:, :], lhsT=ft_[:, i, :], rhs=kab[:, 0, i, :], start=(i == 0), stop=(i == 15)),
                            reads=[r_ft_, r_kab], writes=[r_pKf])
            for q in range(2):
                P.op("act", lambda e, q=q: e.activation(out=Ksb[q][0][:], in_=pK[q][0][:, :], func=AF.Identity),
                     reads=[pK[q][1]], writes=[Ksb[q][1]])
            if j == 0:
                P.op("act", lambda e: e.activation(out=Ksb[1][0][0:1, :], in_=pKf[0:1, :], func=AF.Identity),
                     reads=[r_pKf], writes=[Ksb[1][1]])
            for s_ in range(2):
                t = tt_[s_]
                zr, r_zr = pZ[s_][0]
                zi, r_zi = pZ[s_][1]
                for (dst_t, zsrc, r_zsrc, kq) in ((0, zr, r_zr, 0), (1, zi, r_zi, 1), (2, zr, r_zr, 1), (3, zi, r_zi, 0)):
                    P.op("dve", lambda e, t=t, dst_t=dst_t, zsrc=zsrc, kq=kq: e.tensor_tensor(
                        out=t[dst_t][0][:], in0=zsrc[:, :], in1=Ksb[kq][0][:], op=ALU.mult),
                        reads=[r_zsrc, Ksb[kq][1]], writes=[t[dst_t][1]])
                P.op("pool", lambda e, t=t, s_=s_, j=j: e.tensor_tensor(out=Yp[:, s_, j, :], in0=t[0][0][:], in1=t[1][0][:],
                                                                      op=ALU.subtract),
                     reads=[t[0][1], t[1][1]], pwrites=[r_Yp])
                P.op("pool", lambda e, t=t, s_=s_, j=j: e.tensor_tensor(out=Yp[:, s_, 16 + j, :], in0=t[2][0][:], in1=t[3][0][:],
                                                                      op=ALU.add),
                     reads=[t[2][1], t[3][1]], pwrites=[r_Yp])
                if j == 0:
                    P.op("pool", lambda e, t=t, s_=s_: e.tensor_copy(out=Yp[0:1, s_, 0, :], in_=t[0][0][0:1, :]),
                         reads=[t[0][1]], writes=[r_Yp])
                    P.op("pool", lambda e, t=t, s_=s_: e.tensor_copy(out=Yp[0:1, s_, 16, :], in_=t[1][0][0:1, :]),
                         reads=[t[1][1]], writes=[r_Yp])
        P.barrier()
    with ExitStack() as es:
        gib = [sb(es, "gib%d" % i, [128, 32, 256], BF16, dma=True) for i in range(2)]
        x0t = [sb(es, "x0t%d" % i, [128, 256], F32, dma=True) for i in range(2)]
        zt = [sb(es, "zt%d" % i, [128, 256], F32, dma=True) for i in range(2)]
        htmp, r_htmp = sb(es, "htmp", [128, 256], F32)
        ysb = [sb(es, "ysb%d" % i, [128, 256], BF16, dma=True) for i in range(2)]
        hb, r_hb = sb(es, "hb", [128, 4], F32, dma=True)
        pI = [ps(es, "pI%d" % i) for i in range(4)]
        P.dma("sp", lambda e: e.dma_start(out=hb[:], in_=hy_bias), r_hb, writes=[r_hb])
        items = [(n, s_, cj) for n in range(8) for s_ in range(2) for cj in range(4)]

        def ld_gi(n):
            gi, r_gi = gib[n % 2]
            P.dma("sp", lambda e, gi=gi, n=n: e.dma_start(out=gi[:], in_=GIh[n]), r_gi, writes=[r_gi])

        def ld_xz(idx):
            n, s_, cj = items[idx]
            tsl = slice(n * 256, (n + 1) * 256)
            x0_, r_x0 = x0t[idx % 2]
            z_, r_z = zt[idx % 2]
            P.dma("sp", lambda e, x0_=x0_, s_=s_, cj=cj, tsl=tsl: e.dma_start(
                out=x0_[:], in_=x0_d[s_, cj * 128:(cj + 1) * 128, tsl]), r_x0, reads=[R["x0_d"]], writes=[r_x0])
            P.dma("sp", lambda e, z_=z_, s_=s_, cj=cj, tsl=tsl: e.dma_start(
                out=z_[:], in_=z_d[s_, cj * 128:(cj + 1) * 128, tsl]), r_z, reads=[R["z_d"]], writes=[r_z])

        ld_gi(0)
        ld_xz(0)
        for idx, (n, s_, cj) in enumerate(items):
            gi, r_gi = gib[n % 2]
            tsl = slice(n * 256, (n + 1) * 256)
            if s_ == 0 and cj == 0 and n + 1 < 8:
                ld_gi(n + 1)
            if idx + 1 < len(items):
                ld_xz(idx + 1)
            pi_, r_pi = pI[idx % 4]
            x0_, r_x0 = x0t[idx % 2]
            z_, r_z = zt[idx % 2]
            y_, r_y = ysb[idx % 2]
            for fj in range(32):
                P.op("pe", lambda e, pi_=pi_, gi=gi, s_=s_, cj=cj, fj=fj: e.matmul(
                    pi_[:, 0:256], lhsT=Yp[:, s_, fj, cj * 128:(cj + 1) * 128], rhs=gi[:, fj, :],
                    start=(fj == 0), stop=(fj == 31)), reads=[r_Yp, r_gi], writes=[r_pi])
            P.op("dve", lambda e, z_=z_, cj=cj, pi_=pi_: e.scalar_tensor_tensor(
                out=htmp[:], in0=z_[:], scalar=hb[:, cj:cj + 1], in1=pi_[:, 0:256], op0=ALU.mult, op1=ALU.add),
                reads=[r_z, r_hb, r_pi], writes=[r_htmp])
            P.op("pool", lambda e, y_=y_, x0_=x0_: e.tensor_tensor(out=y_[:], in0=htmp[:], in1=x0_[:], op=ALU.mult),
                 reads=[r_htmp, r_x0], writes=[r_y])
            P.dma("sp", lambda e, y_=y_, s_=s_, cj=cj, tsl=tsl: e.dma_start(
                out=yhy_d[s_, cj * 128:(cj + 1) * 128, tsl], in_=y_[:]), r_y, reads=[r_y], pwrites=[R["yhy_d"]])
        P.barrier()
    hctx.close()
    mid.close()
    if upto == "H":
        return _finish(nc, P, R, dbg_out)

    with ExitStack() as es:
        whp, r_whp = sb(es, "whp", [128, 4, D], BF16, dma=True)
        wgp, r_wgp = sb(es, "wgp", [128, 4, D], BF16, dma=True)
        wo, r_wo = sb(es, "wo", [128, 8, D], BF16, dma=True)
        P.dma("pool", lambda e: e.dma_start(out=whp[:], in_=w_hyp.rearrange("(k p) m -> p k m", p=128)), r_whp, writes=[r_whp])
        P.dma("pool", lambda e: e.dma_start(out=wgp[:], in_=w_glp.rearrange("(k p) m -> p k m", p=128)), r_wgp, writes=[r_wgp])
        P.dma("pool", lambda e: e.dma_start(out=wo[:], in_=w_out.rearrange("(k p) m -> p k m", p=128)), r_wo, writes=[r_wo])
        yh = [sb(es, "yh%d" % i, [128, 4, 512], BF16, dma=True) for i in range(2)]
        yg_ = [sb(es, "ygm%d" % i, [128, 4, 512], BF16, dma=True) for i in range(2)]
        gts = [sb(es, "gts%d" % i, [128, 16, 512], BF16, dma=True) for i in range(2)]
        xts = [sb(es, "mxt%d" % i, [128, 8, 512], F32, dma=True) for i in range(2)]
        mgs = [sb(es, "mg%d" % i, [128, 8, 512], BF16) for i in range(2)]
        mt = [sb(es, "mt%d" % i, [128, 512], F32) for i in range(2)]
        sq, r_sq = sb(es, "msq", [128, 8, 512], BF16)
        std, r_std = sb(es, "mstd", [128, 512], F32)
        tmps = [sb(es, "mtmp%d" % i, [128, 512], F32) for i in range(2)]
        h2t = [sb(es, "h2t%d" % i, [128, 8, 512], BF16, dma=True) for i in range(2)]
        pps = [ps(es, "mp%d" % i) for i in range(6)]
        pss, r_pss = ps(es, "mpss")
        cnt = 0
        mitems = [(s_, tt) for s_ in range(2) for tt in range(4)]
        nm_ = len(mitems)

        def ld_merge(it):
            s_, tt = mitems[it]
            tsl = slice(tt * 512, (tt + 1) * 512)
            yh_, r_yh = yh[it % 2]; ygt, r_ygt = yg_[it % 2]; gt, r_gt = gts[it % 2]
            P.dma("sp", lambda e, yh_=yh_, s_=s_, tsl=tsl: e.dma_start(
                out=yh_[:], in_=yhy_d[s_].rearrange("(k p) t -> p k t", p=128)[:, :, tsl]), r_yh,
                reads=[R["yhy_d"]], writes=[r_yh])
            P.dma("sp", lambda e, ygt=ygt, s_=s_, tsl=tsl: e.dma_start(
                out=ygt[:], in_=ygla_d[s_].rearrange("(k p) t -> p k t", p=128)[:, :, tsl]), r_ygt,
                reads=[R["ygla_d"]], writes=[r_ygt])
            P.dma("sp", lambda e, gt=gt, s_=s_, tsl=tsl: e.dma_start(
                out=gt[:], in_=gate_d[s_].rearrange("(k p) t -> p k t", p=128)[:, :, tsl]), r_gt,
                reads=[R["gate_d"]], writes=[r_gt])

        def ld_x(it):
            s_, tt = mitems[it]
            tsl = slice(tt * 512, (tt + 1) * 512)
            xt, r_xt = xts[it % 2]
            P.dma("sp", lambda e, xt=xt, s_=s_, tsl=tsl: e.dma_start(
                out=xt[:], in_=xT[s_].rearrange("(k p) t -> p k t", p=128)[:, :, tsl]), r_xt, writes=[r_xt])

        mcnt = [0]

        def nps():
            t_ = pps[mcnt[0] % 6]
            mcnt[0] += 1
            return t_

        def merge(it):
            yh_, r_yh = yh[it % 2]; ygt, r_ygt = yg_[it % 2]; gt, r_gt = gts[it % 2]
            mg_, r_mg_ = mgs[it % 2]
            for m in range(8):
                ph, r_ph = nps()
                pg, r_pg = nps()
                for k in range(4):
                    P.op("pe", lambda e, ph=ph, k=k, m=m, yh_=yh_: e.matmul(
                        ph[:, :], lhsT=whp[:, k, m * 128:(m + 1) * 128], rhs=yh_[:, k, :], start=(k == 0), stop=(k == 3)),
                        reads=[r_whp, r_yh], writes=[r_ph])
                for k in range(4):
                    P.op("pe", lambda e, pg=pg, k=k, m=m, ygt=ygt: e.matmul(
                        pg[:, :], lhsT=wgp[:, k, m * 128:(m + 1) * 128], rhs=ygt[:, k, :], start=(k == 0), stop=(k == 3)),
                        reads=[r_wgp, r_ygt], writes=[r_pg])
                P.op("dve", lambda e, ph=ph, gt=gt, m=m: e.tensor_tensor(out=mt[0][0][:], in0=ph[:, :], in1=gt[:, m, :], op=ALU.mult),
                     reads=[r_ph, r_gt], writes=[mt[0][1]])
                P.op("dve", lambda e, pg=pg, gt=gt, m=m: e.tensor_tensor(out=mt[1][0][:], in0=pg[:, :], in1=gt[:, 8 + m, :], op=ALU.mult),
                     reads=[r_pg, r_gt], writes=[mt[1][1]])
                P.op("pool", lambda e, m=m, mg_=mg_: e.tensor_tensor(out=mg_[:, m, :], in0=mt[0][0][:], in1=mt[1][0][:], op=ALU.add),
                     reads=[mt[0][1], mt[1][1]], pwrites=[r_mg_])

        def outproj(it):
            s_, tt = mitems[it]
            tsl = slice(tt * 512, (tt + 1) * 512)
            xt, r_xt = xts[it % 2]
            h2_, r_h2 = h2t[it % 2]
            mg_, r_mg_ = mgs[it % 2]
            for m2 in range(8):
                po, r_po = nps()
                for m in range(8):
                    P.op("pe", lambda e, po=po, m=m, m2=m2, mg_=mg_: e.matmul(
                        po[:, :], lhsT=wo[:, m, m2 * 128:(m2 + 1) * 128], rhs=mg_[:, m, :], start=(m == 0), stop=(m == 7)),
                        reads=[r_wo, r_mg_], writes=[r_po])
                P.op("dve", lambda e, po=po, m2=m2, xt=xt, s_=s_: e.scalar_tensor_tensor(
                    out=xt[:, m2, :], in0=po[:, :], scalar=mod[:, G1 + m2, s_:s_ + 1], in1=xt[:, m2, :],
                    op0=ALU.mult, op1=ALU.add), reads=[r_po, r_mod], writes=[r_xt])
            P.dma("sp", lambda e, xt=xt, s_=s_, tsl=tsl: e.dma_start(
                out=xnew_d[s_].rearrange("(k p) t -> p k t", p=128)[:, :, tsl], in_=xt[:]), r_xt,
                reads=[r_xt], pwrites=[R["xnew_d"]])
            emit_norm(xt, r_xt, 512, a2, r_a2, SH2, s_, lambda k, h2_=h2_: h2_[:, k, :], r_h2,
                      (sq, r_sq, pss, r_pss, std, r_std, [t_[0] for t_ in tmps], [t_[1] for t_ in tmps]))
            P.dma("sp", lambda e, h2_=h2_, s_=s_, tsl=tsl: e.dma_start(
                out=h2_d[s_].rearrange("(k p) t -> p k t", p=128)[:, :, tsl], in_=h2_[:]), r_h2,
                reads=[r_h2], pwrites=[R["h2_d"]])

        ld_merge(0); ld_x(0); ld_merge(1)
        merge(0)
        for it in range(nm_):
            if it + 1 < nm_:
                merge(it + 1)
            if it + 2 < nm_:
                ld_merge(it + 2)
            if it + 1 < nm_:
                ld_x(it + 1)
            outproj(it)
        P.barrier()
    if upto == "M":
        return _finish(nc, P, R, dbg_out)

    GW = 66
    with ExitStack() as es:
        dwt, r_dwt = sb(es, "dwt", [128, NJ, 9], F32, dma=True)
        dbt, r_dbt = sb(es, "dbt", [128, NJ], F32, dma=True)
        P.dma("sp", lambda e: e.dma_start(out=dwt[:], in_=ffn_dw), r_dwt, writes=[r_dwt])
        P.dma("sp", lambda e: e.dma_start(out=dbt[:], in_=ffn_db), r_dbt, writes=[r_dbt])
        h2Ts = [sb(es, "h2T%d" % i, [128, 8, T], BF16, dma=True) for i in range(2)]
        wgs = [sb(es, "wg%d" % i, [128, 8, 512], BF16, dma=True) for i in range(2)]
        wus = [sb(es, "wu%d" % i, [128, 8, 512], BF16, dma=True) for i in range(2)]
        dgs = [sb(es, "dg%d" % i, [128, 9, 128], BF16) for i in range(2)]
        gps = [sb(es, "gp%d" % i, [128, 2 + 34 * GW], BF16) for i in range(2)]
        sgs = [sb(es, "sg%d" % i, [128, T], F32) for i in range(2)]
        acs = [sb(es, "ac%d" % i, [128, T], BF16, dma=True) for i in range(2)]
        pg_ = [ps(es, "fpg%d" % i) for i in range(4)]
        pc_ = [ps(es, "fpc%d" % i) for i in range(3)]
        for gp, r_gp in gps:
            P.op("pool", lambda e, gp=gp: e.memset(gp[:], 0.0), writes=[r_gp])
        wiv = ffn_wi.rearrange("(k p) m -> p k m", p=128)
        cg = 0; cc = 0; it = 0
        for s_ in range(2):
            P.dma("sp", lambda e, s_=s_: e.dma_start(out=h2Ts[s_][0][:], in_=h2_d[s_].rearrange("(k p) t -> p k t", p=128)),
                  h2Ts[s_][1], reads=[R["h2_d"]], writes=[h2Ts[s_][1]])
        groups = [(s_, g) for s_ in range(2) for g in range((NJ + 3) // 4)]

        def ld_wg(gi_):
            s_, g = groups[gi_]
            nj = min(4, NJ - 4 * g)
            wg4, r_wg4 = wgs[gi_ % 2]; wu4, r_wu4 = wus[gi_ % 2]
            P.dma("pool", lambda e, wg4=wg4, g=g, nj=nj: e.dma_start(
                out=wg4[:, :, 0:nj * 128], in_=wiv[:, :, FFH + g * 512:FFH + g * 512 + nj * 128]), r_wg4, writes=[r_wg4])
            P.dma("pool", lambda e, wu4=wu4, g=g, nj=nj: e.dma_start(
                out=wu4[:, :, 0:nj * 128], in_=wiv[:, :, g * 512:g * 512 + nj * 128]), r_wu4, writes=[r_wu4])

        ld_wg(0)
        for gi_, (s_, g) in enumerate(groups):
            if gi_ + 1 < len(groups):
                ld_wg(gi_ + 1)
            h2T, r_h2T = h2Ts[s_]
            wg4, r_wg = wgs[gi_ % 2]; wu4, r_wu = wus[gi_ % 2]
            for jq in range(min(4, NJ - 4 * g)):
                j = 4 * g + jq
                wg = wg4[:, :, jq * 128:(jq + 1) * 128]
                wu = wu4[:, :, jq * 128:(jq + 1) * 128]
                dg, r_dg = dgs[it % 2]
                gp, r_gp = gps[it % 2]; sg, r_sg = sgs[it % 2]; ac, r_ac = acs[it % 2]
                it += 1
                for tap in range(9):
                    P.op("dve", lambda e, dg=dg, j=j, tap=tap: e.tensor_scalar(
                        out=dg[:, tap, :], in0=ident_f[:], scalar1=dwt[:, j, tap:tap + 1], scalar2=None, op0=ALU.mult),
                        reads=[r_identf, r_dwt], writes=[r_dg])
                gpv = gp[:, 1:1 + 34 * GW].rearrange("p (r w) -> p r w", w=GW)
                for tt in range(4):
                    pt, r_pt = pg_[cg % 4]; cg += 1
                    for k in range(8):
                        P.op("pe", lambda e, pt=pt, wg=wg, k=k, tt=tt, h2T=h2T: e.matmul(
                            pt[:, :], lhsT=wg[:, k, :], rhs=h2T[:, k, tt * 512:(tt + 1) * 512], start=(k == 0), stop=(k == 7)),
                            reads=[r_wg, r_h2T], writes=[r_pt])
                    P.op("act", lambda e, pt=pt, gpv=gpv, tt=tt: e.activation(
                        out=gpv[:, 1 + 8 * tt:9 + 8 * tt, 1:65], in_=pt[:, :].rearrange("p (r w) -> p r w", w=64),
                        func=AF.Identity), reads=[r_pt], writes=[r_gp])
                for (R0, nr) in ((1, 7), (8, 7), (15, 7), (22, 7), (29, 4)):
                    pc, r_pc = pc_[cc % 3]; cc += 1
                    p0 = 1 + R0 * GW
                    for tap in range(9):
                        off = (tap // 3 - 1) * GW + (tap % 3 - 1)
                        P.op("pe", lambda e, pc=pc, dg=dg, gp=gp, tap=tap, off=off, p0=p0, nr=nr: e.matmul(
                            pc[:, 0:nr * GW], lhsT=dg[:, tap, :], rhs=gp[:, p0 + off:p0 + off + nr * GW],
                            start=(tap == 0), stop=(tap == 8)), reads=[r_dg, r_gp], writes=[r_pc])
                    P.op("act", lambda e, pc=pc, sg=sg, R0=R0, nr=nr, j=j: e.activation(
                        out=sg[:, (R0 - 1) * 64:(R0 - 1 + nr) * 64].rearrange("p (r w) -> p r w", w=64),
                        in_=pc[:, 0:nr * GW].rearrange("p (r w) -> p r w", w=GW)[:, :, 1:65],
                        func=AF.Silu, bias=dbt[:, j:j + 1], scale=1.0), reads=[r_pc, r_dbt], writes=[r_sg])
                for tt in range(4):
                    pt, r_pt = pg_[cg % 4]; cg += 1
                    for k in range(8):
                        P.op("pe", lambda e, pt=pt, wu=wu, k=k, tt=tt, h2T=h2T: e.matmul(
                            pt[:, :], lhsT=wu[:, k, :], rhs=h2T[:, k, tt * 512:(tt + 1) * 512], start=(k == 0), stop=(k == 7)),
                            reads=[r_wu, r_h2T], writes=[r_pt])
                    P.op("dve", lambda e, pt=pt, sg=sg, ac=ac, tt=tt: e.tensor_tensor(
                        out=ac[:, tt * 512:(tt + 1) * 512], in0=pt[:, :], in1=sg[:, tt * 512:(tt + 1) * 512], op=ALU.mult),
                        reads=[r_pt, r_sg], writes=[r_ac])
                P.dma("sp", lambda e, ac=ac, s_=s_, j=j: e.dma_start(out=act_d[s_, j * 128:(j + 1) * 128, :], in_=ac[:]),
                      r_ac, reads=[r_ac], pwrites=[R["act_d"]])
        P.barrier()
    if upto == "FF":
        return _finish(nc, P, R, dbg_out)

    with ExitStack() as es:
        wd, r_wd = sb(es, "wd", [128, NJ, D], BF16, dma=True)
        wov = ffn_wo.rearrange("(j p) m -> p j m", p=128)
        for jj in range(0, NJ, 2):
            P.dma("pool", lambda e, jj=jj: e.dma_start(out=wd[:, jj:jj + 2, :], in_=wov[:, jj:jj + 2, :]), r_wd, pwrites=[r_wd])
        ats = [sb(es, "at%d" % i, [128, NJ, 512], BF16, dma=True) for i in range(2)]
        xns = [sb(es, "xn%d" % i, [128, 8, 512], F32, dma=True) for i in range(2)]
        ots = [sb(es, "ot%d" % i, [128, 8, 512], F32, dma=True) for i in range(2)]
        sq, r_sq = sb(es, "dsq", [128, 8, 512], BF16)
        std, r_std = sb(es, "dstd", [128, 512], F32)
        pd = [ps(es, "pd%d" % i) for i in range(4)]
        pss, r_pss = ps(es, "dpss")
        cnt = 0
        ditems = [(s_, tt) for s_ in range(2) for tt in range(4)]

        def ld_d(it):
            s_, tt = ditems[it]
            tsl = slice(tt * 512, (tt + 1) * 512)
            at, r_at = ats[it % 2]; xn, r_xn = xns[it % 2]
            P.dma("sp", lambda e, at=at, s_=s_, tsl=tsl: e.dma_start(
                out=at[:], in_=act_d[s_].rearrange("(j p) t -> p j t", p=128)[:, :, tsl]), r_at,
                reads=[R["act_d"]], writes=[r_at])
            P.dma("sp", lambda e, xn=xn, s_=s_, tsl=tsl: e.dma_start(
                out=xn[:], in_=xnew_d[s_].rearrange("(k p) t -> p k t", p=128)[:, :, tsl]), r_xn,
                reads=[R["xnew_d"]], writes=[r_xn])

        ld_d(0)
        for it, (s_, tt) in enumerate(ditems):
            if True:
                tsl = slice(tt * 512, (tt + 1) * 512)
                at, r_at = ats[it % 2]; xn, r_xn = xns[it % 2]; ot, r_ot = ots[it % 2]
                if it + 1 < len(ditems):
                    ld_d(it + 1)
                for m2 in range(8):
                    pt, r_pt = pd[cnt % 4]; cnt += 1
                    for j in range(NJ):
                        P.op("pe", lambda e, pt=pt, j=j, m2=m2, at=at: e.matmul(
                            pt[:, :], lhsT=wd[:, j, m2 * 128:(m2 + 1) * 128], rhs=at[:, j, :], start=(j == 0), stop=(j == NJ - 1)),
                            reads=[r_wd, r_at], writes=[r_pt])
                    P.op("dve", lambda e, pt=pt, m2=m2, xn=xn, s_=s_: e.scalar_tensor_tensor(
                        out=xn[:, m2, :], in0=pt[:, :], scalar=mod[:, G2 + m2, s_:s_ + 1], in1=xn[:, m2, :],
                        op0=ALU.mult, op1=ALU.add), reads=[r_pt, r_mod], writes=[r_xn])
                P.op("act", lambda e, xn=xn: e.activation(out=sq[:], in_=xn[:], func=AF.Square), reads=[r_xn], writes=[r_sq])
                for k in range(8):
                    P.op("pe", lambda e, k=k: e.matmul(pss[:, :], lhsT=ones_b[:], rhs=sq[:, k, :], start=(k == 0), stop=(k == 7)),
                         reads=[r_sq, r_onesb], writes=[r_pss])
                P.op("act", lambda e: e.activation(out=std[:], in_=pss[:, :], func=AF.Ln, scale=1.0 / D, bias=epst[:, 0:1]),
                     reads=[r_pss, r_epst], writes=[r_std])
                P.op("act", lambda e: e.activation(out=std[:], in_=std[:], func=AF.Exp, scale=-0.5),
                     reads=[r_std], writes=[r_std])
                for k in range(8):
                    P.op("dve", lambda e, k=k, xn=xn, ot=ot: e.scalar_tensor_tensor(
                        out=ot[:, k, :], in0=xn[:, k, :], scalar=nwt[:, 2, k:k + 1], in1=std[:], op0=ALU.mult, op1=ALU.mult),
                        reads=[r_xn, r_nwt, r_std], pwrites=[r_ot])
                P.dma("sp", lambda e, ot=ot, s_=s_, tsl=tsl: e.dma_start(
                    out=outT[s_].rearrange("(k p) t -> p k t", p=128)[:, :, tsl], in_=ot[:]), r_ot,
                    reads=[r_ot], pwrites=[R["outT"]])
        P.barrier()
    return _finish(nc, P, R, dbg_out)


def _finish(nc, P, R, dbg_out):
    P.q["sp"].append((P._deps("sp", list(R.values()), (), ()), None, None))
    P.barrier()
    P.emit()
    return nc, dbg_out


def _shared_inputs(inp):
    cst = _constants()
    f32 = np.float32
    A = lambda a: np.ascontiguousarray(a, dtype=f32)
    d = {}
    d["w_ada"] = A(inp["w_ada"][0])
    d["b_adaT"] = A(inp["b_ada"][0].reshape(48, 128).T)
    d["nw"] = A(np.stack([inp["norm1_w"][0].reshape(8, 128).T, inp["norm2_w"][0].reshape(8, 128).T,
                          inp["final_norm_w"].reshape(8, 128).T], axis=1))
    d["w_in"] = A(inp["w_in"][0])
    d["hy_cw"] = A(inp["hy_conv_w"][0].reshape(3, 12, 128).transpose(2, 1, 0))
    d["hy_bias"] = A(inp["hy_bias"][0].reshape(4, 128).T)
    d["f_w1"] = A(inp["hy_filt_w1"][0]); d["f_w2"] = A(inp["hy_filt_w2"][0]); d["f_w3"] = A(inp["hy_filt_w3"][0])
    d["f_vec"] = A(np.stack([inp["hy_filt_b1"][0], inp["hy_filt_b2"][0], inp["hy_filt_freq"][0],
                             np.zeros(64, f32)], axis=1))
    d["f_decay"] = A(inp["hy_decay"][0][None, :])
    we = np.zeros((33, 512), f32)
    we[0:16, 0:256] = inp["gla_a_up_f"][0]; we[16:32, 256:512] = inp["gla_a_up_b"][0]
    we[32, 0:256] = inp["gla_a_bias_f"][0]; we[32, 256:512] = inp["gla_a_bias_b"][0]
    d["w_ext"] = we
    d["gnw"] = A(inp["gla_norm_w"][0].reshape(4, 128).T)
    d["w_hyp"] = A(inp["w_hy_proj"][0]); d["w_glp"] = A(inp["w_gla_proj"][0]); d["w_out"] = A(inp["w_out"][0])
    d["ffn_wi"] = A(inp["ffn_w_in"][0])
    d["ffn_dw"] = A(inp["ffn_dw"][0].reshape(9, NJ, 128).transpose(2, 1, 0))
    d["ffn_db"] = A(inp["ffn_dw_bias"][0].reshape(NJ, 128).T)
    d["ffn_wo"] = A(inp["ffn_w_out"][0])
    for k in ("FTh", "GIh", "featsT", "tneg", "tri", "masks", "ident"):
        d[k] = cst[k]
    return d


def _core_inputs(inp, shared, ci):
    b0 = 2 * ci
    d = dict(shared)
    d["xT"] = np.ascontiguousarray(np.transpose(inp["x"][b0:b0 + 2], (0, 2, 1)), dtype=np.float32)
    d["ctxT"] = np.ascontiguousarray(np.transpose(inp["ctx"][b0:b0 + 2], (0, 2, 1)), dtype=np.float32)
    c3 = np.stack([inp["c"][b0], inp["c"][b0 + 1], inp["c_ctx"]], axis=0)
    d["c3T"] = np.ascontiguousarray(c3.reshape(3, 8, 128).transpose(2, 1, 0), dtype=np.float32)
    return d


def kernel(**inputs):
    inp = {k: np.asarray(v) for k, v in inputs.items()}
    nc, _ = build(debug=False, upto="ALL")
    shared = _shared_inputs(inp)
    in_maps = [_core_inputs(inp, shared, ci) for ci in range(NCORES)]
    res = run_bass_kernel_spmd(nc, in_maps, core_ids=list(range(NCORES)))
    outs = [np.transpose(np.asarray(r["outT"]), (0, 2, 1)) for r in res.results]
    return np.ascontiguousarray(np.concatenate(outs, axis=0), dtype=np.float32)
```
